# Optimizing a Trainium2 kernel written in Bass

```python
import math
import jax
import jax.numpy as jnp
from jax import lax
import numpy as np

D_MODEL = 2048
BATCH = 2
SEQ = 4096
DEPTH = 2
DEC_BATCH = 16
DEC_SEQ = 64
PAST_LEN = 1024

CHUNK = 64
Q_BLOCK = 128
EPS = 1e-6
NEG_INF = -1e30

A_WIDTH = 512
A_KW = 31
B_HEADS = 4
B_DK = 128
B_DV = 256
ROPE_BASE = 10000.0
C_HEADS = 4
C_DQK = 64
C_DV = 128
REL_BUCKETS = 32
REL_MAX_DIST = 128

MIX_WIDTH = A_WIDTH + B_HEADS * B_DV + C_HEADS * C_DV
A_COLS = 2 * A_WIDTH
B_COLS = B_HEADS * (2 * B_DK + 2 * B_DV)
C_COLS = C_HEADS * (4 * C_DQK + C_DV)
IN_COLS = A_COLS + B_COLS + C_COLS

MEM_LEN = 256
M_HEADS = 4
M_DH = 128
D_FF = 5632
F_KW = 3

kernel_name = "hybrid_stream_encoder_step"

F32 = jnp.float32


def rms_norm(x, g):
    xf = x.astype(F32)
    y = xf * lax.rsqrt(jnp.mean(xf * xf, axis=-1, keepdims=True) + EPS)
    return (y * g.astype(F32)).astype(x.dtype)


def layer_norm(x, g, b):
    xf = x.astype(F32)
    mu = jnp.mean(xf, axis=-1, keepdims=True)
    xc = xf - mu
    var = jnp.mean(xc * xc, axis=-1, keepdims=True)
    return (xc * lax.rsqrt(var + EPS) * g.astype(F32) + b.astype(F32)).astype(x.dtype)


def causal_dwconv(x_ext, w, b):
    c = x_ext.shape[-1]
    y = lax.conv_general_dilated(x_ext, w.astype(x_ext.dtype)[:, None, :], window_strides=(1,),
                                 padding="VALID", dimension_numbers=("NWC", "WIO", "NWC"),
                                 feature_group_count=c)
    return y + b.astype(y.dtype)


def rope(x, pos):
    half = x.shape[-1] // 2
    inv = 1.0 / (ROPE_BASE ** (jnp.arange(half, dtype=F32) / half))
    ang = pos.astype(F32)[:, None] * inv[None, :]
    cos = jnp.cos(ang)[None, :, None, :]
    sin = jnp.sin(ang)[None, :, None, :]
    x1 = x[..., :half].astype(F32)
    x2 = x[..., half:].astype(F32)
    return jnp.concatenate([x1 * cos - x2 * sin, x1 * sin + x2 * cos], axis=-1).astype(x.dtype)


def rel_bucket(rel):
    half = REL_BUCKETS // 2
    exact = half // 2
    n = jnp.abs(rel)
    large = exact + (jnp.log(jnp.maximum(n, 1).astype(F32) / exact) / math.log(REL_MAX_DIST / exact)
                     * (half - exact)).astype(jnp.int32)
    large = jnp.minimum(large, half - 1)
    return jnp.where(rel > 0, half, 0) + jnp.where(n < exact, n, large)


def retention_block(q, k, v, state, log_gamma):
    L = q.shape[1]
    idx = jnp.arange(L, dtype=F32)
    diff = idx[:, None] - idx[None, :]
    decay = jnp.where(diff >= 0, jnp.exp(log_gamma[:, None, None] * jnp.maximum(diff, 0.0)), 0.0)
    qf, kf, vf = q.astype(F32), k.astype(F32), v.astype(F32)
    scores = jnp.einsum("bihd,bjhd->bhij", qf, kf) * decay
    inner = jnp.einsum("bhij,bjhe->bihe", scores, vf)
    q_decay = jnp.exp(log_gamma[None, :] * (idx[:, None] + 1.0))
    cross = jnp.einsum("bihd,bhde->bihe", qf, state) * q_decay[None, :, :, None]
    k_decay = jnp.exp(log_gamma[None, :] * (L - 1.0 - idx[:, None]))
    new_state = (state * jnp.exp(log_gamma * L)[None, :, None, None]
                 + jnp.einsum("bjhd,jh,bjhe->bhde", kf, k_decay, vf))
    return inner + cross, new_state


def diff_attention(q, k, v, q_pos, k_pos, rel_bias, lam, lam_init, subln_g):
    logits = jnp.einsum("bqhmd,bkhmd->bmhqk", q.astype(F32), k.astype(F32)) * (C_DQK ** -0.5)
    rel = k_pos[None, :] - q_pos[:, None]
    bias = jnp.transpose(rel_bias.astype(F32)[rel_bucket(rel)], (2, 0, 1))
    visible = (k_pos[None, :] // CHUNK) <= (q_pos[:, None] // CHUNK)
    logits = jnp.where(visible, logits + bias, NEG_INF)
    probs = jax.nn.softmax(logits, axis=-1)
    weights = probs[:, 0] - lam * probs[:, 1]
    o = jnp.einsum("bhqk,bkhe->bqhe", weights, v.astype(F32))
    return rms_norm(o, subln_g) * (1.0 - lam_init)


def mixing_block(x, pos0, conv_hist, ret_state, past_k, past_v, lam, lam_init, w, rel_bias):
    (norm_g, w_in, conv_w, conv_b, ln_g, ln_b, ret_gn_g, qn_g, kn_g, subln_g, w_out) = w
    bsz, t, _ = x.shape
    h = rms_norm(x, norm_g)
    proj = h @ w_in
    a_in, b_in, c_in = jnp.split(proj, [A_COLS, A_COLS + B_COLS], axis=-1)

    a_val, a_gate = jnp.split(a_in, 2, axis=-1)
    a = a_val * jax.nn.sigmoid(a_gate)
    a_ext = jnp.concatenate([conv_hist.astype(a.dtype), a], axis=1)
    a = jax.nn.silu(layer_norm(causal_dwconv(a_ext, conv_w, conv_b), ln_g, ln_b))
    new_conv_hist = a_ext[:, -(A_KW - 1):]

    qk = B_HEADS * B_DK
    bq, bk, bv, bg = jnp.split(b_in, [qk, 2 * qk, 2 * qk + B_HEADS * B_DV], axis=-1)
    pos = pos0 + jnp.arange(t, dtype=jnp.int32)
    bq = rope(bq.reshape(bsz, t, B_HEADS, B_DK), pos)
    bk = rope(bk.reshape(bsz, t, B_HEADS, B_DK), pos) * (B_DK ** -0.5)
    bv = bv.reshape(bsz, t, B_HEADS, B_DV)
    log_gamma = jnp.log(1.0 - 2.0 ** (-5.0 - jnp.arange(B_HEADS, dtype=F32)))
    s0 = ret_state.astype(F32)
    if past_k is None:
        nc = t // CHUNK

        def to_chunks(z):
            return z.reshape(bsz, nc, CHUNK, z.shape[2], z.shape[3]).swapaxes(0, 1)

        def step(s, inp):
            o, s_new = retention_block(inp[0], inp[1], inp[2], s, log_gamma)
            return s_new, o

        new_ret, bo = lax.scan(step, s0, (to_chunks(bq), to_chunks(bk), to_chunks(bv)))
        bo = bo.swapaxes(0, 1).reshape(bsz, t, B_HEADS, B_DV)
    else:
        bo, new_ret = retention_block(bq, bk, bv, s0, log_gamma)
    bo = rms_norm(bo, ret_gn_g.reshape(B_HEADS, B_DV)).reshape(bsz, t, B_HEADS * B_DV)
    bo = bo.astype(x.dtype) * jax.nn.silu(bg)

    cq, ck, cv = jnp.split(c_in, [C_HEADS * 2 * C_DQK, C_HEADS * 4 * C_DQK], axis=-1)
    cq = rms_norm(cq.reshape(bsz, t, C_HEADS, 2, C_DQK), qn_g)
    ck = rms_norm(ck.reshape(bsz, t, C_HEADS, 2, C_DQK), kn_g)
    cv = cv.reshape(bsz, t, C_HEADS, C_DV)
    if past_k is None:
        k_pos = jnp.arange(t, dtype=jnp.int32)
        nb = t // Q_BLOCK
        q_blocks = cq.reshape(bsz, nb, Q_BLOCK, C_HEADS, 2, C_DQK).swapaxes(0, 1)

        def attend_block(args):
            q_blk, i = args
            q_pos = i * Q_BLOCK + jnp.arange(Q_BLOCK, dtype=jnp.int32)
            return diff_attention(q_blk, ck, cv, q_pos, k_pos, rel_bias, lam, lam_init, subln_g)

        co = lax.map(attend_block, (q_blocks, jnp.arange(nb, dtype=jnp.int32)))
        co = co.swapaxes(0, 1).reshape(bsz, t, C_HEADS * C_DV)
    else:
        p = past_k.shape[1]
        k_all = jnp.concatenate([past_k.reshape(bsz, p, C_HEADS, 2, C_DQK).astype(ck.dtype), ck], axis=1)
        v_all = jnp.concatenate([past_v.astype(cv.dtype), cv], axis=1)
        co = diff_attention(cq, k_all, v_all, pos, jnp.arange(p + t, dtype=jnp.int32),
                            rel_bias, lam, lam_init, subln_g)
        co = co.reshape(bsz, t, C_HEADS * C_DV)

    mix = jnp.concatenate([a, bo, co.astype(x.dtype)], axis=-1)
    y = x + mix @ w_out
    return y, new_conv_hist, new_ret.astype(x.dtype), ck.reshape(bsz, t, C_HEADS, 2 * C_DQK), cv


def memory_kv(mem, mem_g, w_k, w_v, kn_g):
    bsz, m, _ = mem.shape
    hm = rms_norm(mem, mem_g)
    k = rms_norm((hm @ w_k).reshape(bsz, m, M_HEADS, M_DH), kn_g)
    v = (hm @ w_v).reshape(bsz, m, M_HEADS, M_DH)
    return k, v


def memory_block(x, mem_k, mem_v, norm_g, w_q, qn_g, w_o):
    bsz, t, _ = x.shape
    q = rms_norm((rms_norm(x, norm_g) @ w_q).reshape(bsz, t, M_HEADS, M_DH), qn_g)
    logits = jnp.einsum("bqhd,bkhd->bhqk", q.astype(F32), mem_k.astype(F32)) * (M_DH ** -0.5)
    probs = jax.nn.softmax(logits, axis=-1)
    o = jnp.einsum("bhqk,bkhd->bqhd", probs, mem_v.astype(F32)).astype(x.dtype)
    return x + o.reshape(bsz, t, M_HEADS * M_DH) @ w_o


def conv_ffn_block(x, hist, norm_g, w_up, conv_w, conv_b, w_down):
    u = rms_norm(x, norm_g) @ w_up
    u_ext = jnp.concatenate([hist.astype(u.dtype), u], axis=1)
    c = causal_dwconv(u_ext, conv_w, conv_b)
    val, gate = jnp.split(c, 2, axis=-1)
    return x + (jax.nn.silu(gate) * val) @ w_down, u_ext[:, -(F_KW - 1):]


def setup_inputs(seed: int = 0) -> dict:
    key = jax.random.key(seed)
    ks = iter(jax.random.split(key, 48))

    def nrm(shape, scale=1.0):
        return jax.random.normal(next(ks), shape, F32) * scale

    def gain(shape):
        return 1.0 + nrm(shape, 0.02)

    L = DEPTH
    return {
        "x_prompt": nrm((BATCH, SEQ, D_MODEL)),
        "x_sample": nrm((DEC_BATCH, DEC_SEQ, D_MODEL)),
        "mem_prompt": nrm((BATCH, MEM_LEN, D_MODEL)),
        "state_conv_a": nrm((L, DEC_BATCH, A_KW - 1, A_WIDTH), 0.5),
        "state_ret": nrm((L, DEC_BATCH, B_HEADS, B_DK, B_DV), 0.5),
        "cache_diff_k": nrm((L, DEC_BATCH, PAST_LEN, C_HEADS, 2 * C_DQK)),
        "cache_diff_v": nrm((L, DEC_BATCH, PAST_LEN, C_HEADS, C_DV)),
        "cache_mem_k": nrm((L, DEC_BATCH, MEM_LEN, M_HEADS, M_DH)),
        "cache_mem_v": nrm((L, DEC_BATCH, MEM_LEN, M_HEADS, M_DH)),
        "state_conv_f": nrm((L, DEC_BATCH, F_KW - 1, 2 * D_FF)),
        "norm1_g": gain((L, D_MODEL)),
        "w_in": nrm((L, D_MODEL, IN_COLS), D_MODEL ** -0.5),
        "conv_a_w": nrm((L, A_KW, A_WIDTH), A_KW ** -0.5),
        "conv_a_b": nrm((L, A_WIDTH), 0.02),
        "ln_a_g": gain((L, A_WIDTH)),
        "ln_a_b": nrm((L, A_WIDTH), 0.02),
        "ret_gn_g": gain((L, B_HEADS * B_DV)),
        "diff_qn_g": gain((L, C_DQK)),
        "diff_kn_g": gain((L, C_DQK)),
        "diff_lq1": nrm((L, C_DQK), 0.1),
        "diff_lk1": nrm((L, C_DQK), 0.1),
        "diff_lq2": nrm((L, C_DQK), 0.1),
        "diff_lk2": nrm((L, C_DQK), 0.1),
        "diff_subln_g": gain((L, C_DV)),
        "w_out": nrm((L, MIX_WIDTH, D_MODEL), MIX_WIDTH ** -0.5),
        "rel_bias": nrm((REL_BUCKETS, C_HEADS), 0.5),
        "norm2_g": gain((L, D_MODEL)),
        "mem_norm_g": gain((L, D_MODEL)),
        "w_xq": nrm((L, D_MODEL, M_HEADS * M_DH), D_MODEL ** -0.5),
        "w_xk": nrm((L, D_MODEL, M_HEADS * M_DH), D_MODEL ** -0.5),
        "w_xv": nrm((L, D_MODEL, M_HEADS * M_DH), D_MODEL ** -0.5),
        "xqn_g": gain((L, M_DH)),
        "xkn_g": gain((L, M_DH)),
        "w_xo": nrm((L, M_HEADS * M_DH, D_MODEL), (M_HEADS * M_DH) ** -0.5),
        "norm3_g": gain((L, D_MODEL)),
        "w_up": nrm((L, D_MODEL, 2 * D_FF), D_MODEL ** -0.5),
        "conv_f_w": nrm((L, F_KW, 2 * D_FF), F_KW ** -0.5),
        "conv_f_b": nrm((L, 2 * D_FF), 0.02),
        "w_down": nrm((L, D_FF, D_MODEL), D_FF ** -0.5),
    }


def reference(x_prompt, x_sample, mem_prompt, state_conv_a, state_ret, cache_diff_k, cache_diff_v,
              cache_mem_k, cache_mem_v, state_conv_f, norm1_g, w_in, conv_a_w, conv_a_b, ln_a_g, ln_a_b,
              ret_gn_g, diff_qn_g, diff_kn_g, diff_lq1, diff_lk1, diff_lq2, diff_lk2, diff_subln_g, w_out,
              rel_bias, norm2_g, mem_norm_g, w_xq, w_xk, w_xv, xqn_g, xkn_g, w_xo, norm3_g, w_up,
              conv_f_w, conv_f_b, w_down):
    xp, xs = x_prompt, x_sample
    bp, bs = xp.shape[0], xs.shape[0]
    past = cache_diff_k.shape[2]
    p_ca, p_rs, p_k, p_v, p_mk, p_mv, p_cf = [], [], [], [], [], [], []
    s_ca, s_rs, s_k, s_v, s_cf = [], [], [], [], []
    for l in range(DEPTH):
        lam_init = 0.8 - 0.6 * math.exp(-0.3 * l)
        lam = (jnp.exp(jnp.sum(diff_lq1[l].astype(F32) * diff_lk1[l].astype(F32)))
               - jnp.exp(jnp.sum(diff_lq2[l].astype(F32) * diff_lk2[l].astype(F32))) + lam_init)
        mix_w = (norm1_g[l], w_in[l], conv_a_w[l], conv_a_b[l], ln_a_g[l], ln_a_b[l], ret_gn_g[l],
                 diff_qn_g[l], diff_kn_g[l], diff_subln_g[l], w_out[l])

        zero_a = jnp.zeros((bp, A_KW - 1, A_WIDTH), xp.dtype)
        zero_r = jnp.zeros((bp, B_HEADS, B_DK, B_DV), F32)
        xp, ca, rs, kn, vn = mixing_block(xp, 0, zero_a, zero_r, None, None, lam, lam_init, mix_w, rel_bias)
        mk, mv = memory_kv(mem_prompt, mem_norm_g[l], w_xk[l], w_xv[l], xkn_g[l])
        xp = memory_block(xp, mk, mv, norm2_g[l], w_xq[l], xqn_g[l], w_xo[l])
        zero_f = jnp.zeros((bp, F_KW - 1, 2 * D_FF), xp.dtype)
        xp, cf = conv_ffn_block(xp, zero_f, norm3_g[l], w_up[l], conv_f_w[l], conv_f_b[l], w_down[l])
        p_ca.append(ca); p_rs.append(rs); p_k.append(kn); p_v.append(vn)
        p_mk.append(mk); p_mv.append(mv); p_cf.append(cf)

        xs, sca, srs, skn, svn = mixing_block(xs, past, state_conv_a[l], state_ret[l], cache_diff_k[l],
                                              cache_diff_v[l], lam, lam_init, mix_w, rel_bias)
        xs = memory_block(xs, cache_mem_k[l], cache_mem_v[l], norm2_g[l], w_xq[l], xqn_g[l], w_xo[l])
        xs, scf = conv_ffn_block(xs, state_conv_f[l], norm3_g[l], w_up[l], conv_f_w[l], conv_f_b[l], w_down[l])
        s_ca.append(sca); s_rs.append(srs); s_k.append(skn); s_v.append(svn); s_cf.append(scf)

    return (xp, xs,
            jnp.stack(p_ca), jnp.stack(p_rs), jnp.stack(p_k), jnp.stack(p_v),
            jnp.stack(p_mk), jnp.stack(p_mv), jnp.stack(p_cf),
            jnp.stack(s_ca), jnp.stack(s_rs), jnp.stack(s_k), jnp.stack(s_v), jnp.stack(s_cf))
```

```python
import numpy as np
import concourse.bass as bass
import concourse.mybir as mybir

F32 = mybir.dt.float32
BF16 = mybir.dt.bfloat16
ALU = mybir.AluOpType
AF = mybir.ActivationFunctionType
AX = mybir.AxisListType

ENGS = ["pe", "act", "dve", "pool", "sp"]
EPOCH = 16000


class Buf:
    def __init__(self, name, t, parent=None):
        self.name = name
        self.t = t
        self._p = parent
        self._whole = {"w": None, "r": {}}
        self._parts = {}

    @property
    def whole(self):
        return self._p.whole if self._p is not None else self._whole

    @whole.setter
    def whole(self, v):
        if self._p is not None:
            self._p.whole = v
        else:
            self._whole = v

    @property
    def parts(self):
        return self._p.parts if self._p is not None else self._parts

    @parts.setter
    def parts(self, v):
        if self._p is not None:
            self._p.parts = v
        else:
            self._parts = v

    def __getitem__(self, idx):
        return self.t[idx]


class Op:
    __slots__ = ("eng", "fn", "waits", "signal", "seq", "dma", "dmaval", "sigval")

    def __init__(self, eng, fn):
        self.eng = eng
        self.fn = fn
        self.waits = []
        self.signal = False
        self.seq = 0
        self.dma = None
        self.dmaval = 0
        self.sigval = 0


class Prog:
    def __init__(self, nc):
        self.nc = nc
        self.ops = {e: [] for e in ENGS}
        self.dma_cnt = {}
        self.waited = {e: {} for e in ENGS}

    @staticmethod
    def _norm(lst):
        out = []
        for x in lst:
            if isinstance(x, tuple):
                out.append(x)
            else:
                out.append((x, None))
        return out

    def op(self, eng, fn, reads=(), writes=(), dma=None):
        o = Op(eng, fn)
        o.seq = len(self.ops[eng])
        reads = self._norm(reads)
        writes = self._norm(writes)
        deps = []

        def add(tok):
            if tok is not None:
                deps.append(tok)

        for b, k in reads:
            add(b.whole["w"])
            if k is None:
                for st in b.parts.values():
                    add(st["w"])
            elif k in b.parts:
                add(b.parts[k]["w"])
        for b, k in writes:
            add(b.whole["w"])
            for t in b.whole["r"].values():
                add(t)
            if k is None:
                for st in b.parts.values():
                    add(st["w"])
                    for t in st["r"].values():
                        add(t)
            elif k in b.parts:
                add(b.parts[k]["w"])
                for t in b.parts[k]["r"].values():
                    add(t)
        if dma is not None:
            self.dma_cnt[dma] = self.dma_cnt.get(dma, 0) + 16
            o.dma = dma
            o.dmaval = self.dma_cnt[dma]
            tok = ("d", dma, o.dmaval, o)
        else:
            tok = ("e", eng, o.seq, o)
        wd = self.waited[eng]
        for t in deps:
            if t[0] == "e":
                if t[1] == "pe" and eng == "pe":
                    continue
                key = ("e", t[1])
                val = t[2]
            else:
                key = ("d", t[1])
                val = t[2]
            if wd.get(key, -1) >= val:
                continue
            wd[key] = val
            o.waits.append(t)
            t[3].signal = True
        self.ops[eng].append(o)
        for b, k in reads:
            st = b.whole if k is None else b.parts.setdefault(k, {"w": None, "r": {}})
            rk = (tok[0], tok[1])
            st["r"][rk] = tok
        for b, k in writes:
            if k is None:
                b.parts = {}
                b.whole = {"w": tok, "r": {}}
            else:
                b.parts[k] = {"w": tok, "r": {}}
        return o

    def emit(self, final_wait_eng="sp"):
        nc = self.nc
        sems = {}

        def getsem(key):
            if key not in sems:
                sems[key] = nc.alloc_semaphore("s_%s" % "_".join(str(x) for x in key))
            return sems[key]

        for e in ENGS:
            c = 0
            for o in self.ops[e]:
                if o.dma is None and o.signal:
                    c += 1
                    o.sigval = c
        fin = Op(final_wait_eng, None)
        for dkey, val in self.dma_cnt.items():
            fin.waits.append(("d", dkey, val, None))
        engmap = {"pe": "tensor", "act": "scalar", "dve": "vector", "pool": "gpsimd", "sp": "sync"}
        with nc.Block() as block:
            for e in ENGS:
                ops = self.ops[e] + ([fin] if e == final_wait_eng else [])

                def body(eng, ops=ops, e=e):
                    for o in ops:
                        for t in o.waits:
                            if t[0] == "e":
                                p = t[3]
                                ep = (p.sigval - 1) // EPOCH
                                eng.wait_ge(getsem(("e", t[1], ep)), p.sigval - ep * EPOCH)
                            else:
                                eng.wait_ge(getsem(("d", t[1])), t[2])
                        if o.fn is None:
                            continue
                        ins = o.fn(eng)
                        if o.dma is not None:
                            ins.then_inc(getsem(("d", o.dma)), 16)
                        elif o.signal:
                            ep = (o.sigval - 1) // EPOCH
                            ins.then_inc(getsem(("e", e, ep)), 1)

                getattr(block, engmap[e])(body)
        return len(sems)


import math
import ml_dtypes
from concourse.bass_utils import run_bass_kernel_spmd


D = 2048
FF = 5632
GAM = [1.0 - 2.0 ** (-5 - h) for h in range(4)]
LAM_INIT = [0.8 - 0.6 * math.exp(-0.3 * l) for l in range(2)]
EPS = 1e-6
SEQ = 4096
PAST = 1024


def build():
    nc = bass.Bass("TRN2", target_bir_lowering=False)
    P = Prog(nc)

    def din(name, shape):
        return Buf(name, nc.dram_tensor(name, list(shape), F32, kind="ExternalInput").ap())

    def dout(name, shape):
        return Buf(name, nc.dram_tensor(name, list(shape), F32, kind="ExternalOutput").ap())

    def dscr(name, shape, dt):
        return Buf(name, nc.dram_tensor(name, list(shape), dt).ap())

    def sb(name, shape, dt=F32):
        return Buf(name, nc.alloc_sbuf_tensor(name, list(shape), dt).ap())

    xp = din("xp", [SEQ, D]); xs = din("xs", [2, 64, D]); memp = din("memp", [256, D])
    s_conva = din("s_conva", [2, 2, 128, 4, 30]); s_ret = din("s_ret", [2, 2, 4, 128, 256])
    c_dk = din("c_dk", [2, 2, PAST, 512]); c_dv = din("c_dv", [2, 2, PAST, 512])
    c_mk = din("c_mk", [2, 2, 256, 512]); c_mv = din("c_mv", [2, 2, 256, 512])
    s_convf = din("s_convf", [2, 2, 128, 88, 2])
    w_in = din("w_in", [2, D, 5632]); w_out = din("w_out", [2, D, D])
    w_xq = din("w_xq", [2, D, 512]); w_xk = din("w_xk", [2, D, 512]); w_xv = din("w_xv", [2, D, 512])
    w_xo = din("w_xo", [2, 512, D]); w_up = din("w_up", [2, D, 2 * FF]); w_down = din("w_down", [2, FF, D])
    gT_d = din("gT", [128, 2, 4, 16])
    convaw_d = din("convaw", [128, 2, 4, 31]); convab_d = din("convab", [128, 2, 4])
    lnag_d = din("lnag", [128, 2, 512]); lnab_d = din("lnab", [128, 2, 512])
    retg_d = din("retg", [128, 2, 1024])
    qng_d = din("qng", [128, 2, 64]); kng_d = din("kng", [128, 2, 64])
    lqk_d = din("lqk", [128, 2, 4, 64])
    subg_d = din("subg", [128, 2, 128])
    relb_d = din("relb", [32, 4]); relbT_d = din("relbT", [4, 32]); b15_d = din("b15", [128, 4])
    xqng_d = din("xqng", [128, 2, 128]); xkng_d = din("xkng", [128, 2, 128])
    convfw_d = din("convfw", [128, 2, 88, 3]); convfb_d = din("convfb", [128, 2, 88])
    ident_d = din("ident", [128, 128]); anti_d = din("anti", [128, 128])
    ropeq_d = din("ropeq", [SEQ, 2, 256]); ropek_d = din("ropek", [SEQ, 2, 256])
    dect_d = din("dect", [128, 4, 128]); qdect_d = din("qdect", [128, 4, 128]); kdec_d = din("kdec", [128, 2, 4])
    oh_d = din("oh", [32, 2, 256]); mkq_d = din("mkq", [1, 2, 128])

    y_p = dout("y_p", [SEQ, D]); y_s = dout("y_s", [2, 64, D])
    o_ca_p = dout("o_ca_p", [2, 128, 4, 30]); o_rs_p = dout("o_rs_p", [2, 4, 128, 256])
    o_k_p = dout("o_k_p", [2, SEQ, 512]); o_v_p = dout("o_v_p", [2, SEQ, 512])
    o_mk_p = dout("o_mk_p", [2, 256, 512]); o_mv_p = dout("o_mv_p", [2, 256, 512])
    o_cf_p = dout("o_cf_p", [2, 128, 88, 2])
    o_ca_s = dout("o_ca_s", [2, 2, 128, 4, 30]); o_rs_s = dout("o_rs_s", [2, 2, 4, 128, 256])
    o_k_s = dout("o_k_s", [2, 2, 64, 512]); o_v_s = dout("o_v_s", [2, 2, 64, 512])
    o_cf_s = dout("o_cf_s", [2, 2, 128, 88, 2])

    win_b = [dscr("win_b%d" % l, [11, 128, 16, 512], BF16) for l in range(2)]
    wout_b = [dscr("wout_b%d" % l, [4, 128, 16, 512], BF16) for l in range(2)]
    wxq_b = [dscr("wxq_b%d" % l, [1, 128, 16, 512], BF16) for l in range(2)]
    wxk_b = [dscr("wxk_b%d" % l, [1, 128, 16, 512], BF16) for l in range(2)]
    wxv_b = [dscr("wxv_b%d" % l, [1, 128, 16, 512], BF16) for l in range(2)]
    wxo_b = [dscr("wxo_b%d" % l, [4, 128, 4, 512], BF16) for l in range(2)]
    wup_b = [dscr("wup_b%d" % l, [22, 128, 16, 512], BF16) for l in range(2)]
    wdn_b = [dscr("wdn_b%d" % l, [16, 128, 44, 128], BF16) for l in range(2)]
    KVLEN = SEQ
    Kc = [[dscr("Kc%d_%d" % (l, h), [128, KVLEN], BF16) for h in range(4)] for l in range(2)]
    Vc = [dscr("Vc%d" % l, [KVLEN, 512], BF16) for l in range(2)]
    tsc = dscr("tsc", [4, 2, 256], F32)

    NSM = NS_PROMPT
    TM = 128 * NSM
    xT = sb("xT", [128, 16, TM]); hT = sb("hT", [128, 16, TM], BF16)
    rstd = sb("rstd", [128, TM]); epsb = sb("epsb", [128, 1])
    sqr = [sb("sq%d" % i, [128, TM], BF16) for i in range(2)]
    wring = [sb("wr%d" % i, [128, 8192], BF16) for i in range(2)]
    R2 = sb("R2", [128, max(24 * TM, 4096)], BF16)
    xin = Buf("xin", R2[:, 0:4096].bitcast(F32), parent=R2)
    mixT = Buf("mixT", R2[:, 0:16 * TM].rearrange("p (a b) -> p a b", a=16), parent=R2)
    gTt = Buf("gTt", R2[:, 0:24 * TM].rearrange("p (a b) -> p a b", a=24), parent=R2)
    AEW = 30 + TM
    R1 = sb("R1", [128, max(4 * AEW + 4 * TM, 2064)], F32)
    a_ext = Buf("a_ext", R1[:, 0:4 * AEW].rearrange("p (a b) -> p a b", a=4), parent=R1)
    cacc = Buf("cacc", R1[:, 4 * AEW:4 * AEW + 4 * TM].rearrange("p (a b) -> p a b", a=4), parent=R1)
    Vaug = sb("Vaug", [128, 32, 129], BF16) if VAUG_SEP else Buf("Vaug", R1[:, 0:2064].bitcast(BF16).rearrange("p (a b) -> p a b", a=32), parent=R1)
    identf = sb("identf", [128, 128]); identb = sb("identb", [128, 128], BF16); antib = sb("antib", [128, 128], BF16)
    onesb = sb("onesb", [128, 128], BF16)
    gTs = sb("gTs", [128, 2, 4, 16]); convaw = sb("convaw_s", [128, 2, 4, 31]); convab = sb("convab_s", [128, 2, 4])
    lnag = sb("lnag_s", [128, 2, 512], BF16); lnab = sb("lnab_s", [128, 2, 512], BF16); retg = sb("retg_s", [128, 2, 1024], BF16)
    qng = sb("qng_s", [128, 2, 64]); kng = sb("kng_s", [128, 2, 64]); lqk = sb("lqk_s", [128, 2, 4, 64])
    subg = sb("subg_s", [128, 2, 128]); b15 = sb("b15_s", [128, 4])
    xqng = sb("xqng_s", [128, 2, 128]); xkng = sb("xkng_s", [128, 2, 128])
    convfw = sb("convfw_s", [128, 2, 88, 3]); convfb = sb("convfb_s", [128, 2, 88])
    dect = sb("dect_s", [128, 4, 128]); qdect = sb("qdect_s", [128, 4, 128]); kdec = sb("kdec_s", [128, 2, 4])
    mkq = sb("mkq_s", [1, 2, 128], BF16)
    Hb = sb("Hb", [128, 2, 4, 128], BF16)
    relb = Buf("relb_s", R2[0:32, 0:8].bitcast(F32), parent=R2)
    relbT = Buf("relbT_s", R2[0:4, 8:72].bitcast(F32), parent=R2)
    ohs = Buf("oh_s", R2[0:32, 128:1152].bitcast(F32).rearrange("p (a b) -> p a b", a=2), parent=R2)
    tsb = Buf("tsb", R2[0:4, 1152:2176].bitcast(F32).rearrange("p (a b) -> p a b", a=2), parent=R2)
    Hf = Buf("Hf", R2[:, 2176:4224].bitcast(F32).rearrange("p (a b c) -> p a b c", a=2, b=4), parent=R2)
    lamt = sb("lamt", [128, 2, 8])
    ropeq = sb("ropeq_s", [128, NSM, 2, 256]); ropek = sb("ropek_s", [128, NSM, 2, 256])
    a_hist = [sb("a_hist%d" % l, [128, 4, 30]) for l in range(2)]
    sig = sb("sig", [128, TM])
    Sst = [sb("S%d" % l, [128, 4, 256]) for l in range(2)]; Sbf = [sb("Sbf%d" % l, [128, 4, 256], BF16) for l in range(2)]
    u_hist = [sb("u_hist%d" % l, [128, 88, 2]) for l in range(2)]
    mix_tok = sb("mix_tok", [128, NSM, 2048], BF16)
    KT = sb("KT", [128, KVLEN], BF16)
    PT = [sb("PT%d" % i, [128, TM], BF16) for i in range(2)]
    t1 = sb("t1", [128, 512]); t2 = sb("t2", [128, 512]); t3 = sb("t3", [128, 512]); t4 = sb("t4", [128, 512])
    qr = sb("qr", [128, 512], BF16); kr = sb("kr", [128, 512], BF16)
    ktil = sb("ktil", [128, NSM, 512], BF16)
    vB = sb("vB", [128, NSM, 1024], BF16); sgB = sb("sgB", [128, NSM, 1024], BF16)
    qTr = sb("qTr", [128, NSM, 4, 128], BF16); kTr = sb("kTr", [128, NSM, 4, 128], BF16); qtT = sb("qtT", [128, NSM, 4, 128], BF16)
    STb = sb("STb", [128, 4, 128], BF16); rsm = sb("rsm", [128, 4])
    small = sb("small", [128, 32])
    qn = sb("qn", [128, 512], BF16); knf = sb("knf", [128, 512]); knb = sb("knb", [128, 512], BF16)
    vCf = sb("vCf", [128, 512]); vCb = sb("vCb", [128, 512], BF16)
    cqT = sb("cqT", [128, 4, TM], BF16); KTn = sb("KTn", [128, 4, 128], BF16)
    ot = sb("ot", [128, 128]); ot2 = sb("ot2", [128, 128])
    mKT = [sb("mKT%d" % l, [128, 4, 256], BF16) for l in range(2)]
    mVa = [sb("mVa%d" % l, [128, 2, 4, 129], BF16) for l in range(2)]
    memx = xin; yout = xin
    mo_tok = sb("mo_tok", [128, NSM, 512], BF16); moT = sb("moT", [128, 4, TM], BF16)
    uext = [sb("uext%d" % i, [128, TM + 2]) for i in range(2)]
    cv4 = sb("cv4", [128, 4, TM]); cg = sb("cg", [128, TM]); sgt = sb("sgt", [128, TM])

    banks = [Buf("ps%d" % i, nc.alloc_psum_tensor("ps%d" % i, [128, 512], F32).ap()) for i in range(8)]
    rr = {"A": [0, [0, 1]], "T": [0, [2, 3]], "L": [0, [4, 5]], "O": [0, [6, 7]], "F": [0, [0, 1, 2, 3, 4, 5, 6, 7]]}

    def ps(role):
        r = rr[role]
        b = banks[r[1][r[0] % len(r[1])]]
        r[0] += 1
        return b

    sqi = [0]

    def sqn():
        sqi[0] += 1
        return sqr[sqi[0] % 2]

    def dma(eng, out_ap, in_ap, reads, writes, key):
        P.op(eng, lambda e: e.dma_start(out=out_ap, in_=in_ap), reads=reads, writes=writes, dma=key)

    def load_small(dst, src):
        dma("sp", dst[:], src[:], [src], [dst], "ld_" + dst.name)

    widx = [0]

    def wload(srcbuf, src_ap, a, b):
        slot = widx[0] % 2
        widx[0] += 1
        wb = wring[slot]
        dst = wb[:, 0:a * b].rearrange("p (a b) -> p a b", a=a)
        dma("sp", dst, src_ap, [srcbuf], [wb], "w%d" % slot)
        return wb, dst

    def transpose_bf(dst_ap, src_ap, rows, cols, reads, writes):
        pt = ps("T")
        ptb = pt[:, :].bitcast(BF16)
        P.op("pe", lambda e: e.transpose(ptb[:cols, 0:rows], src_ap, identb[:rows, :rows]), reads=reads + [identb], writes=[pt])
        P.op("act", lambda e: e.activation(out=dst_ap, in_=ptb[:cols, 0:rows], func=AF.Copy), reads=[pt], writes=writes)

    def transpose4(dst3, srcs, rows, reads, writes):
        pt = ps("T")
        ptb = pt[:, :].bitcast(BF16)
        for i, sa in enumerate(srcs):
            P.op("pe", lambda e, i=i, sa=sa: e.transpose(ptb[:, i * rows:(i + 1) * rows], sa, identb[:rows, :rows]), reads=reads + [identb], writes=[pt])
        P.op("act", lambda e: e.activation(out=dst3, in_=ptb[:, 0:4 * rows].rearrange("p (a b) -> p a b", a=4), func=AF.Copy), reads=[pt], writes=writes)

    def rmsnorm_T(src, l, gi, TT):
        pa = ps("A")
        for kc in range(16):
            sq = sqn()
            P.op("act", lambda e, kc=kc, sq=sq: e.activation(out=sq[:, :TT], in_=src[:, kc, :TT], func=AF.Square), reads=[(src, kc)], writes=[sq])
            P.op("pe", lambda e, kc=kc, sq=sq: e.matmul(pa[:, :TT], lhsT=onesb[:, :], rhs=sq[:, :TT], start=(kc == 0), stop=(kc == 15)), reads=[sq, onesb], writes=[pa])
        P.op("act", lambda e: e.activation(out=rstd[:, :TT], in_=pa[:, :TT], func=AF.Sqrt, bias=epsb[:, 0:1], scale=1.0 / D), reads=[pa, epsb], writes=[rstd])
        P.op("dve", lambda e: e.reciprocal(out=rstd[:, :TT], in_=rstd[:, :TT]), reads=[rstd], writes=[rstd])
        for kc in range(16):
            P.op("dve", lambda e, kc=kc: e.scalar_tensor_tensor(out=hT[:, kc, :TT], in0=src[:, kc, :TT], scalar=gTs[:, l, gi, kc:kc + 1], in1=rstd[:, :TT], op0=ALU.mult, op1=ALU.mult),
                 reads=[(src, kc), rstd, gTs], writes=[(hT, kc)])

    def proj_tok(wb, wv, L, ncol, nk=16, lhs=None, s=0):
        lhs = hT if lhs is None else lhs
        pa = ps("A")
        for kc in range(nk):
            P.op("pe", lambda e, kc=kc: e.matmul(pa[:L, :ncol], lhsT=lhs[:, kc, s * L:(s + 1) * L], rhs=wv[:, kc, :ncol], start=(kc == 0), stop=(kc == nk - 1)), reads=[(lhs, kc), wb], writes=[pa])
        return pa

    def proj_feat(wb, wv, j, TT, nk=16, rhs=None, role="A"):
        rhs = hT if rhs is None else rhs
        pa = ps(role)
        for kc in range(nk):
            P.op("pe", lambda e, kc=kc: e.matmul(pa[:, :TT], lhsT=wv[:, kc, j * 128:(j + 1) * 128], rhs=rhs[:, kc, :TT], start=(kc == 0), stop=(kc == nk - 1)), reads=[(rhs, kc), wb], writes=[pa])
        return pa

    def rstd_free(pa, L, ngrp, gsz, col0):
        P.op("act", lambda e: e.activation(out=t4[:L, :ngrp * gsz], in_=pa[:L, :ngrp * gsz], func=AF.Square), reads=[pa], writes=[t4])
        P.op("dve", lambda e: e.reduce_sum(out=small[:L, col0:col0 + ngrp], in_=t4[:L, :ngrp * gsz].rearrange("p (g j) -> p g j", g=ngrp), axis=AX.X), reads=[t4], writes=[small])
        P.op("act", lambda e: e.activation(out=small[:L, col0:col0 + ngrp], in_=small[:L, col0:col0 + ngrp], func=AF.Sqrt, bias=epsb[:L, 0:1], scale=1.0 / gsz), reads=[small, epsb], writes=[small])
        P.op("dve", lambda e: e.reciprocal(out=small[:L, col0:col0 + ngrp], in_=small[:L, col0:col0 + ngrp]), reads=[small], writes=[small])

    P.op("dve", lambda e: e.memset(epsb[:, :], EPS), writes=[epsb])
    P.op("dve", lambda e: e.memset(onesb[:, :], 1.0), writes=[onesb])
    P.op("dve", lambda e: e.memset(Vaug[:, :, 128:129], 1.0), writes=[Vaug])
    for l in range(2):
        P.op("dve", lambda e, l=l: e.memset(mVa[l][:, :, :, 128:129], 1.0), writes=[mVa[l]])
    for dst, src in [(identf, ident_d), (gTs, gT_d), (convaw, convaw_d), (convab, convab_d), (qng, qng_d), (kng, kng_d), (lqk, lqk_d), (subg, subg_d), (b15, b15_d), (relb, relb_d),
                     (relbT, relbT_d), (ohs, oh_d), (xqng, xqng_d), (xkng, xkng_d), (convfw, convfw_d), (convfb, convfb_d),
                     (dect, dect_d), (qdect, qdect_d), (kdec, kdec_d)]:
        load_small(dst, src)
    for dst, src in [(lnag, lnag_d), (lnab, lnab_d), (retg, retg_d)]:
        dma("pool", dst[:], src[:], [src], [dst], "ld_" + dst.name)
    dma("pool", identb[:], ident_d[:], [ident_d], [identb], "ld_identb")
    dma("pool", antib[:], anti_d[:], [anti_d], [antib], "ld_antib")
    dma("pool", mkq[:], mkq_d[:], [mkq_d], [mkq], "ld_mkq")

    for l in range(2):
        for wi, (src, dst) in enumerate([(w_xk, wxk_b), (w_xv, wxv_b), (w_in, win_b), (w_out, wout_b), (w_xq, wxq_b), (w_xo, wxo_b), (w_up, wup_b), (w_down, wdn_b)]):
            nb, _, nk, bw = dst[l].t.shape
            key = "cv%d_%d" % (wi, l)
            for c in range(nb):
                dma("pool", dst[l][c], src[l, :, c * bw:(c + 1) * bw].rearrange("(kc p) c -> p kc c", p=128), [src], [dst[l]], key)
            _last = P.ops["pool"][-1]
            dst[l].whole["w"] = ("d", key, _last.dmaval, _last)

    for l in range(2):
        P.op("dve", lambda e, l=l: e.tensor_tensor(out=t1[:, 0:64], in0=lqk[:, l, 0, :], in1=lqk[:, l, 1, :], op=ALU.mult), reads=[lqk], writes=[t1])
        P.op("dve", lambda e, l=l: e.reduce_sum(out=lamt[:, l, 2:3], in_=t1[:, 0:64], axis=AX.X), reads=[t1], writes=[lamt])
        P.op("dve", lambda e, l=l: e.tensor_tensor(out=t1[:, 64:128], in0=lqk[:, l, 2, :], in1=lqk[:, l, 3, :], op=ALU.mult), reads=[lqk], writes=[t1])
        P.op("dve", lambda e, l=l: e.reduce_sum(out=lamt[:, l, 3:4], in_=t1[:, 64:128], axis=AX.X), reads=[t1], writes=[lamt])
        P.op("act", lambda e, l=l: e.activation(out=lamt[:, l, 4:6], in_=lamt[:, l, 2:4], func=AF.Exp), reads=[lamt], writes=[lamt])
        P.op("dve", lambda e, l=l: e.tensor_tensor(out=lamt[:, l, 6:7], in0=lamt[:, l, 4:5], in1=lamt[:, l, 5:6], op=ALU.subtract), reads=[lamt], writes=[lamt])
        P.op("dve", lambda e, l=l: e.tensor_scalar(out=lamt[:, l, 1:2], in0=lamt[:, l, 6:7], scalar1=LAM_INIT[l], scalar2=-1.0, op0=ALU.add, op1=ALU.mult), reads=[lamt], writes=[lamt])

    pa = ps("A")
    P.op("pe", lambda e: e.matmul(pa[:4, 0:512], lhsT=relb[:, :], rhs=ohs[:, :, :].rearrange("p a b -> p (a b)"), start=True, stop=True), reads=[relb, ohs], writes=[pa])
    P.op("dve", lambda e: e.tensor_scalar(out=tsb[:, :, :].rearrange("p a b -> p (a b)"), in0=pa[:4, 0:512], scalar1=relbT[:, 15:16], scalar2=8.0, op0=ALU.subtract, op1=ALU.mult), reads=[pa, relbT], writes=[tsb])
    dma("pool", tsc[:], tsb[:], [tsb], [tsc], "tsc")
    for ty in range(2):
        for h in range(4):
            src = bass.AP(tensor=tsc.t.tensor, offset=(h * 2 + ty) * 256, ap=[[1, 128], [1, 128]])
            dma("pool", Hf[:, ty, h, :], src, [tsc], [Hf], "Hf")
    P.op("dve", lambda e: e.tensor_copy(out=Hb[:], in_=Hf[:]), reads=[Hf], writes=[Hb])

    def mem_kv_prompt():
        for l in range(2):
            wk, wkv = wload(wxk_b[l], wxk_b[l][0], 16, 512)
            wv_, wvv = wload(wxv_b[l], wxv_b[l][0], 16, 512)
            for mb in range(2):
                dma("pool", memx[:, :], memp[mb * 128:(mb + 1) * 128, :], [memp], [memx], "memx")
                P.op("act", lambda e: e.activation(out=mix_tok[:, 0, :], in_=memx[:, :], func=AF.Square, accum_out=small[:, 20:21]), reads=[memx], writes=[mix_tok, small])
                P.op("act", lambda e: e.activation(out=small[:, 20:21], in_=small[:, 20:21], func=AF.Sqrt, bias=epsb[:, 0:1], scale=1.0 / D), reads=[small, epsb], writes=[small])
                P.op("dve", lambda e: e.reciprocal(out=small[:, 20:21], in_=small[:, 20:21]), reads=[small], writes=[small])
                P.op("dve", lambda e: e.tensor_scalar(out=memx[:, :], in0=memx[:, :], scalar1=small[:, 20:21], scalar2=None, op0=ALU.mult), reads=[memx, small], writes=[memx])
                for kc in range(16):
                    pt = ps("T")
                    P.op("pe", lambda e, kc=kc, pt=pt: e.transpose(pt[:, 0:128], memx[:, kc * 128:(kc + 1) * 128], identf[:, :]), reads=[memx, identf], writes=[pt])
                    P.op("dve", lambda e, kc=kc, pt=pt, l=l: e.tensor_scalar(out=hT[:, kc, 0:128], in0=pt[:, 0:128], scalar1=gTs[:, l, 3, kc:kc + 1], scalar2=None, op0=ALU.mult), reads=[pt, gTs], writes=[(hT, kc)])
                pk = proj_tok(wk, wkv, 128, 512)
                rstd_free(pk, 128, 4, 128, 0)
                for h in range(4):
                    P.op("dve", lambda e, h=h, l=l, pk=pk: e.scalar_tensor_tensor(out=knf[:, h * 128:(h + 1) * 128], in0=pk[:, h * 128:(h + 1) * 128], scalar=small[:, h:h + 1], in1=xkng[:, l, :], op0=ALU.mult, op1=ALU.mult), reads=[pk, small, xkng], writes=[knf])
                dma("pool", o_mk_p[l, mb * 128:(mb + 1) * 128, :], knf[:, :], [knf], [o_mk_p], "o_knf")
                P.op("dve", lambda e: e.tensor_copy(out=knb[:, :], in_=knf[:, :]), reads=[knf], writes=[knb])
                for h in range(4):
                    transpose_bf(mKT[l][:, h, mb * 128:(mb + 1) * 128], knb[:, h * 128:(h + 1) * 128], 128, 128, [knb], [mKT[l]])
                pv = proj_tok(wv_, wvv, 128, 512)
                P.op("act", lambda e, pv=pv: e.activation(out=vCf[:, :], in_=pv[:, 0:512], func=AF.Copy), reads=[pv], writes=[vCf])
                dma("pool", o_mv_p[l, mb * 128:(mb + 1) * 128, :], vCf[:, :], [vCf], [o_mv_p], "o_vCf")
                P.op("dve", lambda e, l=l, mb=mb: e.tensor_copy(out=mVa[l][:, mb, :, 0:128], in_=vCf[:, :].rearrange("p (h d) -> p h d", h=4)), reads=[vCf], writes=[mVa[l]])

    def mem_kv_sample(si):
        for l in range(2):
            for mb in range(2):
                dma("pool", knb[:, :], c_mk[l, si, mb * 128:(mb + 1) * 128, :], [c_mk], [knb], "ld_knb")
                for h in range(4):
                    transpose_bf(mKT[l][:, h, mb * 128:(mb + 1) * 128], knb[:, h * 128:(h + 1) * 128], 128, 128, [knb], [mKT[l]])
                dma("pool", mVa[l][:, mb, :, 0:128], c_mv[l, si, mb * 128:(mb + 1) * 128, :].rearrange("p (h d) -> p h d", h=4), [c_mv], [mVa[l]], "ld_mVa")

    def past_kv_sample(si):
        for l in range(2):
            dma("pool", Vc[l][0:PAST, :], c_dv[l, si, :, :], [c_dv], [Vc[l]], "convV%d" % l)
            for kb in range(PAST // 128):
                dma("pool", knb[:, :], c_dk[l, si, kb * 128:(kb + 1) * 128, :], [c_dk], [knb], "ld_knb")
                for h in range(4):
                    transpose_bf(KTn[:, h, :], knb[:, h * 128:(h + 1) * 128], 128, 128, [knb], [KTn])
                for h in range(4):
                    dma("pool", Kc[l][h][:, kb * 128:(kb + 1) * 128], KTn[:, h, :], [KTn], [Kc[l][h]], "st_KTn")

    def layer(l, stream, pos0, L, NS, okd, ovd):
        TT = L * NS
        gq0 = pos0 // 128
        kvlen = pos0 + TT
        nkb = (kvlen + 127) // 128
        kd = 0 if L == 128 else 1
        gamL = [g ** L for g in GAM]
        sl = lambda s: slice(s * L, (s + 1) * L)
        rmsnorm_T(xT, l, 0, TT)
        for s in range(NS):
            dma("pool", ropeq[:L, s, :, :], ropeq_d[pos0 + s * L:pos0 + (s + 1) * L, :, :], [ropeq_d], [ropeq], "ropeq")
            dma("pool", ropek[:L, s, :, :], ropek_d[pos0 + s * L:pos0 + (s + 1) * L, :, :], [ropek_d], [ropek], "ropek")
        P.op("dve", lambda e: e.tensor_copy(out=a_ext[:, :, 0:30], in_=a_hist[l][:, :, :]), reads=[a_hist[l]], writes=[(a_ext, "a")])
        def conv_gen_f():
            for j in range(4):
                P.op("dve", lambda e, j=j: e.tensor_scalar(out=cacc[:, j, :TT], in0=a_ext[:, j, 0:TT], scalar1=convaw[:, l, j, 0:1], scalar2=convab[:, l, j:j + 1], op0=ALU.mult, op1=ALU.add), reads=[(a_ext, "a"), convaw, convab], writes=[(cacc, j)])
                yield
                for k in range(1, 31):
                    P.op("dve", lambda e, j=j, k=k: e.scalar_tensor_tensor(out=cacc[:, j, :TT], in0=a_ext[:, j, k:k + TT], scalar=convaw[:, l, j, k:k + 1], in1=cacc[:, j, :TT], op0=ALU.mult, op1=ALU.add), reads=[(a_ext, "a"), convaw, (cacc, j)], writes=[(cacc, j)])
                    yield
            P.op("dve", lambda e: e.tensor_copy(out=a_hist[l][:, :, :], in_=a_ext[:, :, TT:TT + 30]), reads=[(a_ext, "a")], writes=[a_hist[l]])
            yield

        conv_gen = conv_gen_f()

        def conv_step(n):
            for _ in range(n):
                if next(conv_gen, "done") == "done":
                    break

        def ln_phase():
            conv_step(1000)
            for s in range(NS):
                pt = ps("T")
                for j in range(4):
                    P.op("pe", lambda e, j=j, s=s, pt=pt: e.transpose(pt[:L, j * 128:(j + 1) * 128], cacc[:, j, sl(s)], identf[:, :]), reads=[(cacc, j), identf], writes=[pt])
                P.op("dve", lambda e, pt=pt: e.reduce_sum(out=small[:L, 8:9], in_=pt[:L, 0:512], axis=AX.X), reads=[pt], writes=[small])
                P.op("dve", lambda e: e.tensor_scalar(out=small[:L, 9:10], in0=small[:L, 8:9], scalar1=-1.0 / 512, scalar2=None, op0=ALU.mult), reads=[small], writes=[small])
                P.op("act", lambda e, pt=pt: e.activation(out=t1[:L, :], in_=pt[:L, 0:512], func=AF.Identity, bias=small[:L, 9:10], scale=1.0), reads=[pt, small], writes=[t1])
                P.op("act", lambda e: e.activation(out=t2[:L, :], in_=t1[:L, :], func=AF.Square, accum_out=small[:L, 10:11]), reads=[t1], writes=[t2, small])
                P.op("act", lambda e: e.activation(out=small[:L, 10:11], in_=small[:L, 10:11], func=AF.Sqrt, bias=epsb[:L, 0:1], scale=1.0 / 512), reads=[small, epsb], writes=[small])
                P.op("dve", lambda e: e.reciprocal(out=small[:L, 10:11], in_=small[:L, 10:11]), reads=[small], writes=[small])
                P.op("dve", lambda e: e.scalar_tensor_tensor(out=t1[:L, :], in0=t1[:L, :], scalar=small[:L, 10:11], in1=lnag[:L, l, :], op0=ALU.mult, op1=ALU.mult), reads=[t1, small, lnag], writes=[t1])
                P.op("dve", lambda e: e.tensor_tensor(out=t1[:L, :], in0=t1[:L, :], in1=lnab[:L, l, :], op=ALU.add), reads=[t1, lnab], writes=[t1])
                P.op("act", lambda e, s=s: e.activation(out=mix_tok[:L, s, 0:512], in_=t1[:L, :], func=AF.Silu), reads=[t1], writes=[(mix_tok, s)])

        def ret_phase():
            for s in range(NS):
                for h in range(4):
                    pS = ps("L")
                    P.op("pe", lambda e, h=h, s=s, pS=pS: e.matmul(pS[:L, :L], lhsT=kTr[:, s, h, :L], rhs=qTr[:, s, h, :L], start=True, stop=True), reads=[(kTr, s), (qTr, s)], writes=[pS])
                    P.op("dve", lambda e, h=h, pS=pS: e.tensor_tensor(out=STb[:L, h, :L], in0=pS[:L, :L], in1=dect[:L, h, :L], op=ALU.mult), reads=[pS, dect], writes=[(STb, h)])
                    pO = ps("O")
                    P.op("pe", lambda e, h=h, s=s, pO=pO: e.matmul(pO[:L, 0:256], lhsT=STb[:L, h, :L], rhs=vB[:L, s, h * 256:(h + 1) * 256], start=True, stop=False), reads=[(STb, h), (vB, s)], writes=[pO])
                    P.op("pe", lambda e, h=h, s=s, pO=pO: e.matmul(pO[:L, 0:256], lhsT=qtT[:, s, h, :L], rhs=Sbf[l][:, h, :], start=False, stop=True), reads=[(qtT, s), Sbf[l]], writes=[pO])
                    pU = ps("O")
                    P.op("pe", lambda e, h=h, s=s, pU=pU: e.matmul(pU[:, 0:256], lhsT=ktil[:L, s, h * 128:(h + 1) * 128], rhs=vB[:L, s, h * 256:(h + 1) * 256], start=True, stop=True), reads=[(ktil, s), (vB, s)], writes=[pU])
                    P.op("dve", lambda e, h=h, pU=pU: e.scalar_tensor_tensor(out=Sst[l][:, h, :], in0=Sst[l][:, h, :], scalar=float(gamL[h]), in1=pU[:, 0:256], op0=ALU.mult, op1=ALU.add), reads=[(Sst[l], h), pU], writes=[(Sst[l], h)])
                    P.op("act", lambda e, h=h: e.activation(out=Sbf[l][:, h, :], in_=Sst[l][:, h, :], func=AF.Copy), reads=[(Sst[l], h)], writes=[(Sbf[l], h)])
                    rt = (t1 if h < 2 else t3)
                    rc = (h % 2) * 256
                    P.op("act", lambda e, pO=pO, h=h: e.activation(out=t2[:L, 0:256], in_=pO[:L, 0:256], func=AF.Square, accum_out=rsm[:L, h:h + 1]), reads=[pO], writes=[t2, (rsm, h)])
                    P.op("act", lambda e, h=h: e.activation(out=rsm[:L, h:h + 1], in_=rsm[:L, h:h + 1], func=AF.Sqrt, bias=epsb[:L, 0:1], scale=1.0 / 256), reads=[(rsm, h), epsb], writes=[(rsm, h)])
                    P.op("dve", lambda e, h=h: e.reciprocal(out=rsm[:L, h:h + 1], in_=rsm[:L, h:h + 1]), reads=[(rsm, h)], writes=[(rsm, h)])
                    P.op("dve", lambda e, h=h, pO=pO, rt=rt, rc=rc: e.scalar_tensor_tensor(out=rt[:L, rc:rc + 256], in0=pO[:L, 0:256], scalar=rsm[:L, h:h + 1], in1=retg[:L, l, h * 256:(h + 1) * 256], op0=ALU.mult, op1=ALU.mult), reads=[pO, (rsm, h), retg], writes=[(rt, h % 2)])
                    P.op("dve", lambda e, h=h, s=s, rt=rt, rc=rc: e.tensor_tensor(out=mix_tok[:L, s, 512 + h * 256:512 + (h + 1) * 256], in0=rt[:L, rc:rc + 256], in1=sgB[:L, s, h * 256:(h + 1) * 256], op=ALU.mult), reads=[(rt, h % 2), (sgB, s)], writes=[(mix_tok, s)])

        for cb in range(1, 11):
            wb, wv = wload(win_b[l], win_b[l][cb], 16, 512)
            if cb == 1:
                wb0, wv0 = wload(win_b[l], win_b[l][0], 16, 512)
                for j in range(4):
                    pg = proj_feat(wb, wv, j, TT)
                    P.op("act", lambda e, pg=pg: e.activation(out=sig[:, :TT], in_=pg[:, :TT], func=AF.Sigmoid), reads=[pg], writes=[sig])
                    pvv = proj_feat(wb0, wv0, j, TT)
                    P.op("dve", lambda e, j=j, pvv=pvv: e.tensor_tensor(out=a_ext[:, j, 30:30 + TT], in0=pvv[:, :TT], in1=sig[:, :TT], op=ALU.mult), reads=[pvv, sig, (a_ext, "a")], writes=[(a_ext, "a")])
                continue
            for s in range(NS):
                pa_ = proj_tok(wb, wv, L, 512, s=s)
                if cb in (2, 3):
                    tab = ropeq if cb == 2 else ropek
                    dst = qr if cb == 2 else kr
                    pv4 = pa_[:L, 0:512].rearrange("p (h d) -> p h d", h=4)
                    cosv = tab[:L, s, 0, :].rearrange("p (h d) -> p h d", h=4)
                    sinv = tab[:L, s, 1, :].rearrange("p (h d) -> p h d", h=4)
                    d4 = dst[:L, :].rearrange("p (h d) -> p h d", h=4)
                    tv = [t[:L, 0:256].rearrange("p (h d) -> p h d", h=4) for t in (t1, t2)]
                    P.op("dve", lambda e, pv4=pv4, cosv=cosv, tv=tv: e.tensor_tensor(out=tv[0], in0=pv4[:, :, 0:64], in1=cosv, op=ALU.mult), reads=[pa_, tab], writes=[t1])
                    P.op("dve", lambda e, pv4=pv4, sinv=sinv, tv=tv: e.tensor_tensor(out=tv[1], in0=pv4[:, :, 64:128], in1=sinv, op=ALU.mult), reads=[pa_, tab], writes=[t2])
                    P.op("dve", lambda e, d4=d4, tv=tv: e.tensor_tensor(out=d4[:, :, 0:64], in0=tv[0], in1=tv[1], op=ALU.subtract), reads=[t1, t2], writes=[dst])
                    P.op("dve", lambda e, pv4=pv4, sinv=sinv, tv=tv: e.tensor_tensor(out=tv[0], in0=pv4[:, :, 0:64], in1=sinv, op=ALU.mult), reads=[pa_, tab], writes=[t1])
                    P.op("dve", lambda e, pv4=pv4, cosv=cosv, tv=tv: e.tensor_tensor(out=tv[1], in0=pv4[:, :, 64:128], in1=cosv, op=ALU.mult), reads=[pa_, tab], writes=[t2])
                    P.op("dve", lambda e, d4=d4, tv=tv: e.tensor_tensor(out=d4[:, :, 64:128], in0=tv[0], in1=tv[1], op=ALU.add), reads=[t1, t2], writes=[dst])
                    dT = qTr if cb == 2 else kTr
                    transpose4(dT[:, s, :, :L], [dst[:L, h * 128:(h + 1) * 128] for h in range(4)], L, [dst], [(dT, s)])
                    if cb == 2:
                        P.op("dve", lambda e, s=s: e.tensor_tensor(out=qtT[:, s, :, :L], in0=qTr[:, s, :, :L], in1=qdect[:, :, :L], op=ALU.mult), reads=[(qTr, s), qdect], writes=[(qtT, s)])
                    else:
                        for h in range(4):
                            P.op("dve", lambda e, h=h, s=s: e.tensor_scalar(out=ktil[:L, s, h * 128:(h + 1) * 128], in0=kr[:L, h * 128:(h + 1) * 128], scalar1=kdec[:L, kd, h:h + 1], scalar2=None, op0=ALU.mult), reads=[kr, kdec], writes=[(ktil, s)])
                elif cb in (4, 5):
                    P.op("act", lambda e, pa_=pa_, cb=cb, s=s: e.activation(out=vB[:L, s, (cb - 4) * 512:(cb - 3) * 512], in_=pa_[:L, 0:512], func=AF.Copy), reads=[pa_], writes=[(vB, s)])
                elif cb in (6, 7):
                    P.op("act", lambda e, pa_=pa_, cb=cb, s=s: e.activation(out=sgB[:L, s, (cb - 6) * 512:(cb - 5) * 512], in_=pa_[:L, 0:512], func=AF.Silu), reads=[pa_], writes=[(sgB, s)])
                elif cb in (8, 9):
                    rstd_free(pa_, L, 8, 64, 0)
                    gsrc = qng if cb == 8 else kng
                    dstf = t3 if cb == 8 else knf
                    for g in range(8):
                        P.op("dve", lambda e, g=g, pa_=pa_, gsrc=gsrc, dstf=dstf: e.scalar_tensor_tensor(out=dstf[:L, g * 64:(g + 1) * 64], in0=pa_[:L, g * 64:(g + 1) * 64], scalar=small[:L, g:g + 1], in1=gsrc[:L, l, :], op0=ALU.mult, op1=ALU.mult),
                             reads=[pa_, small, gsrc], writes=[dstf])
                    if cb == 8:
                        P.op("act", lambda e: e.activation(out=qn[:L, :], in_=t3[:L, :], func=AF.Copy), reads=[t3], writes=[qn])
                        transpose4(cqT[:, :, sl(s)], [qn[:L, h * 128:(h + 1) * 128] for h in range(4)], L, [qn], [cqT])
                    else:
                        dma("pool", okd[s], knf[:L, :], [knf], [stream["okbuf"]], "o_knf")
                        P.op("act", lambda e: e.activation(out=knb[:L, :], in_=knf[:L, :], func=AF.Copy), reads=[knf], writes=[knb])
                        transpose4(KTn[:, :, :L], [knb[:L, h * 128:(h + 1) * 128] for h in range(4)], L, [knb], [KTn])
                        for h in range(4):
                            dma("pool", Kc[l][h][:, pos0 + s * L:pos0 + (s + 1) * L], KTn[:, h, :L], [KTn], [Kc[l][h]], "st_KTn")
                elif cb == 10:
                    P.op("act", lambda e, pa_=pa_: e.activation(out=vCf[:L, :], in_=pa_[:L, 0:512], func=AF.Copy), reads=[pa_], writes=[vCf])
                    dma("pool", ovd[s], vCf[:L, :], [vCf], [stream["ovbuf"]], "o_vCf")
                    P.op("dve", lambda e: e.tensor_copy(out=vCb[:L, :], in_=vCf[:L, :]), reads=[vCf], writes=[vCb])
                    dma("pool", Vc[l][pos0 + s * L:pos0 + (s + 1) * L, :], vCb[:L, :], [vCb], [Vc[l]], "st_vCb")
                conv_step((125 + 6 * NS - 1) // (6 * NS))
            if cb == 7:
                ret_phase()
            if cb == 8:
                ln_phase()

        P.op("dve", lambda e: e.memset(Vaug[:, :, 128:129], 1.0), writes=[Vaug])
        for h in range(4):
            dma("pool", KT[:, 0:kvlen], Kc[l][h][:, 0:kvlen], [Kc[l][h]], [KT], "ld_KT")
            nfull = kvlen // 128
            if nfull > 0:
                dma("pool", Vaug[:, 0:nfull, 0:128], Vc[l][0:nfull * 128, h * 128:(h + 1) * 128].rearrange("(kb p) d -> p kb d", p=128), [Vc[l]], [Vaug], "ld_V")
            if kvlen % 128:
                dma("pool", Vaug[0:64, nfull, 0:128], Vc[l][nfull * 128:nfull * 128 + 64, h * 128:(h + 1) * 128], [Vc[l]], [Vaug], "ld_V")
            pOs = [ps("O") for s in range(NS)]

            def qk_unit(m, kb):
                kl = min(128, kvlen - kb * 128)
                s0 = max(0, kb - gq0)
                ncol = TT - s0 * L
                spec = [(s, 0 if kb == gq0 + s else 1) for s in range(s0, NS) if kb in (gq0 + s, gq0 + s - 1)]
                pL = ps("L")
                P.op("pe", lambda e, m=m, kb=kb, kl=kl, pL=pL, h=h, s0=s0, ncol=ncol, spec=spec: e.matmul(pL[:kl, :ncol], lhsT=KT[64 * m:64 * m + 64, kb * 128:kb * 128 + kl], rhs=cqT[64 * m:64 * m + 64, h, s0 * L:TT], start=True, stop=(len(spec) == 0)), reads=[KT, cqT], writes=[pL])
                nsp = []
                for (s, ty) in spec:
                    nsp.append((s, ty, False))
                    if ty == 0 and L == 128:
                        nsp.append((s, ty, True))
                for i, (s, ty, ismask) in enumerate(nsp):
                    lastf = (i == len(nsp) - 1)
                    c0 = (s - s0) * L
                    if not ismask:
                        P.op("pe", lambda e, kl=kl, pL=pL, ty=ty, h=h, lastf=lastf, c0=c0: e.matmul(pL[:kl, c0:c0 + L], lhsT=antib[:, 0:kl], rhs=Hb[:, ty, h, :L], start=False, stop=lastf), reads=[antib, Hb], writes=[pL])
                    else:
                        P.op("pe", lambda e, pL=pL, lastf=lastf, c0=c0: e.matmul(pL[:128, c0:c0 + 128], lhsT=mkq[0:1, 0, :], rhs=mkq[0:1, 1, :], start=False, stop=lastf), reads=[mkq], writes=[pL])
                return (m, kb, kl, s0, ncol, pL)

            def exp_unit(u, idx):
                (m, kb, kl, s0, ncol, pL) = u
                pt_ = PT[idx % 2]
                P.op("act", lambda e, kl=kl, pL=pL, pt_=pt_, h=h, ncol=ncol: e.activation(out=pt_[:kl, :ncol], in_=pL[:kl, :ncol], func=AF.Exp, bias=b15[:kl, h:h + 1], scale=0.125), reads=[pL, b15], writes=[pt_])
                return pt_

            def pv_unit(u, pt_):
                (m, kb, kl, s0, ncol, pL) = u
                for s in range(s0, NS):
                    c0 = (s - s0) * L
                    P.op("pe", lambda e, m=m, kb=kb, kl=kl, pt_=pt_, s=s, c0=c0, pOs=pOs: e.matmul(pOs[s][:L, m * 256:m * 256 + 129], lhsT=pt_[:kl, c0:c0 + L], rhs=Vaug[:kl, kb, 0:129], start=(kb == 0), stop=(kb == gq0 + s)), reads=[pt_, Vaug], writes=[pOs[s]])

            pend = None
            idx = 0
            for m in range(2):
                for kb in range(nkb):
                    u = qk_unit(m, kb)
                    if pend is not None:
                        pv_unit(*pend)
                    pt_ = exp_unit(u, idx)
                    pend = (u, pt_)
                    idx += 1
            pv_unit(*pend)
            for s in range(NS):
                pO = pOs[s]
                P.op("dve", lambda e, pO=pO: e.reciprocal(out=small[:L, 14:15], in_=pO[:L, 128:129]), reads=[pO], writes=[small])
                P.op("dve", lambda e, pO=pO: e.reciprocal(out=small[:L, 15:16], in_=pO[:L, 256 + 128:256 + 129]), reads=[pO], writes=[small])
                P.op("dve", lambda e: e.tensor_tensor(out=small[:L, 15:16], in0=small[:L, 15:16], in1=lamt[:L, l, 1:2], op=ALU.mult), reads=[small, lamt], writes=[small])
                P.op("dve", lambda e, pO=pO: e.tensor_scalar(out=ot2[:L, :], in0=pO[:L, 256:256 + 128], scalar1=small[:L, 15:16], scalar2=None, op0=ALU.mult), reads=[pO, small], writes=[ot2])
                P.op("dve", lambda e, pO=pO: e.scalar_tensor_tensor(out=ot[:L, :], in0=pO[:L, 0:128], scalar=small[:L, 14:15], in1=ot2[:L, :], op0=ALU.mult, op1=ALU.add), reads=[pO, small, ot2], writes=[ot])
                P.op("act", lambda e: e.activation(out=ot2[:L, :], in_=ot[:L, :], func=AF.Square, accum_out=small[:L, 16:17]), reads=[ot], writes=[ot2, small])
                P.op("act", lambda e: e.activation(out=small[:L, 16:17], in_=small[:L, 16:17], func=AF.Sqrt, bias=epsb[:L, 0:1], scale=1.0 / 128), reads=[small, epsb], writes=[small])
                P.op("dve", lambda e: e.reciprocal(out=small[:L, 16:17], in_=small[:L, 16:17]), reads=[small], writes=[small])
                P.op("dve", lambda e: e.scalar_tensor_tensor(out=ot[:L, :], in0=ot[:L, :], scalar=small[:L, 16:17], in1=subg[:L, l, :], op0=ALU.mult, op1=ALU.mult), reads=[ot, small, subg], writes=[ot])
                P.op("dve", lambda e, h=h, s=s: e.tensor_scalar(out=mix_tok[:L, s, 1536 + h * 128:1536 + (h + 1) * 128], in0=ot[:L, :], scalar1=float(1.0 - LAM_INIT[l]), scalar2=None, op0=ALU.mult), reads=[ot], writes=[(mix_tok, s)])

        for s in range(NS):
            for g4 in range(4):
                transpose4(mixT[:, g4 * 4:g4 * 4 + 4, sl(s)], [mix_tok[:L, s, kc * 128:(kc + 1) * 128] for kc in range(g4 * 4, g4 * 4 + 4)], L, [(mix_tok, s)], [(mixT, g4 * 4 + i) for i in range(4)])
        for cb in range(4):
            wb, wv = wload(wout_b[l], wout_b[l][cb], 16, 512)
            for j in range(4):
                pa_ = proj_feat(wb, wv, j, TT, rhs=mixT, role="F")
                c = cb * 4 + j
                P.op("dve", lambda e, c=c, pa_=pa_: e.tensor_tensor(out=xT[:, c, :TT], in0=xT[:, c, :TT], in1=pa_[:, :TT], op=ALU.add), reads=[(xT, c), pa_], writes=[(xT, c)])

        rmsnorm_T(xT, l, 1, TT)
        wb, wv = wload(wxq_b[l], wxq_b[l][0], 16, 512)
        for s in range(NS):
            pq = proj_tok(wb, wv, L, 512, s=s)
            rstd_free(pq, L, 4, 128, 0)
            for h in range(4):
                P.op("dve", lambda e, h=h, pq=pq: e.scalar_tensor_tensor(out=qn[:L, h * 128:(h + 1) * 128], in0=pq[:L, h * 128:(h + 1) * 128], scalar=small[:L, h:h + 1], in1=xqng[:L, l, :], op0=ALU.mult, op1=ALU.mult), reads=[pq, small, xqng], writes=[qn])
            transpose4(cqT[:, :, sl(s)], [qn[:L, h * 128:(h + 1) * 128] for h in range(4)], L, [qn], [cqT])
        for h in range(4):
            pOs = [ps("O") for s in range(NS)]
            for mb in range(2):
                pL = ps("L")
                P.op("pe", lambda e, h=h, mb=mb, pL=pL: e.matmul(pL[:, :TT], lhsT=mKT[l][:, h, mb * 128:(mb + 1) * 128], rhs=cqT[:, h, :TT], start=True, stop=True), reads=[mKT[l], cqT], writes=[pL])
                pt_ = PT[mb]
                P.op("act", lambda e, pL=pL, pt_=pt_: e.activation(out=pt_[:, :TT], in_=pL[:, :TT], func=AF.Exp, scale=128.0 ** -0.5), reads=[pL], writes=[pt_])
                for s in range(NS):
                    P.op("pe", lambda e, h=h, mb=mb, pt_=pt_, s=s, pOs=pOs: e.matmul(pOs[s][:L, 0:129], lhsT=pt_[:, sl(s)], rhs=mVa[l][:, mb, h, :], start=(mb == 0), stop=(mb == 1)), reads=[pt_, mVa[l]], writes=[pOs[s]])
            for s in range(NS):
                pO = pOs[s]
                P.op("dve", lambda e, pO=pO: e.reciprocal(out=small[:L, 18:19], in_=pO[:L, 128:129]), reads=[pO], writes=[small])
                P.op("dve", lambda e, h=h, pO=pO, s=s: e.tensor_scalar(out=mo_tok[:L, s, h * 128:(h + 1) * 128], in0=pO[:L, 0:128], scalar1=small[:L, 18:19], scalar2=None, op0=ALU.mult), reads=[pO, small], writes=[(mo_tok, s)])
        for s in range(NS):
            transpose4(moT[:, :, sl(s)], [mo_tok[:L, s, h * 128:(h + 1) * 128] for h in range(4)], L, [(mo_tok, s)], [(moT, h) for h in range(4)])
        for cb in range(4):
            wb, wv = wload(wxo_b[l], wxo_b[l][cb], 4, 512)
            for j in range(4):
                pa_ = proj_feat(wb, wv, j, TT, nk=4, rhs=moT, role="F")
                c = cb * 4 + j
                P.op("dve", lambda e, c=c, pa_=pa_: e.tensor_tensor(out=xT[:, c, :TT], in0=xT[:, c, :TT], in1=pa_[:, :TT], op=ALU.add), reads=[(xT, c), pa_], writes=[(xT, c)])

        rmsnorm_T(xT, l, 2, TT)
        for (b0, b1) in ((0, 6), (6, 11)):
            nfc = (b1 - b0) * 4
            for b in range(b0, b1):
                for (blk, off) in ((b, 0), (11 + b, 44)):
                    wb_, wv_ = wload(wup_b[l], wup_b[l][blk], 16, 512)
                    for j in range(4):
                        fc = b * 4 + j
                        pa_ = proj_feat(wb_, wv_, j, TT, role="F")
                        ue = uext[0] if off == 0 else uext[1]
                        ch = off + fc
                        dstc = cv4[:, j, :] if off == 0 else cg[:, :]
                        dkey = (cv4, j) if off == 0 else cg
                        P.op("dve", lambda e, ue=ue, ch=ch: e.tensor_copy(out=ue[:, 0:2], in_=u_hist[l][:, ch, :]), reads=[(u_hist[l], ch)], writes=[ue])
                        P.op("act", lambda e, ue=ue, pa_=pa_: e.activation(out=ue[:, 2:2 + TT], in_=pa_[:, :TT], func=AF.Copy), reads=[pa_], writes=[ue])
                        P.op("dve", lambda e, ue=ue, ch=ch: e.tensor_copy(out=u_hist[l][:, ch, :], in_=ue[:, TT:TT + 2]), reads=[ue], writes=[(u_hist[l], ch)])
                        P.op("act", lambda e, pa_=pa_, ch=ch, dstc=dstc: e.activation(out=dstc[:, :TT], in_=pa_[:, :TT], func=AF.Identity, bias=convfb[:, l, ch:ch + 1], scale=convfw[:, l, ch, 2:3]), reads=[pa_, convfw, convfb], writes=[dkey])
                        for k in (0, 1):
                            P.op("dve", lambda e, ue=ue, ch=ch, dstc=dstc, k=k: e.scalar_tensor_tensor(out=dstc[:, :TT], in0=ue[:, k:k + TT], scalar=convfw[:, l, ch, k:k + 1], in1=dstc[:, :TT], op0=ALU.mult, op1=ALU.add), reads=[ue, convfw, dkey], writes=[dkey])
                        if off == 44:
                            P.op("act", lambda e: e.activation(out=sgt[:, :TT], in_=cg[:, :TT], func=AF.Silu), reads=[cg], writes=[sgt])
                            P.op("dve", lambda e, fc=fc, b0=b0, j=j: e.tensor_tensor(out=gTt[:, fc - b0 * 4, :TT], in0=sgt[:, :TT], in1=cv4[:, j, :TT], op=ALU.mult), reads=[sgt, (cv4, j)], writes=[(gTt, fc - b0 * 4)])
            for c2 in range(8):
                wb, wv2 = wload(wdn_b[l], wdn_b[l][2 * c2:2 * c2 + 2, :, b0 * 4:b1 * 4, :].rearrange("c p f k -> p c f k"), 2, nfc * 128)
                for ci in range(2):
                    c = 2 * c2 + ci
                    pa_ = ps("F")
                    for fi in range(nfc):
                        P.op("pe", lambda e, fi=fi, pa_=pa_, wv2=wv2, nfc=nfc, ci=ci: e.matmul(pa_[:, :TT], lhsT=wv2[:, ci, fi * 128:(fi + 1) * 128], rhs=gTt[:, fi, :TT], start=(fi == 0), stop=(fi == nfc - 1)), reads=[(gTt, fi), wb], writes=[pa_])
                    P.op("dve", lambda e, c=c, pa_=pa_: e.tensor_tensor(out=xT[:, c, :TT], in0=xT[:, c, :TT], in1=pa_[:, :TT], op=ALU.add), reads=[(xT, c), pa_], writes=[(xT, c)])

    def run_group(stream, xsrc, ydst, pos0, L, NS, okd, ovd):
        for s in range(NS):
            dma("pool", xin[:L, :], xsrc[s], [stream["xbuf"]], [xin], "xin")
            for kc in range(16):
                pt = ps("T")
                P.op("pe", lambda e, kc=kc, pt=pt: e.transpose(pt[:, 0:L], xin[:L, kc * 128:(kc + 1) * 128], identf[:L, :L]), reads=[xin, identf], writes=[pt])
                P.op("act", lambda e, kc=kc, pt=pt, s=s: e.activation(out=xT[:, kc, s * L:(s + 1) * L], in_=pt[:, 0:L], func=AF.Copy), reads=[pt], writes=[(xT, kc)])
        for l in range(2):
            layer(l, stream, pos0, L, NS, okd[l], ovd[l])
        for s in range(NS):
            for g in range(4):
                pt = ps("T")
                for j in range(4):
                    kc = g * 4 + j
                    P.op("pe", lambda e, kc=kc, j=j, pt=pt, s=s: e.transpose(pt[:L, j * 128:(j + 1) * 128], xT[:, kc, s * L:(s + 1) * L], identf[:, :]), reads=[(xT, kc), identf], writes=[pt])
                P.op("act", lambda e, g=g, pt=pt: e.activation(out=yout[:L, g * 512:(g + 1) * 512], in_=pt[:L, 0:512], func=AF.Copy), reads=[pt], writes=[yout])
            dma("pool", ydst[s], yout[:L, :], [yout], [stream["ybuf"]], "yout")

    for l in range(2):
        P.op("pool", lambda e, l=l: e.memset(a_hist[l][:], 0.0), writes=[a_hist[l]])
        P.op("pool", lambda e, l=l: e.memset(Sst[l][:], 0.0), writes=[Sst[l]])
        P.op("pool", lambda e, l=l: e.memset(Sbf[l][:], 0.0), writes=[Sbf[l]])
        P.op("pool", lambda e, l=l: e.memset(u_hist[l][:], 0.0), writes=[u_hist[l]])
    mem_kv_prompt()
    pst = {"xbuf": xp, "ybuf": y_p, "okbuf": o_k_p, "ovbuf": o_v_p}
    NSP = NS_PROMPT
    for t in range(NT_PROMPT // NSP):
        p0 = t * 128 * NSP
        blk = lambda s, p0=p0: slice(p0 + s * 128, p0 + (s + 1) * 128)
        run_group(pst, [xp[blk(s), :] for s in range(NSP)], [y_p[blk(s), :] for s in range(NSP)], p0, 128, NSP,
                  [[o_k_p[l, blk(s), :] for s in range(NSP)] for l in range(2)], [[o_v_p[l, blk(s), :] for s in range(NSP)] for l in range(2)])
    for l in range(2):
        dma("pool", o_ca_p[l], a_hist[l][:], [a_hist[l]], [o_ca_p], "o_ca")
        dma("pool", o_rs_p[l].rearrange("h d e -> d h e"), Sst[l][:], [Sst[l]], [o_rs_p], "o_rs")
        dma("pool", o_cf_p[l], u_hist[l][:], [u_hist[l]], [o_cf_p], "o_cf")

    for si in range(2):
        for l in range(2):
            dma("pool", a_hist[l][:], s_conva[l, si], [s_conva], [a_hist[l]], "ld_ah")
            dma("pool", Sst[l][:], s_ret[l, si].rearrange("h d e -> d h e"), [s_ret], [Sst[l]], "ld_S")
            P.op("pool", lambda e, l=l: e.tensor_copy(out=Sbf[l][:], in_=Sst[l][:]), reads=[Sst[l]], writes=[Sbf[l]])
            dma("pool", u_hist[l][:], s_convf[l, si], [s_convf], [u_hist[l]], "ld_uh")
        mem_kv_sample(si)
        past_kv_sample(si)
        sst = {"xbuf": xs, "ybuf": y_s, "okbuf": o_k_s, "ovbuf": o_v_s}
        run_group(sst, [xs[si]], [y_s[si]], PAST, 64, 1, [[o_k_s[l, si]] for l in range(2)], [[o_v_s[l, si]] for l in range(2)])
        for l in range(2):
            dma("pool", o_ca_s[l, si], a_hist[l][:], [a_hist[l]], [o_ca_s], "o_ca")
            dma("pool", o_rs_s[l, si].rearrange("h d e -> d h e"), Sst[l][:], [Sst[l]], [o_rs_s], "o_rs")
            dma("pool", o_cf_s[l, si], u_hist[l][:], [u_hist[l]], [o_cf_s], "o_cf")

    nsem = P.emit()
    return nc


NT_PROMPT = 32
NS_PROMPT = 2
VAUG_SEP = 0
_NC = None


def _bucket(rel):
    n = abs(rel)
    if n < 8:
        v = n
    else:
        v = 8 + int(np.float32(np.log(np.float32(max(n, 1)) / np.float32(8)) / np.float32(math.log(16.0)) * np.float32(8)))
        v = min(v, 15)
    return (16 if rel > 0 else 0) + v


def _consts():
    c = {}
    c["ident"] = np.eye(128, dtype=np.float32)
    c["anti"] = np.ascontiguousarray(np.eye(128, dtype=np.float32)[::-1])
    half = 64
    inv = (1.0 / (10000.0 ** (np.arange(half, dtype=np.float32) / half))).astype(np.float32)
    pos = np.arange(SEQ, dtype=np.float32)
    ang = pos[:, None] * inv[None, :]
    cos = np.cos(ang).astype(np.float32); sin = np.sin(ang).astype(np.float32)
    rq = np.stack([np.tile(cos, (1, 4)), np.tile(sin, (1, 4))], axis=1).astype(np.float32)
    c["ropeq"] = np.ascontiguousarray(rq)
    c["ropek"] = np.ascontiguousarray(rq * np.float32(128.0 ** -0.5))
    lg = np.log(np.array(GAM, dtype=np.float64))
    i = np.arange(128)
    dect = np.zeros((128, 4, 128), np.float32)
    for h in range(4):
        d = i[None, :] - i[:, None]
        dect[:, h, :] = np.where(d >= 0, np.exp(lg[h] * np.maximum(d, 0)), 0.0)
    c["dect"] = dect
    qd = np.zeros((128, 4, 128), np.float32)
    for h in range(4):
        qd[:, h, :] = np.exp(lg[h] * (i + 1.0))[None, :]
    c["qdect"] = qd
    kd = np.zeros((128, 2, 4), np.float32)
    for h in range(4):
        kd[:, 0, h] = np.exp(lg[h] * (127.0 - i))
        kd[:64, 1, h] = np.exp(lg[h] * (63.0 - i[:64]))
    c["kdec"] = kd
    oh = np.zeros((32, 2, 256), np.float32)
    for ty in range(2):
        for u in range(255):
            t = 254 - u
            rel = (t - 127) if ty == 0 else (t - 255)
            oh[_bucket(rel), ty, u] = 1.0
    c["oh"] = oh
    mk = np.zeros((1, 2, 128), np.float32)
    mk[0, 0, 64:] = 1.0
    mk[0, 1, :64] = -240000.0
    c["mkq"] = mk
    return c


def kernel(**inp):
    global _NC
    f = lambda a: np.ascontiguousarray(np.asarray(a, dtype=np.float32))
    I = {k: f(v) for k, v in inp.items()}
    if _NC is None:
        _NC = build()
    nc = _NC
    cst = _consts()
    rep = lambda v: np.ascontiguousarray(np.broadcast_to(v[None], (128,) + v.shape))
    shared = dict(cst)
    for k in ["w_in", "w_out", "w_xq", "w_xk", "w_xv", "w_xo", "w_up", "w_down"]:
        shared[k] = I[k]
    gs = np.stack([I["norm1_g"], I["norm2_g"], I["norm3_g"], I["mem_norm_g"]], axis=1)
    shared["gT"] = np.ascontiguousarray(gs.reshape(2, 4, 16, 128).transpose(3, 0, 1, 2))
    shared["convaw"] = np.ascontiguousarray(I["conv_a_w"].reshape(2, 31, 4, 128).transpose(3, 0, 2, 1))
    shared["convab"] = np.ascontiguousarray(I["conv_a_b"].reshape(2, 4, 128).transpose(2, 0, 1))
    shared["lnag"] = rep(I["ln_a_g"]); shared["lnab"] = rep(I["ln_a_b"]); shared["retg"] = rep(I["ret_gn_g"])
    shared["qng"] = rep(I["diff_qn_g"]); shared["kng"] = rep(I["diff_kn_g"])
    shared["lqk"] = rep(np.stack([I["diff_lq1"], I["diff_lk1"], I["diff_lq2"], I["diff_lk2"]], axis=1))
    shared["subg"] = rep(I["diff_subln_g"])
    shared["relb"] = I["rel_bias"]; shared["relbT"] = np.ascontiguousarray(I["rel_bias"].T)
    shared["b15"] = rep(I["rel_bias"][15])
    shared["xqng"] = rep(I["xqn_g"]); shared["xkng"] = rep(I["xkn_g"])
    shared["convfw"] = np.ascontiguousarray(I["conv_f_w"].reshape(2, 3, 88, 128).transpose(3, 0, 2, 1))
    shared["convfb"] = np.ascontiguousarray(I["conv_f_b"].reshape(2, 88, 128).transpose(2, 0, 1))
    in_maps = []
    for c in range(8):
        m = dict(shared)
        b = c % 2
        ss = [2 * c, 2 * c + 1]
        m["xp"] = I["x_prompt"][b]; m["xs"] = np.ascontiguousarray(I["x_sample"][ss]); m["memp"] = I["mem_prompt"][b]
        sca = I["state_conv_a"][:, ss]
        m["s_conva"] = np.ascontiguousarray(sca.reshape(2, 2, 30, 4, 128).transpose(0, 1, 4, 3, 2))
        m["s_ret"] = np.ascontiguousarray(I["state_ret"][:, ss])
        m["c_dk"] = np.ascontiguousarray(I["cache_diff_k"][:, ss].reshape(2, 2, PAST, 512))
        m["c_dv"] = np.ascontiguousarray(I["cache_diff_v"][:, ss].reshape(2, 2, PAST, 512))
        m["c_mk"] = np.ascontiguousarray(I["cache_mem_k"][:, ss].reshape(2, 2, 256, 512))
        m["c_mv"] = np.ascontiguousarray(I["cache_mem_v"][:, ss].reshape(2, 2, 256, 512))
        scf = I["state_conv_f"][:, ss]
        m["s_convf"] = np.ascontiguousarray(scf.reshape(2, 2, 2, 88, 128).transpose(0, 1, 4, 3, 2))
        in_maps.append(m)
    res = run_bass_kernel_spmd(nc, in_maps, core_ids=list(range(8)))
    R = res.results
    P2 = [R[0], R[1]]
    y_p = np.stack([r["y_p"] for r in P2])
    y_s = np.concatenate([R[c]["y_s"] for c in range(8)], axis=0)
    ca_p = np.stack([r["o_ca_p"].transpose(0, 3, 2, 1).reshape(2, 30, 512) for r in P2], axis=1)
    rs_p = np.stack([r["o_rs_p"] for r in P2], axis=1)
    k_p = np.stack([r["o_k_p"].reshape(2, SEQ, 4, 128) for r in P2], axis=1)
    v_p = np.stack([r["o_v_p"].reshape(2, SEQ, 4, 128) for r in P2], axis=1)
    mk_p = np.stack([r["o_mk_p"].reshape(2, 256, 4, 128) for r in P2], axis=1)
    mv_p = np.stack([r["o_mv_p"].reshape(2, 256, 4, 128) for r in P2], axis=1)
    cf_p = np.stack([r["o_cf_p"].transpose(0, 3, 2, 1).reshape(2, 2, 2 * FF) for r in P2], axis=1)
    ca_s = np.concatenate([R[c]["o_ca_s"].transpose(0, 1, 4, 3, 2).reshape(2, 2, 30, 512) for c in range(8)], axis=1)
    rs_s = np.concatenate([R[c]["o_rs_s"] for c in range(8)], axis=1)
    k_s = np.concatenate([R[c]["o_k_s"].reshape(2, 2, 64, 4, 128) for c in range(8)], axis=1)
    v_s = np.concatenate([R[c]["o_v_s"].reshape(2, 2, 64, 4, 128) for c in range(8)], axis=1)
    cf_s = np.concatenate([R[c]["o_cf_s"].transpose(0, 1, 4, 3, 2).reshape(2, 2, 2, 2 * FF) for c in range(8)], axis=1)
    outs = (y_p, y_s, ca_p, rs_p, k_p, v_p, mk_p, mv_p, cf_p, ca_s, rs_s, k_s, v_s, cf_s)
    return tuple(np.ascontiguousarray(o.astype(np.float32)) for o in outs)
```

```python
import numpy as np
import concourse.bass as bass
import concourse.mybir as mybir

F32 = mybir.dt.float32
BF16 = mybir.dt.bfloat16
ALU = mybir.AluOpType
AF = mybir.ActivationFunctionType
AX = mybir.AxisListType

ENGS = ["pe", "act", "dve", "pool", "sp"]
EPOCH = 16000


class Buf:
    def __init__(self, name, t, parent=None):
        self.name = name
        self.t = t
        self._p = parent
        self._whole = {"w": None, "r": {}}
        self._parts = {}

    @property
    def whole(self):
        return self._p.whole if self._p is not None else self._whole

    @whole.setter
    def whole(self, v):
        if self._p is not None:
            self._p.whole = v
        else:
            self._whole = v

    @property
    def parts(self):
        return self._p.parts if self._p is not None else self._parts

    @parts.setter
    def parts(self, v):
        if self._p is not None:
            self._p.parts = v
        else:
            self._parts = v

    def __getitem__(self, idx):
        return self.t[idx]


class Op:
    __slots__ = ("eng", "fn", "waits", "signal", "seq", "dma", "dmaval", "sigval")

    def __init__(self, eng, fn):
        self.eng = eng
        self.fn = fn
        self.waits = []
        self.signal = False
        self.seq = 0
        self.dma = None
        self.dmaval = 0
        self.sigval = 0


class Prog:
    def __init__(self, nc):
        self.nc = nc
        self.ops = {e: [] for e in ENGS}
        self.dma_cnt = {}
        self.waited = {e: {} for e in ENGS}

    @staticmethod
    def _norm(lst):
        out = []
        for x in lst:
            if isinstance(x, tuple):
                out.append(x)
            else:
                out.append((x, None))
        return out

    def op(self, eng, fn, reads=(), writes=(), dma=None):
        o = Op(eng, fn)
        o.seq = len(self.ops[eng])
        reads = self._norm(reads)
        writes = self._norm(writes)
        deps = []

        def add(tok):
            if tok is not None:
                deps.append(tok)

        for b, k in reads:
            add(b.whole["w"])
            if k is None:
                for st in b.parts.values():
                    add(st["w"])
            elif k in b.parts:
                add(b.parts[k]["w"])
        for b, k in writes:
            add(b.whole["w"])
            for t in b.whole["r"].values():
                add(t)
            if k is None:
                for st in b.parts.values():
                    add(st["w"])
                    for t in st["r"].values():
                        add(t)
            elif k in b.parts:
                add(b.parts[k]["w"])
                for t in b.parts[k]["r"].values():
                    add(t)
        if dma is not None:
            self.dma_cnt[dma] = self.dma_cnt.get(dma, 0) + 16
            o.dma = dma
            o.dmaval = self.dma_cnt[dma]
            tok = ("d", dma, o.dmaval, o)
        else:
            tok = ("e", eng, o.seq, o)
        wd = self.waited[eng]
        for t in deps:
            if t[0] == "e":
                if t[1] == "pe" and eng == "pe":
                    continue
                key = ("e", t[1])
                val = t[2]
            else:
                key = ("d", t[1])
                val = t[2]
            if wd.get(key, -1) >= val:
                continue
            wd[key] = val
            o.waits.append(t)
            t[3].signal = True
        self.ops[eng].append(o)
        for b, k in reads:
            st = b.whole if k is None else b.parts.setdefault(k, {"w": None, "r": {}})
            rk = (tok[0], tok[1])
            st["r"][rk] = tok
        for b, k in writes:
            if k is None:
                b.parts = {}
                b.whole = {"w": tok, "r": {}}
            else:
                b.parts[k] = {"w": tok, "r": {}}
        return o

    def emit(self, final_wait_eng="sp"):
        nc = self.nc
        sems = {}

        def getsem(key):
            if key not in sems:
                sems[key] = nc.alloc_semaphore("s_%s" % "_".join(str(x) for x in key))
            return sems[key]

        for e in ENGS:
            c = 0
            for o in self.ops[e]:
                if o.dma is None and o.signal:
                    c += 1
                    o.sigval = c
        fin = Op(final_wait_eng, None)
        for dkey, val in self.dma_cnt.items():
            fin.waits.append(("d", dkey, val, None))
        engmap = {"pe": "tensor", "act": "scalar", "dve": "vector", "pool": "gpsimd", "sp": "sync"}
        with nc.Block() as block:
            for e in ENGS:
                ops = self.ops[e] + ([fin] if e == final_wait_eng else [])

                def body(eng, ops=ops, e=e):
                    for o in ops:
                        for t in o.waits:
                            if t[0] == "e":
                                p = t[3]
                                ep = (p.sigval - 1) // EPOCH
                                eng.wait_ge(getsem(("e", t[1], ep)), p.sigval - ep * EPOCH)
                            else:
                                eng.wait_ge(getsem(("d", t[1])), t[2])
                        if o.fn is None:
                            continue
                        ins = o.fn(eng)
                        if o.dma is not None:
                            ins.then_inc(getsem(("d", o.dma)), 16)
                        elif o.signal:
                            ep = (o.sigval - 1) // EPOCH
                            ins.then_inc(getsem(("e", e, ep)), 1)

                getattr(block, engmap[e])(body)
        return len(sems)


import math
import ml_dtypes
from concourse.bass_utils import run_bass_kernel_spmd


D = 2048
FF = 5632
GAM = [1.0 - 2.0 ** (-5 - h) for h in range(4)]
LAM_INIT = [0.8 - 0.6 * math.exp(-0.3 * l) for l in range(2)]
EPS = 1e-6
SEQ = 4096
PAST = 1024


def build():
    nc = bass.Bass("TRN2", target_bir_lowering=False)
    P = Prog(nc)

    def din(name, shape):
        return Buf(name, nc.dram_tensor(name, list(shape), F32, kind="ExternalInput").ap())

    def dout(name, shape):
        return Buf(name, nc.dram_tensor(name, list(shape), F32, kind="ExternalOutput").ap())

    def dscr(name, shape, dt):
        return Buf(name, nc.dram_tensor(name, list(shape), dt).ap())

    def sb(name, shape, dt=F32):
        return Buf(name, nc.alloc_sbuf_tensor(name, list(shape), dt).ap())

    xp = din("xp", [SEQ, D]); xs = din("xs", [2, 64, D]); memp = din("memp", [256, D])
    s_conva = din("s_conva", [2, 2, 128, 4, 30]); s_ret = din("s_ret", [2, 2, 4, 128, 256])
    c_dk = din("c_dk", [2, 2, PAST, 512]); c_dv = din("c_dv", [2, 2, PAST, 512])
    c_mk = din("c_mk", [2, 2, 256, 512]); c_mv = din("c_mv", [2, 2, 256, 512])
    s_convf = din("s_convf", [2, 2, 128, 88, 2])
    w_in = din("w_in", [2, D, 5632]); w_out = din("w_out", [2, D, D])
    w_xq = din("w_xq", [2, D, 512]); w_xk = din("w_xk", [2, D, 512]); w_xv = din("w_xv", [2, D, 512])
    w_xo = din("w_xo", [2, 512, D]); w_up = din("w_up", [2, D, 2 * FF]); w_down = din("w_down", [2, FF, D])
    gT_d = din("gT", [128, 2, 4, 16])
    convaw_d = din("convaw", [128, 2, 4, 31]); convab_d = din("convab", [128, 2, 4])
    lnag_d = din("lnag", [128, 2, 512]); lnab_d = din("lnab", [128, 2, 512])
    retg_d = din("retg", [128, 2, 1024])
    qng_d = din("qng", [128, 2, 64]); kng_d = din("kng", [128, 2, 64])
    lqk_d = din("lqk", [128, 2, 4, 64])
    subg_d = din("subg", [128, 2, 128])
    relb_d = din("relb", [32, 4]); relbT_d = din("relbT", [4, 32]); b15_d = din("b15", [128, 4])
    xqng_d = din("xqng", [128, 2, 128]); xkng_d = din("xkng", [128, 2, 128])
    convfw_d = din("convfw", [128, 2, 88, 3]); convfb_d = din("convfb", [128, 2, 88])
    ident_d = din("ident", [128, 128]); anti_d = din("anti", [128, 128])
    ropeq_d = din("ropeq", [SEQ, 2, 256]); ropek_d = din("ropek", [SEQ, 2, 256])
    dect_d = din("dect", [128, 4, 128]); qdect_d = din("qdect", [128, 4, 128]); kdec_d = din("kdec", [128, 2, 4])
    oh_d = din("oh", [32, 2, 256]); mkq_d = din("mkq", [1, 2, 128])

    y_p = dout("y_p", [SEQ, D]); y_s = dout("y_s", [2, 64, D])
    o_ca_p = dout("o_ca_p", [2, 128, 4, 30]); o_rs_p = dout("o_rs_p", [2, 4, 128, 256])
    o_k_p = dout("o_k_p", [2, SEQ, 512]); o_v_p = dout("o_v_p", [2, SEQ, 512])
    o_mk_p = dout("o_mk_p", [2, 256, 512]); o_mv_p = dout("o_mv_p", [2, 256, 512])
    o_cf_p = dout("o_cf_p", [2, 128, 88, 2])
    o_ca_s = dout("o_ca_s", [2, 2, 128, 4, 30]); o_rs_s = dout("o_rs_s", [2, 2, 4, 128, 256])
    o_k_s = dout("o_k_s", [2, 2, 64, 512]); o_v_s = dout("o_v_s", [2, 2, 64, 512])
    o_cf_s = dout("o_cf_s", [2, 2, 128, 88, 2])

    win_b = [dscr("win_b%d" % l, [11, 128, 16, 512], BF16) for l in range(2)]
    wout_b = [dscr("wout_b%d" % l, [4, 128, 16, 512], BF16) for l in range(2)]
    wxq_b = [dscr("wxq_b%d" % l, [1, 128, 16, 512], BF16) for l in range(2)]
    wxk_b = [dscr("wxk_b%d" % l, [1, 128, 16, 512], BF16) for l in range(2)]
    wxv_b = [dscr("wxv_b%d" % l, [1, 128, 16, 512], BF16) for l in range(2)]
    wxo_b = [dscr("wxo_b%d" % l, [4, 128, 4, 512], BF16) for l in range(2)]
    wup_b = [dscr("wup_b%d" % l, [22, 128, 16, 512], BF16) for l in range(2)]
    wdn_b = [dscr("wdn_b%d" % l, [16, 128, 44, 128], BF16) for l in range(2)]
    KVLEN = SEQ
    Kc = [[dscr("Kc%d_%d" % (l, h), [128, KVLEN], BF16) for h in range(4)] for l in range(2)]
    Vc = [dscr("Vc%d" % l, [KVLEN, 512], BF16) for l in range(2)]
    tsc = dscr("tsc", [4, 2, 256], F32)

    NSM = NS_PROMPT
    TM = 128 * NSM
    xT = sb("xT", [128, 16, TM]); hT = sb("hT", [128, 16, TM], BF16)
    rstd = sb("rstd", [128, TM]); epsb = sb("epsb", [128, 1])
    sqr = [sb("sq%d" % i, [128, TM], BF16) for i in range(8)]
    wring = [sb("wr%d" % i, [128, 8192], BF16) for i in range(2)]
    R2 = sb("R2", [128, max(24 * TM, 4096)], BF16)
    xin = Buf("xin", R2[:, 0:4096].bitcast(F32), parent=R2)
    mixT = Buf("mixT", R2[:, 0:16 * TM].rearrange("p (a b) -> p a b", a=16), parent=R2)
    gTt = Buf("gTt", R2[:, 0:24 * TM].rearrange("p (a b) -> p a b", a=24), parent=R2)
    AEW = 30 + TM
    R1 = sb("R1", [128, max(4 * AEW + 4 * TM, 2064)], F32)
    a_ext = Buf("a_ext", R1[:, 0:4 * AEW].rearrange("p (a b) -> p a b", a=4), parent=R1)
    cacc = Buf("cacc", R1[:, 4 * AEW:4 * AEW + 4 * TM].rearrange("p (a b) -> p a b", a=4), parent=R1)
    Vaug = sb("Vaug", [128, 32, 129], BF16) if VAUG_SEP else Buf("Vaug", R1[:, 0:2064].bitcast(BF16).rearrange("p (a b) -> p a b", a=32), parent=R1)
    identf = sb("identf", [128, 128]); identb = sb("identb", [128, 128], BF16); antib = sb("antib", [128, 128], BF16)
    onesb = sb("onesb", [128, 128], BF16)
    gTs = sb("gTs", [128, 2, 4, 16]); convaw = sb("convaw_s", [128, 2, 4, 31]); convab = sb("convab_s", [128, 2, 4])
    lnag = sb("lnag_s", [128, 2, 512], BF16); lnab = sb("lnab_s", [128, 2, 512], BF16); retg = sb("retg_s", [128, 2, 1024], BF16)
    qng = sb("qng_s", [128, 2, 64]); kng = sb("kng_s", [128, 2, 64]); lqk = sb("lqk_s", [128, 2, 4, 64])
    subg = sb("subg_s", [128, 2, 128]); b15 = sb("b15_s", [128, 4])
    xqng = sb("xqng_s", [128, 2, 128]); xkng = sb("xkng_s", [128, 2, 128])
    convfw = sb("convfw_s", [128, 2, 88, 3]); convfb = sb("convfb_s", [128, 2, 88])
    dect = sb("dect_s", [128, 4, 128]); qdect = sb("qdect_s", [128, 4, 128]); kdec = sb("kdec_s", [128, 2, 4])
    mkq = sb("mkq_s", [1, 2, 128], BF16)
    Hb = sb("Hb", [128, 2, 4, 128], BF16)
    relb = Buf("relb_s", R2[0:32, 0:8].bitcast(F32), parent=R2)
    relbT = Buf("relbT_s", R2[0:4, 8:72].bitcast(F32), parent=R2)
    ohs = Buf("oh_s", R2[0:32, 128:1152].bitcast(F32).rearrange("p (a b) -> p a b", a=2), parent=R2)
    tsb = Buf("tsb", R2[0:4, 1152:2176].bitcast(F32).rearrange("p (a b) -> p a b", a=2), parent=R2)
    Hf = Buf("Hf", R2[:, 2176:4224].bitcast(F32).rearrange("p (a b c) -> p a b c", a=2, b=4), parent=R2)
    lamt = sb("lamt", [128, 2, 8])
    ropeq = sb("ropeq_s", [128, NSM, 2, 256]); ropek = sb("ropek_s", [128, NSM, 2, 256])
    a_hist = [sb("a_hist%d" % l, [128, 4, 30]) for l in range(2)]
    sig = sb("sig", [128, TM])
    Sst = [sb("S%d" % l, [128, 4, 256]) for l in range(2)]; Sbf = [sb("Sbf%d" % l, [128, 4, 256], BF16) for l in range(2)]
    u_hist = [sb("u_hist%d" % l, [128, 88, 2]) for l in range(2)]
    mix_tok = sb("mix_tok", [128, NSM, 2048], BF16)
    KT = sb("KT", [128, KVLEN], BF16)
    PT = [sb("PT%d" % i, [128, TM], BF16) for i in range(2)]
    t1 = sb("t1", [128, 512]); t2 = sb("t2", [128, 512]); t3 = sb("t3", [128, 512]); t4 = sb("t4", [128, 512])
    qr = sb("qr", [128, 512], BF16); kr = sb("kr", [128, 512], BF16)
    ktil = sb("ktil", [128, NSM, 512], BF16)
    vB = sb("vB", [128, NSM, 1024], BF16); sgB = sb("sgB", [128, NSM, 1024], BF16)
    qTr = sb("qTr", [128, NSM, 4, 128], BF16); kTr = sb("kTr", [128, NSM, 4, 128], BF16); qtT = sb("qtT", [128, NSM, 4, 128], BF16)
    STb = sb("STb", [128, 4, 128], BF16); rsm = sb("rsm", [128, 4])
    small = sb("small", [128, 32])
    qn = sb("qn", [128, 512], BF16); knf = sb("knf", [128, 512]); knb = sb("knb", [128, 512], BF16)
    vCf = sb("vCf", [128, 512]); vCb = sb("vCb", [128, 512], BF16)
    cqT = sb("cqT", [128, 4, TM], BF16); KTn = sb("KTn", [128, 4, 128], BF16)
    ot = sb("ot", [128, 128]); ot2 = sb("ot2", [128, 128])
    mKT = [sb("mKT%d" % l, [128, 4, 256], BF16) for l in range(2)]
    mVa = [sb("mVa%d" % l, [128, 2, 4, 129], BF16) for l in range(2)]
    memx = xin; yout = xin
    mo_tok = sb("mo_tok", [128, NSM, 512], BF16); moT = sb("moT", [128, 4, TM], BF16)
    uext = [sb("uext%d" % i, [128, TM + 2]) for i in range(2)]
    cv4 = sb("cv4", [128, 4, TM]); cg = sb("cg", [128, TM]); sgt = sb("sgt", [128, TM])

    banks = [Buf("ps%d" % i, nc.alloc_psum_tensor("ps%d" % i, [128, 512], F32).ap()) for i in range(8)]
    rr = {"A": [0, [0, 1]], "T": [0, [2, 3]], "L": [0, [4, 5]], "O": [0, [6, 7]], "F": [0, [0, 1, 2, 3, 4, 5, 6, 7]]}

    def ps(role):
        r = rr[role]
        b = banks[r[1][r[0] % len(r[1])]]
        r[0] += 1
        return b

    sqi = [0]

    def sqn():
        sqi[0] += 1
        return sqr[sqi[0] % 8]

    def dma(eng, out_ap, in_ap, reads, writes, key):
        P.op(eng, lambda e: e.dma_start(out=out_ap, in_=in_ap), reads=reads, writes=writes, dma=key)

    def load_small(dst, src):
        dma("sp", dst[:], src[:], [src], [dst], "ld_" + dst.name)

    widx = [0]

    def wload(srcbuf, src_ap, a, b):
        slot = widx[0] % 2
        widx[0] += 1
        wb = wring[slot]
        dst = wb[:, 0:a * b].rearrange("p (a b) -> p a b", a=a)
        dma("sp", dst, src_ap, [srcbuf], [wb], "w%d" % slot)
        return wb, dst

    def transpose_bf(dst_ap, src_ap, rows, cols, reads, writes):
        pt = ps("T")
        ptb = pt[:, :].bitcast(BF16)
        P.op("pe", lambda e: e.transpose(ptb[:cols, 0:rows], src_ap, identb[:rows, :rows]), reads=reads + [identb], writes=[pt])
        P.op("act", lambda e: e.activation(out=dst_ap, in_=ptb[:cols, 0:rows], func=AF.Copy), reads=[pt], writes=writes)

    def transpose4(dst3, srcs, rows, reads, writes):
        pt = ps("T")
        ptb = pt[:, :].bitcast(BF16)
        for i, sa in enumerate(srcs):
            P.op("pe", lambda e, i=i, sa=sa: e.transpose(ptb[:, i * rows:(i + 1) * rows], sa, identb[:rows, :rows]), reads=reads + [identb], writes=[pt])
        P.op("act", lambda e: e.activation(out=dst3, in_=ptb[:, 0:4 * rows].rearrange("p (a b) -> p a b", a=4), func=AF.Copy), reads=[pt], writes=writes)

    def rmsnorm_T(src, l, gi, TT):
        pa = ps("A")
        for kc in range(16):
            sq = sqn()
            P.op("act", lambda e, kc=kc, sq=sq: e.activation(out=sq[:, :TT], in_=src[:, kc, :TT], func=AF.Square), reads=[(src, kc)], writes=[sq])
            P.op("pe", lambda e, kc=kc, sq=sq: e.matmul(pa[:, :TT], lhsT=onesb[:, :], rhs=sq[:, :TT], start=(kc == 0), stop=(kc == 15)), reads=[sq, onesb], writes=[pa])
        P.op("act", lambda e: e.activation(out=rstd[:, :TT], in_=pa[:, :TT], func=AF.Sqrt, bias=epsb[:, 0:1], scale=1.0 / D), reads=[pa, epsb], writes=[rstd])
        P.op("dve", lambda e: e.reciprocal(out=rstd[:, :TT], in_=rstd[:, :TT]), reads=[rstd], writes=[rstd])
        for kc in range(16):
            P.op("dve", lambda e, kc=kc: e.scalar_tensor_tensor(out=hT[:, kc, :TT], in0=src[:, kc, :TT], scalar=gTs[:, l, gi, kc:kc + 1], in1=rstd[:, :TT], op0=ALU.mult, op1=ALU.mult),
                 reads=[(src, kc), rstd, gTs], writes=[(hT, kc)])

    def proj_tok(wb, wv, L, ncol, nk=16, lhs=None, s=0):
        lhs = hT if lhs is None else lhs
        pa = ps("A")
        for kc in range(nk):
            P.op("pe", lambda e, kc=kc: e.matmul(pa[:L, :ncol], lhsT=lhs[:, kc, s * L:(s + 1) * L], rhs=wv[:, kc, :ncol], start=(kc == 0), stop=(kc == nk - 1)), reads=[(lhs, kc), wb], writes=[pa])
        return pa

    def proj_feat(wb, wv, j, TT, nk=16, rhs=None, role="A"):
        rhs = hT if rhs is None else rhs
        pa = ps(role)
        for kc in range(nk):
            P.op("pe", lambda e, kc=kc: e.matmul(pa[:, :TT], lhsT=wv[:, kc, j * 128:(j + 1) * 128], rhs=rhs[:, kc, :TT], start=(kc == 0), stop=(kc == nk - 1)), reads=[(rhs, kc), wb], writes=[pa])
        return pa

    def rstd_free(pa, L, ngrp, gsz, col0):
        P.op("act", lambda e: e.activation(out=t4[:L, :ngrp * gsz], in_=pa[:L, :ngrp * gsz], func=AF.Square), reads=[pa], writes=[t4])
        P.op("dve", lambda e: e.reduce_sum(out=small[:L, col0:col0 + ngrp], in_=t4[:L, :ngrp * gsz].rearrange("p (g j) -> p g j", g=ngrp), axis=AX.X), reads=[t4], writes=[small])
        P.op("act", lambda e: e.activation(out=small[:L, col0:col0 + ngrp], in_=small[:L, col0:col0 + ngrp], func=AF.Sqrt, bias=epsb[:L, 0:1], scale=1.0 / gsz), reads=[small, epsb], writes=[small])
        P.op("dve", lambda e: e.reciprocal(out=small[:L, col0:col0 + ngrp], in_=small[:L, col0:col0 + ngrp]), reads=[small], writes=[small])

    P.op("dve", lambda e: e.memset(epsb[:, :], EPS), writes=[epsb])
    P.op("dve", lambda e: e.memset(onesb[:, :], 1.0), writes=[onesb])
    P.op("dve", lambda e: e.memset(Vaug[:, :, 128:129], 1.0), writes=[Vaug])
    for l in range(2):
        P.op("dve", lambda e, l=l: e.memset(mVa[l][:, :, :, 128:129], 1.0), writes=[mVa[l]])
    for dst, src in [(identf, ident_d), (gTs, gT_d), (convaw, convaw_d), (convab, convab_d), (qng, qng_d), (kng, kng_d), (lqk, lqk_d), (subg, subg_d), (b15, b15_d), (relb, relb_d),
                     (relbT, relbT_d), (ohs, oh_d), (xqng, xqng_d), (xkng, xkng_d), (convfw, convfw_d), (convfb, convfb_d),
                     (dect, dect_d), (qdect, qdect_d), (kdec, kdec_d)]:
        load_small(dst, src)
    for dst, src in [(lnag, lnag_d), (lnab, lnab_d), (retg, retg_d)]:
        dma("pool", dst[:], src[:], [src], [dst], "ld_" + dst.name)
    dma("pool", identb[:], ident_d[:], [ident_d], [identb], "ld_identb")
    dma("pool", antib[:], anti_d[:], [anti_d], [antib], "ld_antib")
    dma("pool", mkq[:], mkq_d[:], [mkq_d], [mkq], "ld_mkq")

    for l in range(2):
        for wi, (src, dst) in enumerate([(w_xk, wxk_b), (w_xv, wxv_b), (w_in, win_b), (w_out, wout_b), (w_xq, wxq_b), (w_xo, wxo_b), (w_up, wup_b), (w_down, wdn_b)]):
            nb, _, nk, bw = dst[l].t.shape
            key = "cv%d_%d" % (wi, l)
            for c in range(nb):
                dma("pool", dst[l][c], src[l, :, c * bw:(c + 1) * bw].rearrange("(kc p) c -> p kc c", p=128), [src], [dst[l]], key)
            _last = P.ops["pool"][-1]
            dst[l].whole["w"] = ("d", key, _last.dmaval, _last)

    for l in range(2):
        P.op("dve", lambda e, l=l: e.tensor_tensor(out=t1[:, 0:64], in0=lqk[:, l, 0, :], in1=lqk[:, l, 1, :], op=ALU.mult), reads=[lqk], writes=[t1])
        P.op("dve", lambda e, l=l: e.reduce_sum(out=lamt[:, l, 2:3], in_=t1[:, 0:64], axis=AX.X), reads=[t1], writes=[lamt])
        P.op("dve", lambda e, l=l: e.tensor_tensor(out=t1[:, 64:128], in0=lqk[:, l, 2, :], in1=lqk[:, l, 3, :], op=ALU.mult), reads=[lqk], writes=[t1])
        P.op("dve", lambda e, l=l: e.reduce_sum(out=lamt[:, l, 3:4], in_=t1[:, 64:128], axis=AX.X), reads=[t1], writes=[lamt])
        P.op("act", lambda e, l=l: e.activation(out=lamt[:, l, 4:6], in_=lamt[:, l, 2:4], func=AF.Exp), reads=[lamt], writes=[lamt])
        P.op("dve", lambda e, l=l: e.tensor_tensor(out=lamt[:, l, 6:7], in0=lamt[:, l, 4:5], in1=lamt[:, l, 5:6], op=ALU.subtract), reads=[lamt], writes=[lamt])
        P.op("dve", lambda e, l=l: e.tensor_scalar(out=lamt[:, l, 1:2], in0=lamt[:, l, 6:7], scalar1=LAM_INIT[l], scalar2=-1.0, op0=ALU.add, op1=ALU.mult), reads=[lamt], writes=[lamt])

    pa = ps("A")
    P.op("pe", lambda e: e.matmul(pa[:4, 0:512], lhsT=relb[:, :], rhs=ohs[:, :, :].rearrange("p a b -> p (a b)"), start=True, stop=True), reads=[relb, ohs], writes=[pa])
    P.op("dve", lambda e: e.tensor_scalar(out=tsb[:, :, :].rearrange("p a b -> p (a b)"), in0=pa[:4, 0:512], scalar1=relbT[:, 15:16], scalar2=8.0, op0=ALU.subtract, op1=ALU.mult), reads=[pa, relbT], writes=[tsb])
    dma("sp", tsc[:], tsb[:], [tsb], [tsc], "tsc")
    for ty in range(2):
        for h in range(4):
            src = bass.AP(tensor=tsc.t.tensor, offset=(h * 2 + ty) * 256, ap=[[1, 128], [1, 128]])
            dma("sp", Hf[:, ty, h, :], src, [tsc], [Hf], "Hf")
    P.op("dve", lambda e: e.tensor_copy(out=Hb[:], in_=Hf[:]), reads=[Hf], writes=[Hb])

    def mem_kv_prompt():
        for l in range(2):
            wk, wkv = wload(wxk_b[l], wxk_b[l][0], 16, 512)
            wv_, wvv = wload(wxv_b[l], wxv_b[l][0], 16, 512)
            for mb in range(2):
                dma("sp", memx[:, :], memp[mb * 128:(mb + 1) * 128, :], [memp], [memx], "memx")
                P.op("act", lambda e: e.activation(out=mix_tok[:, 0, :], in_=memx[:, :], func=AF.Square, accum_out=small[:, 20:21]), reads=[memx], writes=[mix_tok, small])
                P.op("act", lambda e: e.activation(out=small[:, 20:21], in_=small[:, 20:21], func=AF.Sqrt, bias=epsb[:, 0:1], scale=1.0 / D), reads=[small, epsb], writes=[small])
                P.op("dve", lambda e: e.reciprocal(out=small[:, 20:21], in_=small[:, 20:21]), reads=[small], writes=[small])
                P.op("dve", lambda e: e.tensor_scalar(out=memx[:, :], in0=memx[:, :], scalar1=small[:, 20:21], scalar2=None, op0=ALU.mult), reads=[memx, small], writes=[memx])
                for kc in range(16):
                    pt = ps("T")
                    P.op("pe", lambda e, kc=kc, pt=pt: e.transpose(pt[:, 0:128], memx[:, kc * 128:(kc + 1) * 128], identf[:, :]), reads=[memx, identf], writes=[pt])
                    P.op("dve", lambda e, kc=kc, pt=pt, l=l: e.tensor_scalar(out=hT[:, kc, 0:128], in0=pt[:, 0:128], scalar1=gTs[:, l, 3, kc:kc + 1], scalar2=None, op0=ALU.mult), reads=[pt, gTs], writes=[(hT, kc)])
                pk = proj_tok(wk, wkv, 128, 512)
                rstd_free(pk, 128, 4, 128, 0)
                for h in range(4):
                    P.op("dve", lambda e, h=h, l=l, pk=pk: e.scalar_tensor_tensor(out=knf[:, h * 128:(h + 1) * 128], in0=pk[:, h * 128:(h + 1) * 128], scalar=small[:, h:h + 1], in1=xkng[:, l, :], op0=ALU.mult, op1=ALU.mult), reads=[pk, small, xkng], writes=[knf])
                dma("sp", o_mk_p[l, mb * 128:(mb + 1) * 128, :], knf[:, :], [knf], [o_mk_p], "o_knf")
                P.op("dve", lambda e: e.tensor_copy(out=knb[:, :], in_=knf[:, :]), reads=[knf], writes=[knb])
                for h in range(4):
                    transpose_bf(mKT[l][:, h, mb * 128:(mb + 1) * 128], knb[:, h * 128:(h + 1) * 128], 128, 128, [knb], [mKT[l]])
                pv = proj_tok(wv_, wvv, 128, 512)
                P.op("act", lambda e, pv=pv: e.activation(out=vCf[:, :], in_=pv[:, 0:512], func=AF.Copy), reads=[pv], writes=[vCf])
                dma("sp", o_mv_p[l, mb * 128:(mb + 1) * 128, :], vCf[:, :], [vCf], [o_mv_p], "o_vCf")
                P.op("dve", lambda e, l=l, mb=mb: e.tensor_copy(out=mVa[l][:, mb, :, 0:128], in_=vCf[:, :].rearrange("p (h d) -> p h d", h=4)), reads=[vCf], writes=[mVa[l]])

    def mem_kv_sample(si):
        for l in range(2):
            for mb in range(2):
                dma("pool", knb[:, :], c_mk[l, si, mb * 128:(mb + 1) * 128, :], [c_mk], [knb], "ld_knb")
                for h in range(4):
                    transpose_bf(mKT[l][:, h, mb * 128:(mb + 1) * 128], knb[:, h * 128:(h + 1) * 128], 128, 128, [knb], [mKT[l]])
                dma("pool", mVa[l][:, mb, :, 0:128], c_mv[l, si, mb * 128:(mb + 1) * 128, :].rearrange("p (h d) -> p h d", h=4), [c_mv], [mVa[l]], "ld_mVa")

    def past_kv_sample(si):
        for l in range(2):
            dma("pool", Vc[l][0:PAST, :], c_dv[l, si, :, :], [c_dv], [Vc[l]], "convV%d" % l)
            for kb in range(PAST // 128):
                dma("pool", knb[:, :], c_dk[l, si, kb * 128:(kb + 1) * 128, :], [c_dk], [knb], "ld_knb")
                for h in range(4):
                    transpose_bf(KTn[:, h, :], knb[:, h * 128:(h + 1) * 128], 128, 128, [knb], [KTn])
                for h in range(4):
                    dma("sp", Kc[l][h][:, kb * 128:(kb + 1) * 128], KTn[:, h, :], [KTn], [Kc[l][h]], "st_KTn")

    def layer(l, stream, pos0, L, NS, okd, ovd):
        TT = L * NS
        gq0 = pos0 // 128
        kvlen = pos0 + TT
        nkb = (kvlen + 127) // 128
        kd = 0 if L == 128 else 1
        gamL = [g ** L for g in GAM]
        sl = lambda s: slice(s * L, (s + 1) * L)
        rmsnorm_T(xT, l, 0, TT)
        for s in range(NS):
            dma("sp", ropeq[:L, s, :, :], ropeq_d[pos0 + s * L:pos0 + (s + 1) * L, :, :], [ropeq_d], [ropeq], "ropeq")
            dma("sp", ropek[:L, s, :, :], ropek_d[pos0 + s * L:pos0 + (s + 1) * L, :, :], [ropek_d], [ropek], "ropek")
        P.op("pool", lambda e: e.tensor_copy(out=a_ext[:, :, 0:30], in_=a_hist[l][:, :, :]), reads=[a_hist[l]], writes=[(a_ext, "a")])
        def conv_gen_f():
            for j in range(4):
                P.op("dve", lambda e, j=j: e.tensor_scalar(out=cacc[:, j, :TT], in0=a_ext[:, j, 0:TT], scalar1=convaw[:, l, j, 0:1], scalar2=convab[:, l, j:j + 1], op0=ALU.mult, op1=ALU.add), reads=[(a_ext, "a"), convaw, convab], writes=[(cacc, j)])
                yield
                for k in range(1, 31):
                    P.op("dve", lambda e, j=j, k=k: e.scalar_tensor_tensor(out=cacc[:, j, :TT], in0=a_ext[:, j, k:k + TT], scalar=convaw[:, l, j, k:k + 1], in1=cacc[:, j, :TT], op0=ALU.mult, op1=ALU.add), reads=[(a_ext, "a"), convaw, (cacc, j)], writes=[(cacc, j)])
                    yield
            P.op("pool", lambda e: e.tensor_copy(out=a_hist[l][:, :, :], in_=a_ext[:, :, TT:TT + 30]), reads=[(a_ext, "a")], writes=[a_hist[l]])
            yield

        conv_gen = conv_gen_f()

        def conv_step(n):
            for _ in range(n):
                if next(conv_gen, "done") == "done":
                    break

        def ln_phase():
            conv_step(1000)
            for s in range(NS):
                pt = ps("T")
                for j in range(4):
                    P.op("pe", lambda e, j=j, s=s, pt=pt: e.transpose(pt[:L, j * 128:(j + 1) * 128], cacc[:, j, sl(s)], identf[:, :]), reads=[(cacc, j), identf], writes=[pt])
                P.op("dve", lambda e, pt=pt: e.reduce_sum(out=small[:L, 8:9], in_=pt[:L, 0:512], axis=AX.X), reads=[pt], writes=[small])
                P.op("dve", lambda e: e.tensor_scalar(out=small[:L, 9:10], in0=small[:L, 8:9], scalar1=-1.0 / 512, scalar2=None, op0=ALU.mult), reads=[small], writes=[small])
                P.op("act", lambda e, pt=pt: e.activation(out=t1[:L, :], in_=pt[:L, 0:512], func=AF.Identity, bias=small[:L, 9:10], scale=1.0), reads=[pt, small], writes=[t1])
                P.op("act", lambda e: e.activation(out=t2[:L, :], in_=t1[:L, :], func=AF.Square, accum_out=small[:L, 10:11]), reads=[t1], writes=[t2, small])
                P.op("act", lambda e: e.activation(out=small[:L, 10:11], in_=small[:L, 10:11], func=AF.Sqrt, bias=epsb[:L, 0:1], scale=1.0 / 512), reads=[small, epsb], writes=[small])
                P.op("dve", lambda e: e.reciprocal(out=small[:L, 10:11], in_=small[:L, 10:11]), reads=[small], writes=[small])
                P.op("dve", lambda e: e.scalar_tensor_tensor(out=t1[:L, :], in0=t1[:L, :], scalar=small[:L, 10:11], in1=lnag[:L, l, :], op0=ALU.mult, op1=ALU.mult), reads=[t1, small, lnag], writes=[t1])
                P.op("dve", lambda e: e.tensor_tensor(out=t1[:L, :], in0=t1[:L, :], in1=lnab[:L, l, :], op=ALU.add), reads=[t1, lnab], writes=[t1])
                P.op("act", lambda e, s=s: e.activation(out=mix_tok[:L, s, 0:512], in_=t1[:L, :], func=AF.Silu), reads=[t1], writes=[(mix_tok, s)])

        def ret_phase():
            for s in range(NS):
                for h in range(4):
                    pS = ps("L")
                    P.op("pe", lambda e, h=h, s=s, pS=pS: e.matmul(pS[:L, :L], lhsT=kTr[:, s, h, :L], rhs=qTr[:, s, h, :L], start=True, stop=True), reads=[(kTr, s), (qTr, s)], writes=[pS])
                    P.op("dve", lambda e, h=h, pS=pS: e.tensor_tensor(out=STb[:L, h, :L], in0=pS[:L, :L], in1=dect[:L, h, :L], op=ALU.mult), reads=[pS, dect], writes=[(STb, h)])
                    pO = ps("O")
                    P.op("pe", lambda e, h=h, s=s, pO=pO: e.matmul(pO[:L, 0:256], lhsT=STb[:L, h, :L], rhs=vB[:L, s, h * 256:(h + 1) * 256], start=True, stop=False), reads=[(STb, h), (vB, s)], writes=[pO])
                    P.op("pe", lambda e, h=h, s=s, pO=pO: e.matmul(pO[:L, 0:256], lhsT=qtT[:, s, h, :L], rhs=Sbf[l][:, h, :], start=False, stop=True), reads=[(qtT, s), Sbf[l]], writes=[pO])
                    pU = ps("O")
                    P.op("pe", lambda e, h=h, s=s, pU=pU: e.matmul(pU[:, 0:256], lhsT=ktil[:L, s, h * 128:(h + 1) * 128], rhs=vB[:L, s, h * 256:(h + 1) * 256], start=True, stop=True), reads=[(ktil, s), (vB, s)], writes=[pU])
                    P.op("dve", lambda e, h=h, pU=pU: e.scalar_tensor_tensor(out=Sst[l][:, h, :], in0=Sst[l][:, h, :], scalar=float(gamL[h]), in1=pU[:, 0:256], op0=ALU.mult, op1=ALU.add), reads=[(Sst[l], h), pU], writes=[(Sst[l], h)])
                    P.op("pool", lambda e, h=h: e.tensor_copy(out=Sbf[l][:, h, :], in_=Sst[l][:, h, :]), reads=[(Sst[l], h)], writes=[(Sbf[l], h)])
                    rt = (t1 if h < 2 else t3)
                    rc = (h % 2) * 256
                    P.op("act", lambda e, pO=pO, h=h: e.activation(out=t2[:L, 0:256], in_=pO[:L, 0:256], func=AF.Square, accum_out=rsm[:L, h:h + 1]), reads=[pO], writes=[t2, (rsm, h)])
                    P.op("act", lambda e, h=h: e.activation(out=rsm[:L, h:h + 1], in_=rsm[:L, h:h + 1], func=AF.Sqrt, bias=epsb[:L, 0:1], scale=1.0 / 256), reads=[(rsm, h), epsb], writes=[(rsm, h)])
                    P.op("dve", lambda e, h=h: e.reciprocal(out=rsm[:L, h:h + 1], in_=rsm[:L, h:h + 1]), reads=[(rsm, h)], writes=[(rsm, h)])
                    P.op("dve", lambda e, h=h, pO=pO, rt=rt, rc=rc: e.scalar_tensor_tensor(out=rt[:L, rc:rc + 256], in0=pO[:L, 0:256], scalar=rsm[:L, h:h + 1], in1=retg[:L, l, h * 256:(h + 1) * 256], op0=ALU.mult, op1=ALU.mult), reads=[pO, (rsm, h), retg], writes=[(rt, h % 2)])
                    P.op("dve", lambda e, h=h, s=s, rt=rt, rc=rc: e.tensor_tensor(out=mix_tok[:L, s, 512 + h * 256:512 + (h + 1) * 256], in0=rt[:L, rc:rc + 256], in1=sgB[:L, s, h * 256:(h + 1) * 256], op=ALU.mult), reads=[(rt, h % 2), (sgB, s)], writes=[(mix_tok, s)])

        for cb in range(1, 11):
            wb, wv = wload(win_b[l], win_b[l][cb], 16, 512)
            if cb == 1:
                wb0, wv0 = wload(win_b[l], win_b[l][0], 16, 512)
                for j in range(4):
                    pg = proj_feat(wb, wv, j, TT)
                    P.op("act", lambda e, pg=pg: e.activation(out=sig[:, :TT], in_=pg[:, :TT], func=AF.Sigmoid), reads=[pg], writes=[sig])
                    pvv = proj_feat(wb0, wv0, j, TT)
                    P.op("dve", lambda e, j=j, pvv=pvv: e.tensor_tensor(out=a_ext[:, j, 30:30 + TT], in0=pvv[:, :TT], in1=sig[:, :TT], op=ALU.mult), reads=[pvv, sig, (a_ext, "a")], writes=[(a_ext, "a")])
                continue
            for s in range(NS):
                pa_ = proj_tok(wb, wv, L, 512, s=s)
                if cb in (2, 3):
                    tab = ropeq if cb == 2 else ropek
                    dst = qr if cb == 2 else kr
                    pv4 = pa_[:L, 0:512].rearrange("p (h d) -> p h d", h=4)
                    cosv = tab[:L, s, 0, :].rearrange("p (h d) -> p h d", h=4)
                    sinv = tab[:L, s, 1, :].rearrange("p (h d) -> p h d", h=4)
                    d4 = dst[:L, :].rearrange("p (h d) -> p h d", h=4)
                    tv = [t[:L, 0:256].rearrange("p (h d) -> p h d", h=4) for t in (t1, t2)]
                    P.op("dve", lambda e, pv4=pv4, cosv=cosv, tv=tv: e.tensor_tensor(out=tv[0], in0=pv4[:, :, 0:64], in1=cosv, op=ALU.mult), reads=[pa_, tab], writes=[t1])
                    P.op("dve", lambda e, pv4=pv4, sinv=sinv, tv=tv: e.tensor_tensor(out=tv[1], in0=pv4[:, :, 64:128], in1=sinv, op=ALU.mult), reads=[pa_, tab], writes=[t2])
                    P.op("dve", lambda e, d4=d4, tv=tv: e.tensor_tensor(out=d4[:, :, 0:64], in0=tv[0], in1=tv[1], op=ALU.subtract), reads=[t1, t2], writes=[dst])
                    P.op("dve", lambda e, pv4=pv4, sinv=sinv, tv=tv: e.tensor_tensor(out=tv[0], in0=pv4[:, :, 0:64], in1=sinv, op=ALU.mult), reads=[pa_, tab], writes=[t1])
                    P.op("dve", lambda e, pv4=pv4, cosv=cosv, tv=tv: e.tensor_tensor(out=tv[1], in0=pv4[:, :, 64:128], in1=cosv, op=ALU.mult), reads=[pa_, tab], writes=[t2])
                    P.op("dve", lambda e, d4=d4, tv=tv: e.tensor_tensor(out=d4[:, :, 64:128], in0=tv[0], in1=tv[1], op=ALU.add), reads=[t1, t2], writes=[dst])
                    dT = qTr if cb == 2 else kTr
                    transpose4(dT[:, s, :, :L], [dst[:L, h * 128:(h + 1) * 128] for h in range(4)], L, [dst], [(dT, s)])
                    if cb == 2:
                        P.op("dve", lambda e, s=s: e.tensor_tensor(out=qtT[:, s, :, :L], in0=qTr[:, s, :, :L], in1=qdect[:, :, :L], op=ALU.mult), reads=[(qTr, s), qdect], writes=[(qtT, s)])
                    else:
                        for h in range(4):
                            P.op("dve", lambda e, h=h, s=s: e.tensor_scalar(out=ktil[:L, s, h * 128:(h + 1) * 128], in0=kr[:L, h * 128:(h + 1) * 128], scalar1=kdec[:L, kd, h:h + 1], scalar2=None, op0=ALU.mult), reads=[kr, kdec], writes=[(ktil, s)])
                elif cb in (4, 5):
                    P.op("act", lambda e, pa_=pa_, cb=cb, s=s: e.activation(out=vB[:L, s, (cb - 4) * 512:(cb - 3) * 512], in_=pa_[:L, 0:512], func=AF.Copy), reads=[pa_], writes=[(vB, s)])
                elif cb in (6, 7):
                    P.op("act", lambda e, pa_=pa_, cb=cb, s=s: e.activation(out=sgB[:L, s, (cb - 6) * 512:(cb - 5) * 512], in_=pa_[:L, 0:512], func=AF.Silu), reads=[pa_], writes=[(sgB, s)])
                elif cb in (8, 9):
                    rstd_free(pa_, L, 8, 64, 0)
                    gsrc = qng if cb == 8 else kng
                    dstf = t3 if cb == 8 else knf
                    for g in range(8):
                        P.op("dve", lambda e, g=g, pa_=pa_, gsrc=gsrc, dstf=dstf: e.scalar_tensor_tensor(out=dstf[:L, g * 64:(g + 1) * 64], in0=pa_[:L, g * 64:(g + 1) * 64], scalar=small[:L, g:g + 1], in1=gsrc[:L, l, :], op0=ALU.mult, op1=ALU.mult),
                             reads=[pa_, small, gsrc], writes=[dstf])
                    if cb == 8:
                        P.op("act", lambda e: e.activation(out=qn[:L, :], in_=t3[:L, :], func=AF.Copy), reads=[t3], writes=[qn])
                        transpose4(cqT[:, :, sl(s)], [qn[:L, h * 128:(h + 1) * 128] for h in range(4)], L, [qn], [cqT])
                    else:
                        dma("sp", okd[s], knf[:L, :], [knf], [stream["okbuf"]], "o_knf")
                        P.op("act", lambda e: e.activation(out=knb[:L, :], in_=knf[:L, :], func=AF.Copy), reads=[knf], writes=[knb])
                        transpose4(KTn[:, :, :L], [knb[:L, h * 128:(h + 1) * 128] for h in range(4)], L, [knb], [KTn])
                        for h in range(4):
                            dma("sp", Kc[l][h][:, pos0 + s * L:pos0 + (s + 1) * L], KTn[:, h, :L], [KTn], [Kc[l][h]], "st_KTn")
                elif cb == 10:
                    P.op("act", lambda e, pa_=pa_: e.activation(out=vCf[:L, :], in_=pa_[:L, 0:512], func=AF.Copy), reads=[pa_], writes=[vCf])
                    dma("sp", ovd[s], vCf[:L, :], [vCf], [stream["ovbuf"]], "o_vCf")
                    P.op("dve", lambda e: e.tensor_copy(out=vCb[:L, :], in_=vCf[:L, :]), reads=[vCf], writes=[vCb])
                    dma("sp", Vc[l][pos0 + s * L:pos0 + (s + 1) * L, :], vCb[:L, :], [vCb], [Vc[l]], "st_vCb")
                conv_step((125 + 6 * NS - 1) // (6 * NS))
            if cb == 7:
                ret_phase()
            if cb == 8:
                ln_phase()

        P.op("pool", lambda e: e.memset(Vaug[:, :, 128:129], 1.0), writes=[Vaug])
        for h in range(4):
            dma("sp", KT[:, 0:kvlen], Kc[l][h][:, 0:kvlen], [Kc[l][h]], [KT], "ld_KT")
            nfull = kvlen // 128
            if nfull > 0:
                dma("sp", Vaug[:, 0:nfull, 0:128], Vc[l][0:nfull * 128, h * 128:(h + 1) * 128].rearrange("(kb p) d -> p kb d", p=128), [Vc[l]], [Vaug], "ld_V")
            if kvlen % 128:
                dma("sp", Vaug[0:64, nfull, 0:128], Vc[l][nfull * 128:nfull * 128 + 64, h * 128:(h + 1) * 128], [Vc[l]], [Vaug], "ld_V")
            pOs = [ps("O") for s in range(NS)]

            def qk_unit(m, kb):
                kl = min(128, kvlen - kb * 128)
                s0 = max(0, kb - gq0)
                ncol = TT - s0 * L
                spec = [(s, 0 if kb == gq0 + s else 1) for s in range(s0, NS) if kb in (gq0 + s, gq0 + s - 1)]
                pL = ps("L")
                P.op("pe", lambda e, m=m, kb=kb, kl=kl, pL=pL, h=h, s0=s0, ncol=ncol, spec=spec: e.matmul(pL[:kl, :ncol], lhsT=KT[64 * m:64 * m + 64, kb * 128:kb * 128 + kl], rhs=cqT[64 * m:64 * m + 64, h, s0 * L:TT], start=True, stop=(len(spec) == 0)), reads=[KT, cqT], writes=[pL])
                nsp = []
                for (s, ty) in spec:
                    nsp.append((s, ty, False))
                    if ty == 0 and L == 128:
                        nsp.append((s, ty, True))
                for i, (s, ty, ismask) in enumerate(nsp):
                    lastf = (i == len(nsp) - 1)
                    c0 = (s - s0) * L
                    if not ismask:
                        P.op("pe", lambda e, kl=kl, pL=pL, ty=ty, h=h, lastf=lastf, c0=c0: e.matmul(pL[:kl, c0:c0 + L], lhsT=antib[:, 0:kl], rhs=Hb[:, ty, h, :L], start=False, stop=lastf), reads=[antib, Hb], writes=[pL])
                    else:
                        P.op("pe", lambda e, pL=pL, lastf=lastf, c0=c0: e.matmul(pL[:128, c0:c0 + 128], lhsT=mkq[0:1, 0, :], rhs=mkq[0:1, 1, :], start=False, stop=lastf), reads=[mkq], writes=[pL])
                return (m, kb, kl, s0, ncol, pL)

            def exp_unit(u, idx):
                (m, kb, kl, s0, ncol, pL) = u
                pt_ = PT[idx % 2]
                P.op("act", lambda e, kl=kl, pL=pL, pt_=pt_, h=h, ncol=ncol: e.activation(out=pt_[:kl, :ncol], in_=pL[:kl, :ncol], func=AF.Exp, bias=b15[:kl, h:h + 1], scale=0.125), reads=[pL, b15], writes=[pt_])
                return pt_

            def pv_unit(u, pt_):
                (m, kb, kl, s0, ncol, pL) = u
                for s in range(s0, NS):
                    c0 = (s - s0) * L
                    P.op("pe", lambda e, m=m, kb=kb, kl=kl, pt_=pt_, s=s, c0=c0, pOs=pOs: e.matmul(pOs[s][:L, m * 256:m * 256 + 129], lhsT=pt_[:kl, c0:c0 + L], rhs=Vaug[:kl, kb, 0:129], start=(kb == 0), stop=(kb == gq0 + s)), reads=[pt_, Vaug], writes=[pOs[s]])

            pend = None
            idx = 0
            for m in range(2):
                for kb in range(nkb):
                    u = qk_unit(m, kb)
                    if pend is not None:
                        pv_unit(*pend)
                    pt_ = exp_unit(u, idx)
                    pend = (u, pt_)
                    idx += 1
            pv_unit(*pend)
            for s in range(NS):
                pO = pOs[s]
                P.op("dve", lambda e, pO=pO: e.reciprocal(out=small[:L, 14:15], in_=pO[:L, 128:129]), reads=[pO], writes=[small])
                P.op("dve", lambda e, pO=pO: e.reciprocal(out=small[:L, 15:16], in_=pO[:L, 256 + 128:256 + 129]), reads=[pO], writes=[small])
                P.op("dve", lambda e: e.tensor_tensor(out=small[:L, 15:16], in0=small[:L, 15:16], in1=lamt[:L, l, 1:2], op=ALU.mult), reads=[small, lamt], writes=[small])
                P.op("dve", lambda e, pO=pO: e.tensor_scalar(out=ot2[:L, :], in0=pO[:L, 256:256 + 128], scalar1=small[:L, 15:16], scalar2=None, op0=ALU.mult), reads=[pO, small], writes=[ot2])
                P.op("dve", lambda e, pO=pO: e.scalar_tensor_tensor(out=ot[:L, :], in0=pO[:L, 0:128], scalar=small[:L, 14:15], in1=ot2[:L, :], op0=ALU.mult, op1=ALU.add), reads=[pO, small, ot2], writes=[ot])
                P.op("act", lambda e: e.activation(out=ot2[:L, :], in_=ot[:L, :], func=AF.Square, accum_out=small[:L, 16:17]), reads=[ot], writes=[ot2, small])
                P.op("act", lambda e: e.activation(out=small[:L, 16:17], in_=small[:L, 16:17], func=AF.Sqrt, bias=epsb[:L, 0:1], scale=1.0 / 128), reads=[small, epsb], writes=[small])
                P.op("dve", lambda e: e.reciprocal(out=small[:L, 16:17], in_=small[:L, 16:17]), reads=[small], writes=[small])
                P.op("dve", lambda e: e.scalar_tensor_tensor(out=ot[:L, :], in0=ot[:L, :], scalar=small[:L, 16:17], in1=subg[:L, l, :], op0=ALU.mult, op1=ALU.mult), reads=[ot, small, subg], writes=[ot])
                P.op("dve", lambda e, h=h, s=s: e.tensor_scalar(out=mix_tok[:L, s, 1536 + h * 128:1536 + (h + 1) * 128], in0=ot[:L, :], scalar1=float(1.0 - LAM_INIT[l]), scalar2=None, op0=ALU.mult), reads=[ot], writes=[(mix_tok, s)])

        for s in range(NS):
            for g4 in range(4):
                transpose4(mixT[:, g4 * 4:g4 * 4 + 4, sl(s)], [mix_tok[:L, s, kc * 128:(kc + 1) * 128] for kc in range(g4 * 4, g4 * 4 + 4)], L, [(mix_tok, s)], [(mixT, g4 * 4 + i) for i in range(4)])
        for cb in range(4):
            wb, wv = wload(wout_b[l], wout_b[l][cb], 16, 512)
            for j in range(4):
                pa_ = proj_feat(wb, wv, j, TT, rhs=mixT, role="F")
                c = cb * 4 + j
                P.op("dve", lambda e, c=c, pa_=pa_: e.tensor_tensor(out=xT[:, c, :TT], in0=xT[:, c, :TT], in1=pa_[:, :TT], op=ALU.add), reads=[(xT, c), pa_], writes=[(xT, c)])

        rmsnorm_T(xT, l, 1, TT)
        wb, wv = wload(wxq_b[l], wxq_b[l][0], 16, 512)
        for s in range(NS):
            pq = proj_tok(wb, wv, L, 512, s=s)
            rstd_free(pq, L, 4, 128, 0)
            for h in range(4):
                P.op("dve", lambda e, h=h, pq=pq: e.scalar_tensor_tensor(out=qn[:L, h * 128:(h + 1) * 128], in0=pq[:L, h * 128:(h + 1) * 128], scalar=small[:L, h:h + 1], in1=xqng[:L, l, :], op0=ALU.mult, op1=ALU.mult), reads=[pq, small, xqng], writes=[qn])
            transpose4(cqT[:, :, sl(s)], [qn[:L, h * 128:(h + 1) * 128] for h in range(4)], L, [qn], [cqT])
        for h in range(4):
            pOs = [ps("O") for s in range(NS)]
            for mb in range(2):
                pL = ps("L")
                P.op("pe", lambda e, h=h, mb=mb, pL=pL: e.matmul(pL[:, :TT], lhsT=mKT[l][:, h, mb * 128:(mb + 1) * 128], rhs=cqT[:, h, :TT], start=True, stop=True), reads=[mKT[l], cqT], writes=[pL])
                pt_ = PT[mb]
                P.op("act", lambda e, pL=pL, pt_=pt_: e.activation(out=pt_[:, :TT], in_=pL[:, :TT], func=AF.Exp, scale=128.0 ** -0.5), reads=[pL], writes=[pt_])
                for s in range(NS):
                    P.op("pe", lambda e, h=h, mb=mb, pt_=pt_, s=s, pOs=pOs: e.matmul(pOs[s][:L, 0:129], lhsT=pt_[:, sl(s)], rhs=mVa[l][:, mb, h, :], start=(mb == 0), stop=(mb == 1)), reads=[pt_, mVa[l]], writes=[pOs[s]])
            for s in range(NS):
                pO = pOs[s]
                P.op("dve", lambda e, pO=pO: e.reciprocal(out=small[:L, 18:19], in_=pO[:L, 128:129]), reads=[pO], writes=[small])
                P.op("dve", lambda e, h=h, pO=pO, s=s: e.tensor_scalar(out=mo_tok[:L, s, h * 128:(h + 1) * 128], in0=pO[:L, 0:128], scalar1=small[:L, 18:19], scalar2=None, op0=ALU.mult), reads=[pO, small], writes=[(mo_tok, s)])
        for s in range(NS):
            transpose4(moT[:, :, sl(s)], [mo_tok[:L, s, h * 128:(h + 1) * 128] for h in range(4)], L, [(mo_tok, s)], [(moT, h) for h in range(4)])
        for cb in range(4):
            wb, wv = wload(wxo_b[l], wxo_b[l][cb], 4, 512)
            for j in range(4):
                pa_ = proj_feat(wb, wv, j, TT, nk=4, rhs=moT, role="F")
                c = cb * 4 + j
                P.op("dve", lambda e, c=c, pa_=pa_: e.tensor_tensor(out=xT[:, c, :TT], in0=xT[:, c, :TT], in1=pa_[:, :TT], op=ALU.add), reads=[(xT, c), pa_], writes=[(xT, c)])

        rmsnorm_T(xT, l, 2, TT)
        for (b0, b1) in ((0, 6), (6, 11)):
            nfc = (b1 - b0) * 4
            for b in range(b0, b1):
                for (blk, off) in ((b, 0), (11 + b, 44)):
                    wb_, wv_ = wload(wup_b[l], wup_b[l][blk], 16, 512)
                    for j in range(4):
                        fc = b * 4 + j
                        pa_ = proj_feat(wb_, wv_, j, TT, role="F")
                        ue = uext[0] if off == 0 else uext[1]
                        ch = off + fc
                        dstc = cv4[:, j, :] if off == 0 else cg[:, :]
                        dkey = (cv4, j) if off == 0 else cg
                        P.op("pool", lambda e, ue=ue, ch=ch: e.tensor_copy(out=ue[:, 0:2], in_=u_hist[l][:, ch, :]), reads=[(u_hist[l], ch)], writes=[ue])
                        P.op("act", lambda e, ue=ue, pa_=pa_: e.activation(out=ue[:, 2:2 + TT], in_=pa_[:, :TT], func=AF.Copy), reads=[pa_], writes=[ue])
                        P.op("pool", lambda e, ue=ue, ch=ch: e.tensor_copy(out=u_hist[l][:, ch, :], in_=ue[:, TT:TT + 2]), reads=[ue], writes=[(u_hist[l], ch)])
                        P.op("act", lambda e, pa_=pa_, ch=ch, dstc=dstc: e.activation(out=dstc[:, :TT], in_=pa_[:, :TT], func=AF.Identity, bias=convfb[:, l, ch:ch + 1], scale=convfw[:, l, ch, 2:3]), reads=[pa_, convfw, convfb], writes=[dkey])
                        for k in (0, 1):
                            P.op("dve", lambda e, ue=ue, ch=ch, dstc=dstc, k=k: e.scalar_tensor_tensor(out=dstc[:, :TT], in0=ue[:, k:k + TT], scalar=convfw[:, l, ch, k:k + 1], in1=dstc[:, :TT], op0=ALU.mult, op1=ALU.add), reads=[ue, convfw, dkey], writes=[dkey])
                        if off == 44:
                            P.op("act", lambda e: e.activation(out=sgt[:, :TT], in_=cg[:, :TT], func=AF.Silu), reads=[cg], writes=[sgt])
                            P.op("dve", lambda e, fc=fc, b0=b0, j=j: e.tensor_tensor(out=gTt[:, fc - b0 * 4, :TT], in0=sgt[:, :TT], in1=cv4[:, j, :TT], op=ALU.mult), reads=[sgt, (cv4, j)], writes=[(gTt, fc - b0 * 4)])
            for c2 in range(8):
                wb, wv2 = wload(wdn_b[l], wdn_b[l][2 * c2:2 * c2 + 2, :, b0 * 4:b1 * 4, :].rearrange("c p f k -> p c f k"), 2, nfc * 128)
                for ci in range(2):
                    c = 2 * c2 + ci
                    pa_ = ps("F")
                    for fi in range(nfc):
                        P.op("pe", lambda e, fi=fi, pa_=pa_, wv2=wv2, nfc=nfc, ci=ci: e.matmul(pa_[:, :TT], lhsT=wv2[:, ci, fi * 128:(fi + 1) * 128], rhs=gTt[:, fi, :TT], start=(fi == 0), stop=(fi == nfc - 1)), reads=[(gTt, fi), wb], writes=[pa_])
                    P.op("dve", lambda e, c=c, pa_=pa_: e.tensor_tensor(out=xT[:, c, :TT], in0=xT[:, c, :TT], in1=pa_[:, :TT], op=ALU.add), reads=[(xT, c), pa_], writes=[(xT, c)])

    def run_group(stream, xsrc, ydst, pos0, L, NS, okd, ovd):
        for s in range(NS):
            dma("sp", xin[:L, :], xsrc[s], [stream["xbuf"]], [xin], "xin")
            for g in range(4):
                pt = ps("T")
                for j in range(4):
                    kc = g * 4 + j
                    P.op("pe", lambda e, kc=kc, j=j, pt=pt: e.transpose(pt[:, j * L:(j + 1) * L], xin[:L, kc * 128:(kc + 1) * 128], identf[:L, :L]), reads=[xin, identf], writes=[pt])
                P.op("act", lambda e, g=g, pt=pt, s=s: e.activation(out=xT[:, g * 4:g * 4 + 4, s * L:(s + 1) * L], in_=pt[:, 0:4 * L].rearrange("p (a b) -> p a b", a=4), func=AF.Copy), reads=[pt], writes=[(xT, g * 4 + i) for i in range(4)])
        for l in range(2):
            layer(l, stream, pos0, L, NS, okd[l], ovd[l])
        for s in range(NS):
            for g in range(4):
                pt = ps("T")
                for j in range(4):
                    kc = g * 4 + j
                    P.op("pe", lambda e, kc=kc, j=j, pt=pt, s=s: e.transpose(pt[:L, j * 128:(j + 1) * 128], xT[:, kc, s * L:(s + 1) * L], identf[:, :]), reads=[(xT, kc), identf], writes=[pt])
                P.op("act", lambda e, g=g, pt=pt: e.activation(out=yout[:L, g * 512:(g + 1) * 512], in_=pt[:L, 0:512], func=AF.Copy), reads=[pt], writes=[yout])
            dma("sp", ydst[s], yout[:L, :], [yout], [stream["ybuf"]], "yout")

    for l in range(2):
        P.op("pool", lambda e, l=l: e.memset(a_hist[l][:], 0.0), writes=[a_hist[l]])
        P.op("pool", lambda e, l=l: e.memset(Sst[l][:], 0.0), writes=[Sst[l]])
        P.op("pool", lambda e, l=l: e.memset(Sbf[l][:], 0.0), writes=[Sbf[l]])
        P.op("pool", lambda e, l=l: e.memset(u_hist[l][:], 0.0), writes=[u_hist[l]])
    mem_kv_prompt()
    pst = {"xbuf": xp, "ybuf": y_p, "okbuf": o_k_p, "ovbuf": o_v_p}
    NSP = NS_PROMPT
    for t in range(NT_PROMPT // NSP):
        p0 = t * 128 * NSP
        blk = lambda s, p0=p0: slice(p0 + s * 128, p0 + (s + 1) * 128)
        run_group(pst, [xp[blk(s), :] for s in range(NSP)], [y_p[blk(s), :] for s in range(NSP)], p0, 128, NSP,
                  [[o_k_p[l, blk(s), :] for s in range(NSP)] for l in range(2)], [[o_v_p[l, blk(s), :] for s in range(NSP)] for l in range(2)])
    for l in range(2):
        dma("sp", o_ca_p[l], a_hist[l][:], [a_hist[l]], [o_ca_p], "o_ca")
        dma("sp", o_rs_p[l].rearrange("h d e -> d h e"), Sst[l][:], [Sst[l]], [o_rs_p], "o_rs")
        dma("sp", o_cf_p[l], u_hist[l][:], [u_hist[l]], [o_cf_p], "o_cf")

    for si in range(2):
        for l in range(2):
            dma("sp", a_hist[l][:], s_conva[l, si], [s_conva], [a_hist[l]], "ld_ah")
            dma("sp", Sst[l][:], s_ret[l, si].rearrange("h d e -> d h e"), [s_ret], [Sst[l]], "ld_S")
            P.op("pool", lambda e, l=l: e.tensor_copy(out=Sbf[l][:], in_=Sst[l][:]), reads=[Sst[l]], writes=[Sbf[l]])
            dma("sp", u_hist[l][:], s_convf[l, si], [s_convf], [u_hist[l]], "ld_uh")
        mem_kv_sample(si)
        past_kv_sample(si)
        sst = {"xbuf": xs, "ybuf": y_s, "okbuf": o_k_s, "ovbuf": o_v_s}
        run_group(sst, [xs[si]], [y_s[si]], PAST, 64, 1, [[o_k_s[l, si]] for l in range(2)], [[o_v_s[l, si]] for l in range(2)])
        for l in range(2):
            dma("sp", o_ca_s[l, si], a_hist[l][:], [a_hist[l]], [o_ca_s], "o_ca")
            dma("sp", o_rs_s[l, si].rearrange("h d e -> d h e"), Sst[l][:], [Sst[l]], [o_rs_s], "o_rs")
            dma("sp", o_cf_s[l, si], u_hist[l][:], [u_hist[l]], [o_cf_s], "o_cf")

    nsem = P.emit()
    return nc


NT_PROMPT = 32
NS_PROMPT = 2
VAUG_SEP = 0
_NC = None


def _bucket(rel):
    n = abs(rel)
    if n < 8:
        v = n
    else:
        v = 8 + int(np.float32(np.log(np.float32(max(n, 1)) / np.float32(8)) / np.float32(math.log(16.0)) * np.float32(8)))
        v = min(v, 15)
    return (16 if rel > 0 else 0) + v


def _consts():
    c = {}
    c["ident"] = np.eye(128, dtype=np.float32)
    c["anti"] = np.ascontiguousarray(np.eye(128, dtype=np.float32)[::-1])
    half = 64
    inv = (1.0 / (10000.0 ** (np.arange(half, dtype=np.float32) / half))).astype(np.float32)
    pos = np.arange(SEQ, dtype=np.float32)
    ang = pos[:, None] * inv[None, :]
    cos = np.cos(ang).astype(np.float32); sin = np.sin(ang).astype(np.float32)
    rq = np.stack([np.tile(cos, (1, 4)), np.tile(sin, (1, 4))], axis=1).astype(np.float32)
    c["ropeq"] = np.ascontiguousarray(rq)
    c["ropek"] = np.ascontiguousarray(rq * np.float32(128.0 ** -0.5))
    lg = np.log(np.array(GAM, dtype=np.float64))
    i = np.arange(128)
    dect = np.zeros((128, 4, 128), np.float32)
    for h in range(4):
        d = i[None, :] - i[:, None]
        dect[:, h, :] = np.where(d >= 0, np.exp(lg[h] * np.maximum(d, 0)), 0.0)
    c["dect"] = dect
    qd = np.zeros((128, 4, 128), np.float32)
    for h in range(4):
        qd[:, h, :] = np.exp(lg[h] * (i + 1.0))[None, :]
    c["qdect"] = qd
    kd = np.zeros((128, 2, 4), np.float32)
    for h in range(4):
        kd[:, 0, h] = np.exp(lg[h] * (127.0 - i))
        kd[:64, 1, h] = np.exp(lg[h] * (63.0 - i[:64]))
    c["kdec"] = kd
    oh = np.zeros((32, 2, 256), np.float32)
    for ty in range(2):
        for u in range(255):
            t = 254 - u
            rel = (t - 127) if ty == 0 else (t - 255)
            oh[_bucket(rel), ty, u] = 1.0
    c["oh"] = oh
    mk = np.zeros((1, 2, 128), np.float32)
    mk[0, 0, 64:] = 1.0
    mk[0, 1, :64] = -240000.0
    c["mkq"] = mk
    return c


def kernel(**inp):
    global _NC
    f = lambda a: np.ascontiguousarray(np.asarray(a, dtype=np.float32))
    I = {k: f(v) for k, v in inp.items()}
    if _NC is None:
        _NC = build()
    nc = _NC
    cst = _consts()
    rep = lambda v: np.ascontiguousarray(np.broadcast_to(v[None], (128,) + v.shape))
    shared = dict(cst)
    for k in ["w_in", "w_out", "w_xq", "w_xk", "w_xv", "w_xo", "w_up", "w_down"]:
        shared[k] = I[k]
    gs = np.stack([I["norm1_g"], I["norm2_g"], I["norm3_g"], I["mem_norm_g"]], axis=1)
    shared["gT"] = np.ascontiguousarray(gs.reshape(2, 4, 16, 128).transpose(3, 0, 1, 2))
    shared["convaw"] = np.ascontiguousarray(I["conv_a_w"].reshape(2, 31, 4, 128).transpose(3, 0, 2, 1))
    shared["convab"] = np.ascontiguousarray(I["conv_a_b"].reshape(2, 4, 128).transpose(2, 0, 1))
    shared["lnag"] = rep(I["ln_a_g"]); shared["lnab"] = rep(I["ln_a_b"]); shared["retg"] = rep(I["ret_gn_g"])
    shared["qng"] = rep(I["diff_qn_g"]); shared["kng"] = rep(I["diff_kn_g"])
    shared["lqk"] = rep(np.stack([I["diff_lq1"], I["diff_lk1"], I["diff_lq2"], I["diff_lk2"]], axis=1))
    shared["subg"] = rep(I["diff_subln_g"])
    shared["relb"] = I["rel_bias"]; shared["relbT"] = np.ascontiguousarray(I["rel_bias"].T)
    shared["b15"] = rep(I["rel_bias"][15])
    shared["xqng"] = rep(I["xqn_g"]); shared["xkng"] = rep(I["xkn_g"])
    shared["convfw"] = np.ascontiguousarray(I["conv_f_w"].reshape(2, 3, 88, 128).transpose(3, 0, 2, 1))
    shared["convfb"] = np.ascontiguousarray(I["conv_f_b"].reshape(2, 88, 128).transpose(2, 0, 1))
    in_maps = []
    for c in range(8):
        m = dict(shared)
        b = c % 2
        ss = [2 * c, 2 * c + 1]
        m["xp"] = I["x_prompt"][b]; m["xs"] = np.ascontiguousarray(I["x_sample"][ss]); m["memp"] = I["mem_prompt"][b]
        sca = I["state_conv_a"][:, ss]
        m["s_conva"] = np.ascontiguousarray(sca.reshape(2, 2, 30, 4, 128).transpose(0, 1, 4, 3, 2))
        m["s_ret"] = np.ascontiguousarray(I["state_ret"][:, ss])
        m["c_dk"] = np.ascontiguousarray(I["cache_diff_k"][:, ss].reshape(2, 2, PAST, 512))
        m["c_dv"] = np.ascontiguousarray(I["cache_diff_v"][:, ss].reshape(2, 2, PAST, 512))
        m["c_mk"] = np.ascontiguousarray(I["cache_mem_k"][:, ss].reshape(2, 2, 256, 512))
        m["c_mv"] = np.ascontiguousarray(I["cache_mem_v"][:, ss].reshape(2, 2, 256, 512))
        scf = I["state_conv_f"][:, ss]
        m["s_convf"] = np.ascontiguousarray(scf.reshape(2, 2, 2, 88, 128).transpose(0, 1, 4, 3, 2))
        in_maps.append(m)
    res = run_bass_kernel_spmd(nc, in_maps, core_ids=list(range(8)))
    R = res.results
    P2 = [R[0], R[1]]
    y_p = np.stack([r["y_p"] for r in P2])
    y_s = np.concatenate([R[c]["y_s"] for c in range(8)], axis=0)
    ca_p = np.stack([r["o_ca_p"].transpose(0, 3, 2, 1).reshape(2, 30, 512) for r in P2], axis=1)
    rs_p = np.stack([r["o_rs_p"] for r in P2], axis=1)
    k_p = np.stack([r["o_k_p"].reshape(2, SEQ, 4, 128) for r in P2], axis=1)
    v_p = np.stack([r["o_v_p"].reshape(2, SEQ, 4, 128) for r in P2], axis=1)
    mk_p = np.stack([r["o_mk_p"].reshape(2, 256, 4, 128) for r in P2], axis=1)
    mv_p = np.stack([r["o_mv_p"].reshape(2, 256, 4, 128) for r in P2], axis=1)
    cf_p = np.stack([r["o_cf_p"].transpose(0, 3, 2, 1).reshape(2, 2, 2 * FF) for r in P2], axis=1)
    ca_s = np.concatenate([R[c]["o_ca_s"].transpose(0, 1, 4, 3, 2).reshape(2, 2, 30, 512) for c in range(8)], axis=1)
    rs_s = np.concatenate([R[c]["o_rs_s"] for c in range(8)], axis=1)
    k_s = np.concatenate([R[c]["o_k_s"].reshape(2, 2, 64, 4, 128) for c in range(8)], axis=1)
    v_s = np.concatenate([R[c]["o_v_s"].reshape(2, 2, 64, 4, 128) for c in range(8)], axis=1)
    cf_s = np.concatenate([R[c]["o_cf_s"].transpose(0, 1, 4, 3, 2).reshape(2, 2, 2, 2 * FF) for c in range(8)], axis=1)
    outs = (y_p, y_s, ca_p, rs_p, k_p, v_p, mk_p, mv_p, cf_p, ca_s, rs_s, k_s, v_s, cf_s)
    return tuple(np.ascontiguousarray(o.astype(np.float32)) for o in outs)
```

```python
import numpy as np
import concourse.bass as bass
import concourse.mybir as mybir

F32 = mybir.dt.float32
BF16 = mybir.dt.bfloat16
ALU = mybir.AluOpType
AF = mybir.ActivationFunctionType
AX = mybir.AxisListType

ENGS = ["pe", "act", "dve", "pool", "sp"]
EPOCH = 16000


class Buf:
    def __init__(self, name, t, parent=None):
        self.name = name
        self.t = t
        self._p = parent
        self._whole = {"w": None, "r": {}}
        self._parts = {}

    @property
    def whole(self):
        return self._p.whole if self._p is not None else self._whole

    @whole.setter
    def whole(self, v):
        if self._p is not None:
            self._p.whole = v
        else:
            self._whole = v

    @property
    def parts(self):
        return self._p.parts if self._p is not None else self._parts

    @parts.setter
    def parts(self, v):
        if self._p is not None:
            self._p.parts = v
        else:
            self._parts = v

    def __getitem__(self, idx):
        return self.t[idx]


class Op:
    __slots__ = ("eng", "fn", "waits", "signal", "seq", "dma", "dmaval", "sigval")

    def __init__(self, eng, fn):
        self.eng = eng
        self.fn = fn
        self.waits = []
        self.signal = False
        self.seq = 0
        self.dma = None
        self.dmaval = 0
        self.sigval = 0


class Prog:
    def __init__(self, nc):
        self.nc = nc
        self.ops = {e: [] for e in ENGS}
        self.dma_cnt = {}
        self.waited = {e: {} for e in ENGS}

    @staticmethod
    def _norm(lst):
        out = []
        for x in lst:
            if isinstance(x, tuple):
                out.append(x)
            else:
                out.append((x, None))
        return out

    def op(self, eng, fn, reads=(), writes=(), dma=None):
        o = Op(eng, fn)
        o.seq = len(self.ops[eng])
        reads = self._norm(reads)
        writes = self._norm(writes)
        deps = []

        def add(tok):
            if tok is not None:
                deps.append(tok)

        for b, k in reads:
            add(b.whole["w"])
            if k is None:
                for st in b.parts.values():
                    add(st["w"])
            elif k in b.parts:
                add(b.parts[k]["w"])
        for b, k in writes:
            add(b.whole["w"])
            for t in b.whole["r"].values():
                add(t)
            if k is None:
                for st in b.parts.values():
                    add(st["w"])
                    for t in st["r"].values():
                        add(t)
            elif k in b.parts:
                add(b.parts[k]["w"])
                for t in b.parts[k]["r"].values():
                    add(t)
        if dma is not None:
            self.dma_cnt[dma] = self.dma_cnt.get(dma, 0) + 16
            o.dma = dma
            o.dmaval = self.dma_cnt[dma]
            tok = ("d", dma, o.dmaval, o)
        else:
            tok = ("e", eng, o.seq, o)
        wd = self.waited[eng]
        for t in deps:
            if t[0] == "e":
                if t[1] == "pe" and eng == "pe":
                    continue
                key = ("e", t[1])
                val = t[2]
            else:
                key = ("d", t[1])
                val = t[2]
            if wd.get(key, -1) >= val:
                continue
            wd[key] = val
            o.waits.append(t)
            t[3].signal = True
        self.ops[eng].append(o)
        for b, k in reads:
            st = b.whole if k is None else b.parts.setdefault(k, {"w": None, "r": {}})
            rk = (tok[0], tok[1])
            st["r"][rk] = tok
        for b, k in writes:
            if k is None:
                b.parts = {}
                b.whole = {"w": tok, "r": {}}
            else:
                b.parts[k] = {"w": tok, "r": {}}
        return o

    def emit(self, final_wait_eng="sp"):
        nc = self.nc
        sems = {}

        def getsem(key):
            if key not in sems:
                sems[key] = nc.alloc_semaphore("s_%s" % "_".join(str(x) for x in key))
            return sems[key]

        for e in ENGS:
            c = 0
            for o in self.ops[e]:
                if o.dma is None and o.signal:
                    c += 1
                    o.sigval = c
        fin = Op(final_wait_eng, None)
        for dkey, val in self.dma_cnt.items():
            fin.waits.append(("d", dkey, val, None))
        engmap = {"pe": "tensor", "act": "scalar", "dve": "vector", "pool": "gpsimd", "sp": "sync"}
        with nc.Block() as block:
            for e in ENGS:
                ops = self.ops[e] + ([fin] if e == final_wait_eng else [])

                def body(eng, ops=ops, e=e):
                    for o in ops:
                        for t in o.waits:
                            if t[0] == "e":
                                p = t[3]
                                ep = (p.sigval - 1) // EPOCH
                                eng.wait_ge(getsem(("e", t[1], ep)), p.sigval - ep * EPOCH)
                            else:
                                eng.wait_ge(getsem(("d", t[1])), t[2])
                        if o.fn is None:
                            continue
                        ins = o.fn(eng)
                        if o.dma is not None:
                            ins.then_inc(getsem(("d", o.dma)), 16)
                        elif o.signal:
                            ep = (o.sigval - 1) // EPOCH
                            ins.then_inc(getsem(("e", e, ep)), 1)

                getattr(block, engmap[e])(body)
        return len(sems)


import math
import ml_dtypes
from concourse.bass_utils import run_bass_kernel_spmd


D = 2048
FF = 5632
GAM = [1.0 - 2.0 ** (-5 - h) for h in range(4)]
LAM_INIT = [0.8 - 0.6 * math.exp(-0.3 * l) for l in range(2)]
EPS = 1e-6
SEQ = 4096
PAST = 1024


def build():
    nc = bass.Bass("TRN2", target_bir_lowering=False)
    P = Prog(nc)

    def din(name, shape):
        return Buf(name, nc.dram_tensor(name, list(shape), F32, kind="ExternalInput").ap())

    def dout(name, shape):
        return Buf(name, nc.dram_tensor(name, list(shape), F32, kind="ExternalOutput").ap())

    def dscr(name, shape, dt):
        return Buf(name, nc.dram_tensor(name, list(shape), dt).ap())

    def sb(name, shape, dt=F32):
        return Buf(name, nc.alloc_sbuf_tensor(name, list(shape), dt).ap())

    xp = din("xp", [SEQ, D]); xs = din("xs", [2, 64, D]); memp = din("memp", [256, D])
    s_conva = din("s_conva", [2, 2, 128, 4, 30]); s_ret = din("s_ret", [2, 2, 4, 128, 256])
    c_dk = din("c_dk", [2, 2, PAST, 512]); c_dv = din("c_dv", [2, 2, PAST, 512])
    c_mk = din("c_mk", [2, 2, 256, 512]); c_mv = din("c_mv", [2, 2, 256, 512])
    s_convf = din("s_convf", [2, 2, 128, 88, 2])
    w_in = din("w_in", [2, D, 5632]); w_out = din("w_out", [2, D, D])
    w_xq = din("w_xq", [2, D, 512]); w_xk = din("w_xk", [2, D, 512]); w_xv = din("w_xv", [2, D, 512])
    w_xo = din("w_xo", [2, 512, D]); w_up = din("w_up", [2, D, 2 * FF]); w_down = din("w_down", [2, FF, D])
    gT_d = din("gT", [128, 2, 4, 16])
    convaw_d = din("convaw", [128, 2, 4, 31]); convab_d = din("convab", [128, 2, 4])
    lnag_d = din("lnag", [128, 2, 512]); lnab_d = din("lnab", [128, 2, 512])
    retg_d = din("retg", [128, 2, 1024])
    qng_d = din("qng", [128, 2, 64]); kng_d = din("kng", [128, 2, 64])
    lqk_d = din("lqk", [128, 2, 4, 64])
    subg_d = din("subg", [128, 2, 128])
    relb_d = din("relb", [32, 4]); relbT_d = din("relbT", [4, 32]); b15_d = din("b15", [128, 4])
    xqng_d = din("xqng", [128, 2, 128]); xkng_d = din("xkng", [128, 2, 128])
    convfw_d = din("convfw", [128, 2, 88, 3]); convfb_d = din("convfb", [128, 2, 88])
    ident_d = din("ident", [128, 128]); anti_d = din("anti", [128, 128])
    ropeq_d = din("ropeq", [SEQ, 2, 256]); ropek_d = din("ropek", [SEQ, 2, 256])
    dect_d = din("dect", [128, 4, 128]); qdect_d = din("qdect", [128, 4, 128]); kdec_d = din("kdec", [128, 2, 4])
    oh_d = din("oh", [32, 2, 256]); mkq_d = din("mkq", [1, 2, 128])

    y_p = dout("y_p", [SEQ, D]); y_s = dout("y_s", [2, 64, D])
    o_ca_p = dout("o_ca_p", [2, 128, 4, 30]); o_rs_p = dout("o_rs_p", [2, 4, 128, 256])
    o_k_p = dout("o_k_p", [2, SEQ, 512]); o_v_p = dout("o_v_p", [2, SEQ, 512])
    o_mk_p = dout("o_mk_p", [2, 256, 512]); o_mv_p = dout("o_mv_p", [2, 256, 512])
    o_cf_p = dout("o_cf_p", [2, 128, 88, 2])
    o_ca_s = dout("o_ca_s", [2, 2, 128, 4, 30]); o_rs_s = dout("o_rs_s", [2, 2, 4, 128, 256])
    o_k_s = dout("o_k_s", [2, 2, 64, 512]); o_v_s = dout("o_v_s", [2, 2, 64, 512])
    o_cf_s = dout("o_cf_s", [2, 2, 128, 88, 2])

    win_b = [dscr("win_b%d" % l, [11, 128, 16, 512], BF16) for l in range(2)]
    wout_b = [dscr("wout_b%d" % l, [4, 128, 16, 512], BF16) for l in range(2)]
    wxq_b = [dscr("wxq_b%d" % l, [1, 128, 16, 512], BF16) for l in range(2)]
    wxk_b = [dscr("wxk_b%d" % l, [1, 128, 16, 512], BF16) for l in range(2)]
    wxv_b = [dscr("wxv_b%d" % l, [1, 128, 16, 512], BF16) for l in range(2)]
    wxo_b = [dscr("wxo_b%d" % l, [4, 128, 4, 512], BF16) for l in range(2)]
    wup_b = [dscr("wup_b%d" % l, [22, 128, 16, 512], BF16) for l in range(2)]
    wdn_b = [dscr("wdn_b%d" % l, [16, 128, 44, 128], BF16) for l in range(2)]
    KVLEN = SEQ
    Kc = [[dscr("Kc%d_%d" % (l, h), [128, KVLEN], BF16) for h in range(4)] for l in range(2)]
    Vc = [dscr("Vc%d" % l, [KVLEN, 512], BF16) for l in range(2)]
    tsc = dscr("tsc", [4, 2, 256], F32)

    NSM = NS_PROMPT
    TM = 128 * NSM
    xT = sb("xT", [128, 16, TM]); hT = sb("hT", [128, 16, TM], BF16)
    rstd = sb("rstd", [128, TM]); epsb = sb("epsb", [128, 1])
    sqr = [sb("sq%d" % i, [128, TM], BF16) for i in range(8)]
    wring = [sb("wr%d" % i, [128, 8192], BF16) for i in range(2)]
    R2 = sb("R2", [128, max(24 * TM, 4096)], BF16)
    xin = Buf("xin", R2[:, 0:4096].bitcast(F32), parent=R2)
    mixT = Buf("mixT", R2[:, 0:16 * TM].rearrange("p (a b) -> p a b", a=16), parent=R2)
    gTt = Buf("gTt", R2[:, 0:24 * TM].rearrange("p (a b) -> p a b", a=24), parent=R2)
    AEW = 30 + TM
    R1 = sb("R1", [128, max(4 * AEW + 4 * TM, 2064)], F32)
    a_ext = Buf("a_ext", R1[:, 0:4 * AEW].rearrange("p (a b) -> p a b", a=4), parent=R1)
    cacc = Buf("cacc", R1[:, 4 * AEW:4 * AEW + 4 * TM].rearrange("p (a b) -> p a b", a=4), parent=R1)
    Vaug = sb("Vaug", [128, 32, 129], BF16) if VAUG_SEP else Buf("Vaug", R1[:, 0:2064].bitcast(BF16).rearrange("p (a b) -> p a b", a=32), parent=R1)
    identf = sb("identf", [128, 128]); identb = sb("identb", [128, 128], BF16); antib = sb("antib", [128, 128], BF16)
    onesb = sb("onesb", [128, 128], BF16)
    gTs = sb("gTs", [128, 2, 4, 16]); convaw = sb("convaw_s", [128, 2, 4, 31]); convab = sb("convab_s", [128, 2, 4])
    lnag = sb("lnag_s", [128, 2, 512], BF16); lnab = sb("lnab_s", [128, 2, 512], BF16); retg = sb("retg_s", [128, 2, 1024], BF16)
    qng = sb("qng_s", [128, 2, 64]); kng = sb("kng_s", [128, 2, 64]); lqk = sb("lqk_s", [128, 2, 4, 64])
    subg = sb("subg_s", [128, 2, 128]); b15 = sb("b15_s", [128, 4])
    xqng = sb("xqng_s", [128, 2, 128]); xkng = sb("xkng_s", [128, 2, 128])
    convfw = sb("convfw_s", [128, 2, 88, 3]); convfb = sb("convfb_s", [128, 2, 88])
    dect = sb("dect_s", [128, 4, 128]); qdect = sb("qdect_s", [128, 4, 128]); kdec = sb("kdec_s", [128, 2, 4])
    mkq = sb("mkq_s", [1, 2, 128], BF16)
    Hb = sb("Hb", [128, 2, 4, 128], BF16)
    relb = Buf("relb_s", R2[0:32, 0:8].bitcast(F32), parent=R2)
    relbT = Buf("relbT_s", R2[0:4, 8:72].bitcast(F32), parent=R2)
    ohs = Buf("oh_s", R2[0:32, 128:1152].bitcast(F32).rearrange("p (a b) -> p a b", a=2), parent=R2)
    tsb = Buf("tsb", R2[0:4, 1152:2176].bitcast(F32).rearrange("p (a b) -> p a b", a=2), parent=R2)
    Hf = Buf("Hf", R2[:, 2176:4224].bitcast(F32).rearrange("p (a b c) -> p a b c", a=2, b=4), parent=R2)
    lamt = sb("lamt", [128, 2, 8])
    ropeq = sb("ropeq_s", [128, NSM, 2, 256]); ropek = sb("ropek_s", [128, NSM, 2, 256])
    a_hist = [sb("a_hist%d" % l, [128, 4, 30]) for l in range(2)]
    sig = sb("sig", [128, TM])
    Sst = [sb("S%d" % l, [128, 4, 256]) for l in range(2)]; Sbf = [sb("Sbf%d" % l, [128, 4, 256], BF16) for l in range(2)]
    u_hist = [sb("u_hist%d" % l, [128, 88, 2]) for l in range(2)]
    mix_tok = sb("mix_tok", [128, NSM, 2048], BF16)
    KT = sb("KT", [128, KVLEN], BF16)
    PT = [sb("PT%d" % i, [128, TM], BF16) for i in range(2)]
    t1 = sb("t1", [128, 512]); t2 = sb("t2", [128, 512]); t3 = sb("t3", [128, 512]); t4 = sb("t4", [128, 512])
    qr = sb("qr", [128, 512], BF16); kr = sb("kr", [128, 512], BF16)
    ktil = sb("ktil", [128, NSM, 512], BF16)
    vB = sb("vB", [128, NSM, 1024], BF16); sgB = sb("sgB", [128, NSM, 1024], BF16)
    qTr = sb("qTr", [128, NSM, 4, 128], BF16); kTr = sb("kTr", [128, NSM, 4, 128], BF16); qtT = sb("qtT", [128, NSM, 4, 128], BF16)
    STb = sb("STb", [128, 4, 128], BF16); rsm = sb("rsm", [128, 4])
    small = sb("small", [128, 32])
    qn = sb("qn", [128, 512], BF16); knf = sb("knf", [128, 512]); knb = sb("knb", [128, 512], BF16)
    vCf = sb("vCf", [128, 512]); vCb = sb("vCb", [128, 512], BF16)
    cqT = sb("cqT", [128, 4, TM], BF16); KTn = sb("KTn", [128, 4, 128], BF16)
    ot = sb("ot", [128, 128]); ot2 = sb("ot2", [128, 128])
    mKT = [sb("mKT%d" % l, [128, 4, 256], BF16) for l in range(2)]
    mVa = [sb("mVa%d" % l, [128, 2, 4, 129], BF16) for l in range(2)]
    memx = xin; yout = xin
    mo_tok = sb("mo_tok", [128, NSM, 512], BF16); moT = sb("moT", [128, 4, TM], BF16)
    uext = [sb("uext%d" % i, [128, TM + 2]) for i in range(2)]
    cv4 = sb("cv4", [128, 4, TM]); cg = sb("cg", [128, TM]); sgt = sb("sgt", [128, TM])

    banks = [Buf("ps%d" % i, nc.alloc_psum_tensor("ps%d" % i, [128, 512], F32).ap()) for i in range(8)]
    rr = {"A": [0, [0, 1]], "T": [0, [2, 3]], "L": [0, [4, 5]], "O": [0, [6, 7]], "F": [0, [0, 1, 2, 3, 4, 5, 6, 7]]}

    def ps(role):
        r = rr[role]
        b = banks[r[1][r[0] % len(r[1])]]
        r[0] += 1
        return b

    sqi = [0]

    def sqn():
        sqi[0] += 1
        return sqr[sqi[0] % 8]

    def dma(eng, out_ap, in_ap, reads, writes, key):
        P.op(eng, lambda e: e.dma_start(out=out_ap, in_=in_ap), reads=reads, writes=writes, dma=key)

    def load_small(dst, src):
        dma("sp", dst[:], src[:], [src], [dst], "ld_" + dst.name)

    widx = [0]

    def wload(srcbuf, src_ap, a, b):
        slot = widx[0] % 2
        widx[0] += 1
        wb = wring[slot]
        dst = wb[:, 0:a * b].rearrange("p (a b) -> p a b", a=a)
        dma("sp", dst, src_ap, [srcbuf], [wb], "w%d" % slot)
        return wb, dst

    hidx = [0]

    def wload_half(srcbuf, src_ap, a, b):
        k = hidx[0] % 4
        hidx[0] += 1
        wb = wring[k // 2]
        off = (k % 2) * 4096
        key = "ab"[k % 2]
        dst = wb[:, off:off + a * b].rearrange("p (a b) -> p a b", a=a)
        dma("sp", dst, src_ap, [srcbuf], [(wb, key)], "w%d%s" % (k // 2, key))
        return (wb, key), dst

    def transpose_bf(dst_ap, src_ap, rows, cols, reads, writes):
        pt = ps("T")
        ptb = pt[:, :].bitcast(BF16)
        P.op("pe", lambda e: e.transpose(ptb[:cols, 0:rows], src_ap, identb[:rows, :rows]), reads=reads + [identb], writes=[pt])
        P.op("act", lambda e: e.activation(out=dst_ap, in_=ptb[:cols, 0:rows], func=AF.Copy), reads=[pt], writes=writes)

    def transpose4(dst3, srcs, rows, reads, writes):
        pt = ps("T")
        ptb = pt[:, :].bitcast(BF16)
        for i, sa in enumerate(srcs):
            P.op("pe", lambda e, i=i, sa=sa: e.transpose(ptb[:, i * rows:(i + 1) * rows], sa, identb[:rows, :rows]), reads=reads + [identb], writes=[pt])
        P.op("act", lambda e: e.activation(out=dst3, in_=ptb[:, 0:4 * rows].rearrange("p (a b) -> p a b", a=4), func=AF.Copy), reads=[pt], writes=writes)

    def rmsnorm_T(src, l, gi, TT):
        pa = ps("A")
        for kc in range(16):
            sq = sqn()
            P.op("act", lambda e, kc=kc, sq=sq: e.activation(out=sq[:, :TT], in_=src[:, kc, :TT], func=AF.Square), reads=[(src, kc)], writes=[sq])
            P.op("pe", lambda e, kc=kc, sq=sq: e.matmul(pa[:, :TT], lhsT=onesb[:, :], rhs=sq[:, :TT], start=(kc == 0), stop=(kc == 15)), reads=[sq, onesb], writes=[pa])
        P.op("act", lambda e: e.activation(out=rstd[:, :TT], in_=pa[:, :TT], func=AF.Sqrt, bias=epsb[:, 0:1], scale=1.0 / D), reads=[pa, epsb], writes=[rstd])
        P.op("dve", lambda e: e.reciprocal(out=rstd[:, :TT], in_=rstd[:, :TT]), reads=[rstd], writes=[rstd])
        for kc in range(16):
            P.op("dve", lambda e, kc=kc: e.scalar_tensor_tensor(out=hT[:, kc, :TT], in0=src[:, kc, :TT], scalar=gTs[:, l, gi, kc:kc + 1], in1=rstd[:, :TT], op0=ALU.mult, op1=ALU.mult),
                 reads=[(src, kc), rstd, gTs], writes=[(hT, kc)])

    def proj_tok(wb, wv, L, ncol, nk=16, lhs=None, s=0):
        lhs = hT if lhs is None else lhs
        pa = ps("A")
        for kc in range(nk):
            P.op("pe", lambda e, kc=kc: e.matmul(pa[:L, :ncol], lhsT=lhs[:, kc, s * L:(s + 1) * L], rhs=wv[:, kc, :ncol], start=(kc == 0), stop=(kc == nk - 1)), reads=[(lhs, kc), wb], writes=[pa])
        return pa

    def proj_feat(wb, wv, j, TT, nk=16, rhs=None, role="A"):
        rhs = hT if rhs is None else rhs
        pa = ps(role)
        for kc in range(nk):
            P.op("pe", lambda e, kc=kc: e.matmul(pa[:, :TT], lhsT=wv[:, kc, j * 128:(j + 1) * 128], rhs=rhs[:, kc, :TT], start=(kc == 0), stop=(kc == nk - 1)), reads=[(rhs, kc), wb], writes=[pa])
        return pa

    def rstd_free(pa, L, ngrp, gsz, col0):
        P.op("act", lambda e: e.activation(out=t4[:L, :ngrp * gsz], in_=pa[:L, :ngrp * gsz], func=AF.Square), reads=[pa], writes=[t4])
        P.op("dve", lambda e: e.reduce_sum(out=small[:L, col0:col0 + ngrp], in_=t4[:L, :ngrp * gsz].rearrange("p (g j) -> p g j", g=ngrp), axis=AX.X), reads=[t4], writes=[small])
        P.op("act", lambda e: e.activation(out=small[:L, col0:col0 + ngrp], in_=small[:L, col0:col0 + ngrp], func=AF.Sqrt, bias=epsb[:L, 0:1], scale=1.0 / gsz), reads=[small, epsb], writes=[small])
        P.op("dve", lambda e: e.reciprocal(out=small[:L, col0:col0 + ngrp], in_=small[:L, col0:col0 + ngrp]), reads=[small], writes=[small])

    P.op("dve", lambda e: e.memset(epsb[:, :], EPS), writes=[epsb])
    P.op("dve", lambda e: e.memset(onesb[:, :], 1.0), writes=[onesb])
    P.op("dve", lambda e: e.memset(Vaug[:, :, 128:129], 1.0), writes=[Vaug])
    for l in range(2):
        P.op("dve", lambda e, l=l: e.memset(mVa[l][:, :, :, 128:129], 1.0), writes=[mVa[l]])
    for dst, src in [(identf, ident_d), (gTs, gT_d), (convaw, convaw_d), (convab, convab_d), (qng, qng_d), (kng, kng_d), (lqk, lqk_d), (subg, subg_d), (b15, b15_d), (relb, relb_d),
                     (relbT, relbT_d), (ohs, oh_d), (xqng, xqng_d), (xkng, xkng_d), (convfw, convfw_d), (convfb, convfb_d),
                     (dect, dect_d), (qdect, qdect_d), (kdec, kdec_d)]:
        load_small(dst, src)
    for dst, src in [(lnag, lnag_d), (lnab, lnab_d), (retg, retg_d)]:
        dma("pool", dst[:], src[:], [src], [dst], "ld_" + dst.name)
    dma("pool", identb[:], ident_d[:], [ident_d], [identb], "ld_identb")
    dma("pool", antib[:], anti_d[:], [anti_d], [antib], "ld_antib")
    dma("pool", mkq[:], mkq_d[:], [mkq_d], [mkq], "ld_mkq")

    for l in range(2):
        for wi, (src, dst) in enumerate([(w_xk, wxk_b), (w_xv, wxv_b), (w_in, win_b), (w_out, wout_b), (w_xq, wxq_b), (w_xo, wxo_b), (w_up, wup_b), (w_down, wdn_b)]):
            nb, _, nk, bw = dst[l].t.shape
            key = "cv%d_%d" % (wi, l)
            for c in range(nb):
                dma("pool", dst[l][c], src[l, :, c * bw:(c + 1) * bw].rearrange("(kc p) c -> p kc c", p=128), [src], [dst[l]], key)
            _last = P.ops["pool"][-1]
            dst[l].whole["w"] = ("d", key, _last.dmaval, _last)

    for l in range(2):
        P.op("dve", lambda e, l=l: e.tensor_tensor(out=t1[:, 0:64], in0=lqk[:, l, 0, :], in1=lqk[:, l, 1, :], op=ALU.mult), reads=[lqk], writes=[t1])
        P.op("dve", lambda e, l=l: e.reduce_sum(out=lamt[:, l, 2:3], in_=t1[:, 0:64], axis=AX.X), reads=[t1], writes=[lamt])
        P.op("dve", lambda e, l=l: e.tensor_tensor(out=t1[:, 64:128], in0=lqk[:, l, 2, :], in1=lqk[:, l, 3, :], op=ALU.mult), reads=[lqk], writes=[t1])
        P.op("dve", lambda e, l=l: e.reduce_sum(out=lamt[:, l, 3:4], in_=t1[:, 64:128], axis=AX.X), reads=[t1], writes=[lamt])
        P.op("act", lambda e, l=l: e.activation(out=lamt[:, l, 4:6], in_=lamt[:, l, 2:4], func=AF.Exp), reads=[lamt], writes=[lamt])
        P.op("dve", lambda e, l=l: e.tensor_tensor(out=lamt[:, l, 6:7], in0=lamt[:, l, 4:5], in1=lamt[:, l, 5:6], op=ALU.subtract), reads=[lamt], writes=[lamt])
        P.op("dve", lambda e, l=l: e.tensor_scalar(out=lamt[:, l, 1:2], in0=lamt[:, l, 6:7], scalar1=LAM_INIT[l], scalar2=-1.0, op0=ALU.add, op1=ALU.mult), reads=[lamt], writes=[lamt])

    pa = ps("A")
    P.op("pe", lambda e: e.matmul(pa[:4, 0:512], lhsT=relb[:, :], rhs=ohs[:, :, :].rearrange("p a b -> p (a b)"), start=True, stop=True), reads=[relb, ohs], writes=[pa])
    P.op("dve", lambda e: e.tensor_scalar(out=tsb[:, :, :].rearrange("p a b -> p (a b)"), in0=pa[:4, 0:512], scalar1=relbT[:, 15:16], scalar2=8.0, op0=ALU.subtract, op1=ALU.mult), reads=[pa, relbT], writes=[tsb])
    dma("sp", tsc[:], tsb[:], [tsb], [tsc], "tsc")
    for ty in range(2):
        for h in range(4):
            src = bass.AP(tensor=tsc.t.tensor, offset=(h * 2 + ty) * 256, ap=[[1, 128], [1, 128]])
            dma("sp", Hf[:, ty, h, :], src, [tsc], [Hf], "Hf")
    P.op("dve", lambda e: e.tensor_copy(out=Hb[:], in_=Hf[:]), reads=[Hf], writes=[Hb])

    def mem_kv_prompt():
        for l in range(2):
            wk, wkv = wload(wxk_b[l], wxk_b[l][0], 16, 512)
            wv_, wvv = wload(wxv_b[l], wxv_b[l][0], 16, 512)
            for mb in range(2):
                dma("sp", memx[:, :], memp[mb * 128:(mb + 1) * 128, :], [memp], [memx], "memx")
                P.op("act", lambda e: e.activation(out=mix_tok[:, 0, :], in_=memx[:, :], func=AF.Square, accum_out=small[:, 20:21]), reads=[memx], writes=[mix_tok, small])
                P.op("act", lambda e: e.activation(out=small[:, 20:21], in_=small[:, 20:21], func=AF.Sqrt, bias=epsb[:, 0:1], scale=1.0 / D), reads=[small, epsb], writes=[small])
                P.op("dve", lambda e: e.reciprocal(out=small[:, 20:21], in_=small[:, 20:21]), reads=[small], writes=[small])
                P.op("dve", lambda e: e.tensor_scalar(out=memx[:, :], in0=memx[:, :], scalar1=small[:, 20:21], scalar2=None, op0=ALU.mult), reads=[memx, small], writes=[memx])
                for kc in range(16):
                    pt = ps("T")
                    P.op("pe", lambda e, kc=kc, pt=pt: e.transpose(pt[:, 0:128], memx[:, kc * 128:(kc + 1) * 128], identf[:, :]), reads=[memx, identf], writes=[pt])
                    P.op("dve", lambda e, kc=kc, pt=pt, l=l: e.tensor_scalar(out=hT[:, kc, 0:128], in0=pt[:, 0:128], scalar1=gTs[:, l, 3, kc:kc + 1], scalar2=None, op0=ALU.mult), reads=[pt, gTs], writes=[(hT, kc)])
                pk = proj_tok(wk, wkv, 128, 512)
                rstd_free(pk, 128, 4, 128, 0)
                for h in range(4):
                    P.op("dve", lambda e, h=h, l=l, pk=pk: e.scalar_tensor_tensor(out=knf[:, h * 128:(h + 1) * 128], in0=pk[:, h * 128:(h + 1) * 128], scalar=small[:, h:h + 1], in1=xkng[:, l, :], op0=ALU.mult, op1=ALU.mult), reads=[pk, small, xkng], writes=[knf])
                dma("sp", o_mk_p[l, mb * 128:(mb + 1) * 128, :], knf[:, :], [knf], [o_mk_p], "o_knf")
                P.op("dve", lambda e: e.tensor_copy(out=knb[:, :], in_=knf[:, :]), reads=[knf], writes=[knb])
                for h in range(4):
                    transpose_bf(mKT[l][:, h, mb * 128:(mb + 1) * 128], knb[:, h * 128:(h + 1) * 128], 128, 128, [knb], [mKT[l]])
                pv = proj_tok(wv_, wvv, 128, 512)
                P.op("act", lambda e, pv=pv: e.activation(out=vCf[:, :], in_=pv[:, 0:512], func=AF.Copy), reads=[pv], writes=[vCf])
                dma("sp", o_mv_p[l, mb * 128:(mb + 1) * 128, :], vCf[:, :], [vCf], [o_mv_p], "o_vCf")
                P.op("dve", lambda e, l=l, mb=mb: e.tensor_copy(out=mVa[l][:, mb, :, 0:128], in_=vCf[:, :].rearrange("p (h d) -> p h d", h=4)), reads=[vCf], writes=[mVa[l]])

    def mem_kv_sample(si):
        for l in range(2):
            for mb in range(2):
                dma("pool", knb[:, :], c_mk[l, si, mb * 128:(mb + 1) * 128, :], [c_mk], [knb], "ld_knb")
                for h in range(4):
                    transpose_bf(mKT[l][:, h, mb * 128:(mb + 1) * 128], knb[:, h * 128:(h + 1) * 128], 128, 128, [knb], [mKT[l]])
                dma("pool", mVa[l][:, mb, :, 0:128], c_mv[l, si, mb * 128:(mb + 1) * 128, :].rearrange("p (h d) -> p h d", h=4), [c_mv], [mVa[l]], "ld_mVa")

    def past_kv_sample(si):
        for l in range(2):
            dma("pool", Vc[l][0:PAST, :], c_dv[l, si, :, :], [c_dv], [Vc[l]], "convV%d" % l)
            for kb in range(PAST // 128):
                dma("pool", knb[:, :], c_dk[l, si, kb * 128:(kb + 1) * 128, :], [c_dk], [knb], "ld_knb")
                for h in range(4):
                    transpose_bf(KTn[:, h, :], knb[:, h * 128:(h + 1) * 128], 128, 128, [knb], [KTn])
                for h in range(4):
                    dma("sp", Kc[l][h][:, kb * 128:(kb + 1) * 128], KTn[:, h, :], [KTn], [Kc[l][h]], "st_KTn")

    def layer(l, stream, pos0, L, NS, okd, ovd):
        TT = L * NS
        gq0 = pos0 // 128
        kvlen = pos0 + TT
        nkb = (kvlen + 127) // 128
        kd = 0 if L == 128 else 1
        gamL = [g ** L for g in GAM]
        sl = lambda s: slice(s * L, (s + 1) * L)
        rmsnorm_T(xT, l, 0, TT)
        for s in range(NS):
            dma("sp", ropeq[:L, s, :, :], ropeq_d[pos0 + s * L:pos0 + (s + 1) * L, :, :], [ropeq_d], [ropeq], "ropeq")
            dma("sp", ropek[:L, s, :, :], ropek_d[pos0 + s * L:pos0 + (s + 1) * L, :, :], [ropek_d], [ropek], "ropek")
        P.op("pool", lambda e: e.tensor_copy(out=a_ext[:, :, 0:30], in_=a_hist[l][:, :, :]), reads=[a_hist[l]], writes=[(a_ext, "a")])
        def conv_gen_f():
            for j in range(4):
                P.op("dve", lambda e, j=j: e.tensor_scalar(out=cacc[:, j, :TT], in0=a_ext[:, j, 0:TT], scalar1=convaw[:, l, j, 0:1], scalar2=convab[:, l, j:j + 1], op0=ALU.mult, op1=ALU.add), reads=[(a_ext, "a"), convaw, convab], writes=[(cacc, j)])
                yield
                for k in range(1, 31):
                    P.op("dve", lambda e, j=j, k=k: e.scalar_tensor_tensor(out=cacc[:, j, :TT], in0=a_ext[:, j, k:k + TT], scalar=convaw[:, l, j, k:k + 1], in1=cacc[:, j, :TT], op0=ALU.mult, op1=ALU.add), reads=[(a_ext, "a"), convaw, (cacc, j)], writes=[(cacc, j)])
                    yield
            P.op("pool", lambda e: e.tensor_copy(out=a_hist[l][:, :, :], in_=a_ext[:, :, TT:TT + 30]), reads=[(a_ext, "a")], writes=[a_hist[l]])
            yield

        conv_gen = conv_gen_f()

        def conv_step(n):
            for _ in range(n):
                if next(conv_gen, "done") == "done":
                    break

        def ln_phase():
            conv_step(1000)
            for s in range(NS):
                pt = ps("T")
                for j in range(4):
                    P.op("pe", lambda e, j=j, s=s, pt=pt: e.transpose(pt[:L, j * 128:(j + 1) * 128], cacc[:, j, sl(s)], identf[:, :]), reads=[(cacc, j), identf], writes=[pt])
                P.op("dve", lambda e, pt=pt: e.reduce_sum(out=small[:L, 8:9], in_=pt[:L, 0:512], axis=AX.X), reads=[pt], writes=[small])
                P.op("dve", lambda e: e.tensor_scalar(out=small[:L, 9:10], in0=small[:L, 8:9], scalar1=-1.0 / 512, scalar2=None, op0=ALU.mult), reads=[small], writes=[small])
                P.op("act", lambda e, pt=pt: e.activation(out=t1[:L, :], in_=pt[:L, 0:512], func=AF.Identity, bias=small[:L, 9:10], scale=1.0), reads=[pt, small], writes=[t1])
                P.op("act", lambda e: e.activation(out=t2[:L, :], in_=t1[:L, :], func=AF.Square, accum_out=small[:L, 10:11]), reads=[t1], writes=[t2, small])
                P.op("act", lambda e: e.activation(out=small[:L, 10:11], in_=small[:L, 10:11], func=AF.Sqrt, bias=epsb[:L, 0:1], scale=1.0 / 512), reads=[small, epsb], writes=[small])
                P.op("dve", lambda e: e.reciprocal(out=small[:L, 10:11], in_=small[:L, 10:11]), reads=[small], writes=[small])
                P.op("dve", lambda e: e.scalar_tensor_tensor(out=t1[:L, :], in0=t1[:L, :], scalar=small[:L, 10:11], in1=lnag[:L, l, :], op0=ALU.mult, op1=ALU.mult), reads=[t1, small, lnag], writes=[t1])
                P.op("dve", lambda e: e.tensor_tensor(out=t1[:L, :], in0=t1[:L, :], in1=lnab[:L, l, :], op=ALU.add), reads=[t1, lnab], writes=[t1])
                P.op("act", lambda e, s=s: e.activation(out=mix_tok[:L, s, 0:512], in_=t1[:L, :], func=AF.Silu), reads=[t1], writes=[(mix_tok, s)])

        def ret_phase():
            for s in range(NS):
                for h in range(4):
                    pS = ps("L")
                    P.op("pe", lambda e, h=h, s=s, pS=pS: e.matmul(pS[:L, :L], lhsT=kTr[:, s, h, :L], rhs=qTr[:, s, h, :L], start=True, stop=True), reads=[(kTr, s), (qTr, s)], writes=[pS])
                    P.op("dve", lambda e, h=h, pS=pS: e.tensor_tensor(out=STb[:L, h, :L], in0=pS[:L, :L], in1=dect[:L, h, :L], op=ALU.mult), reads=[pS, dect], writes=[(STb, h)])
                    pO = ps("O")
                    P.op("pe", lambda e, h=h, s=s, pO=pO: e.matmul(pO[:L, 0:256], lhsT=STb[:L, h, :L], rhs=vB[:L, s, h * 256:(h + 1) * 256], start=True, stop=False), reads=[(STb, h), (vB, s)], writes=[pO])
                    P.op("pe", lambda e, h=h, s=s, pO=pO: e.matmul(pO[:L, 0:256], lhsT=qtT[:, s, h, :L], rhs=Sbf[l][:, h, :], start=False, stop=True), reads=[(qtT, s), Sbf[l]], writes=[pO])
                    pU = ps("O")
                    P.op("pe", lambda e, h=h, s=s, pU=pU: e.matmul(pU[:, 0:256], lhsT=ktil[:L, s, h * 128:(h + 1) * 128], rhs=vB[:L, s, h * 256:(h + 1) * 256], start=True, stop=True), reads=[(ktil, s), (vB, s)], writes=[pU])
                    P.op("dve", lambda e, h=h, pU=pU: e.scalar_tensor_tensor(out=Sst[l][:, h, :], in0=Sst[l][:, h, :], scalar=float(gamL[h]), in1=pU[:, 0:256], op0=ALU.mult, op1=ALU.add), reads=[(Sst[l], h), pU], writes=[(Sst[l], h)])
                    P.op("pool", lambda e, h=h: e.tensor_copy(out=Sbf[l][:, h, :], in_=Sst[l][:, h, :]), reads=[(Sst[l], h)], writes=[(Sbf[l], h)])
                    rt = (t1 if h < 2 else t3)
                    rc = (h % 2) * 256
                    P.op("act", lambda e, pO=pO, h=h: e.activation(out=t2[:L, 0:256], in_=pO[:L, 0:256], func=AF.Square, accum_out=rsm[:L, h:h + 1]), reads=[pO], writes=[t2, (rsm, h)])
                    P.op("act", lambda e, h=h: e.activation(out=rsm[:L, h:h + 1], in_=rsm[:L, h:h + 1], func=AF.Sqrt, bias=epsb[:L, 0:1], scale=1.0 / 256), reads=[(rsm, h), epsb], writes=[(rsm, h)])
                    P.op("dve", lambda e, h=h: e.reciprocal(out=rsm[:L, h:h + 1], in_=rsm[:L, h:h + 1]), reads=[(rsm, h)], writes=[(rsm, h)])
                    P.op("dve", lambda e, h=h, pO=pO, rt=rt, rc=rc: e.scalar_tensor_tensor(out=rt[:L, rc:rc + 256], in0=pO[:L, 0:256], scalar=rsm[:L, h:h + 1], in1=retg[:L, l, h * 256:(h + 1) * 256], op0=ALU.mult, op1=ALU.mult), reads=[pO, (rsm, h), retg], writes=[(rt, h % 2)])
                    P.op("dve", lambda e, h=h, s=s, rt=rt, rc=rc: e.tensor_tensor(out=mix_tok[:L, s, 512 + h * 256:512 + (h + 1) * 256], in0=rt[:L, rc:rc + 256], in1=sgB[:L, s, h * 256:(h + 1) * 256], op=ALU.mult), reads=[(rt, h % 2), (sgB, s)], writes=[(mix_tok, s)])

        for cb in range(1, 11):
            wb, wv = wload(win_b[l], win_b[l][cb], 16, 512)
            if cb == 1:
                wb0, wv0 = wload(win_b[l], win_b[l][0], 16, 512)
                for j in range(4):
                    pg = proj_feat(wb, wv, j, TT)
                    P.op("act", lambda e, pg=pg: e.activation(out=sig[:, :TT], in_=pg[:, :TT], func=AF.Sigmoid), reads=[pg], writes=[sig])
                    pvv = proj_feat(wb0, wv0, j, TT)
                    P.op("dve", lambda e, j=j, pvv=pvv: e.tensor_tensor(out=a_ext[:, j, 30:30 + TT], in0=pvv[:, :TT], in1=sig[:, :TT], op=ALU.mult), reads=[pvv, sig, (a_ext, "a")], writes=[(a_ext, "a")])
                continue
            for s in range(NS):
                pa_ = proj_tok(wb, wv, L, 512, s=s)
                if cb in (2, 3):
                    tab = ropeq if cb == 2 else ropek
                    dst = qr if cb == 2 else kr
                    pv4 = pa_[:L, 0:512].rearrange("p (h d) -> p h d", h=4)
                    cosv = tab[:L, s, 0, :].rearrange("p (h d) -> p h d", h=4)
                    sinv = tab[:L, s, 1, :].rearrange("p (h d) -> p h d", h=4)
                    d4 = dst[:L, :].rearrange("p (h d) -> p h d", h=4)
                    tv = [t[:L, 0:256].rearrange("p (h d) -> p h d", h=4) for t in (t1, t2)]
                    P.op("dve", lambda e, pv4=pv4, cosv=cosv, tv=tv: e.tensor_tensor(out=tv[0], in0=pv4[:, :, 0:64], in1=cosv, op=ALU.mult), reads=[pa_, tab], writes=[t1])
                    P.op("dve", lambda e, pv4=pv4, sinv=sinv, tv=tv: e.tensor_tensor(out=tv[1], in0=pv4[:, :, 64:128], in1=sinv, op=ALU.mult), reads=[pa_, tab], writes=[t2])
                    P.op("dve", lambda e, d4=d4, tv=tv: e.tensor_tensor(out=d4[:, :, 0:64], in0=tv[0], in1=tv[1], op=ALU.subtract), reads=[t1, t2], writes=[dst])
                    P.op("dve", lambda e, pv4=pv4, sinv=sinv, tv=tv: e.tensor_tensor(out=tv[0], in0=pv4[:, :, 0:64], in1=sinv, op=ALU.mult), reads=[pa_, tab], writes=[t1])
                    P.op("dve", lambda e, pv4=pv4, cosv=cosv, tv=tv: e.tensor_tensor(out=tv[1], in0=pv4[:, :, 64:128], in1=cosv, op=ALU.mult), reads=[pa_, tab], writes=[t2])
                    P.op("dve", lambda e, d4=d4, tv=tv: e.tensor_tensor(out=d4[:, :, 64:128], in0=tv[0], in1=tv[1], op=ALU.add), reads=[t1, t2], writes=[dst])
                    dT = qTr if cb == 2 else kTr
                    transpose4(dT[:, s, :, :L], [dst[:L, h * 128:(h + 1) * 128] for h in range(4)], L, [dst], [(dT, s)])
                    if cb == 2:
                        P.op("dve", lambda e, s=s: e.tensor_tensor(out=qtT[:, s, :, :L], in0=qTr[:, s, :, :L], in1=qdect[:, :, :L], op=ALU.mult), reads=[(qTr, s), qdect], writes=[(qtT, s)])
                    else:
                        for h in range(4):
                            P.op("dve", lambda e, h=h, s=s: e.tensor_scalar(out=ktil[:L, s, h * 128:(h + 1) * 128], in0=kr[:L, h * 128:(h + 1) * 128], scalar1=kdec[:L, kd, h:h + 1], scalar2=None, op0=ALU.mult), reads=[kr, kdec], writes=[(ktil, s)])
                elif cb in (4, 5):
                    P.op("act", lambda e, pa_=pa_, cb=cb, s=s: e.activation(out=vB[:L, s, (cb - 4) * 512:(cb - 3) * 512], in_=pa_[:L, 0:512], func=AF.Copy), reads=[pa_], writes=[(vB, s)])
                elif cb in (6, 7):
                    P.op("act", lambda e, pa_=pa_, cb=cb, s=s: e.activation(out=sgB[:L, s, (cb - 6) * 512:(cb - 5) * 512], in_=pa_[:L, 0:512], func=AF.Silu), reads=[pa_], writes=[(sgB, s)])
                elif cb in (8, 9):
                    rstd_free(pa_, L, 8, 64, 0)
                    gsrc = qng if cb == 8 else kng
                    dstf = t3 if cb == 8 else knf
                    for g in range(8):
                        P.op("dve", lambda e, g=g, pa_=pa_, gsrc=gsrc, dstf=dstf: e.scalar_tensor_tensor(out=dstf[:L, g * 64:(g + 1) * 64], in0=pa_[:L, g * 64:(g + 1) * 64], scalar=small[:L, g:g + 1], in1=gsrc[:L, l, :], op0=ALU.mult, op1=ALU.mult),
                             reads=[pa_, small, gsrc], writes=[dstf])
                    if cb == 8:
                        P.op("act", lambda e: e.activation(out=qn[:L, :], in_=t3[:L, :], func=AF.Copy), reads=[t3], writes=[qn])
                        transpose4(cqT[:, :, sl(s)], [qn[:L, h * 128:(h + 1) * 128] for h in range(4)], L, [qn], [cqT])
                    else:
                        dma("sp", okd[s], knf[:L, :], [knf], [stream["okbuf"]], "o_knf")
                        P.op("act", lambda e: e.activation(out=knb[:L, :], in_=knf[:L, :], func=AF.Copy), reads=[knf], writes=[knb])
                        transpose4(KTn[:, :, :L], [knb[:L, h * 128:(h + 1) * 128] for h in range(4)], L, [knb], [KTn])
                        for h in range(4):
                            dma("sp", Kc[l][h][:, pos0 + s * L:pos0 + (s + 1) * L], KTn[:, h, :L], [KTn], [Kc[l][h]], "st_KTn")
                elif cb == 10:
                    P.op("act", lambda e, pa_=pa_: e.activation(out=vCf[:L, :], in_=pa_[:L, 0:512], func=AF.Copy), reads=[pa_], writes=[vCf])
                    dma("sp", ovd[s], vCf[:L, :], [vCf], [stream["ovbuf"]], "o_vCf")
                    P.op("dve", lambda e: e.tensor_copy(out=vCb[:L, :], in_=vCf[:L, :]), reads=[vCf], writes=[vCb])
                    dma("sp", Vc[l][pos0 + s * L:pos0 + (s + 1) * L, :], vCb[:L, :], [vCb], [Vc[l]], "st_vCb")
                conv_step((125 + 6 * NS - 1) // (6 * NS))
            if cb == 7:
                ret_phase()
            if cb == 8:
                ln_phase()

        P.op("pool", lambda e: e.memset(Vaug[:, :, 128:129], 1.0), writes=[Vaug])
        for h in range(4):
            dma("sp", KT[:, 0:kvlen], Kc[l][h][:, 0:kvlen], [Kc[l][h]], [KT], "ld_KT")
            nfull = kvlen // 128
            if nfull > 0:
                dma("sp", Vaug[:, 0:nfull, 0:128], Vc[l][0:nfull * 128, h * 128:(h + 1) * 128].rearrange("(kb p) d -> p kb d", p=128), [Vc[l]], [Vaug], "ld_V")
            if kvlen % 128:
                dma("sp", Vaug[0:64, nfull, 0:128], Vc[l][nfull * 128:nfull * 128 + 64, h * 128:(h + 1) * 128], [Vc[l]], [Vaug], "ld_V")
            pOs = [ps("O") for s in range(NS)]

            def qk_unit(m, kb):
                kl = min(128, kvlen - kb * 128)
                s0 = max(0, kb - gq0)
                ncol = TT - s0 * L
                spec = [(s, 0 if kb == gq0 + s else 1) for s in range(s0, NS) if kb in (gq0 + s, gq0 + s - 1)]
                pL = ps("L")
                P.op("pe", lambda e, m=m, kb=kb, kl=kl, pL=pL, h=h, s0=s0, ncol=ncol, spec=spec: e.matmul(pL[:kl, :ncol], lhsT=KT[64 * m:64 * m + 64, kb * 128:kb * 128 + kl], rhs=cqT[64 * m:64 * m + 64, h, s0 * L:TT], start=True, stop=(len(spec) == 0)), reads=[KT, cqT], writes=[pL])
                nsp = []
                for (s, ty) in spec:
                    nsp.append((s, ty, False))
                    if ty == 0 and L == 128:
                        nsp.append((s, ty, True))
                for i, (s, ty, ismask) in enumerate(nsp):
                    lastf = (i == len(nsp) - 1)
                    c0 = (s - s0) * L
                    if not ismask:
                        P.op("pe", lambda e, kl=kl, pL=pL, ty=ty, h=h, lastf=lastf, c0=c0: e.matmul(pL[:kl, c0:c0 + L], lhsT=antib[:, 0:kl], rhs=Hb[:, ty, h, :L], start=False, stop=lastf), reads=[antib, Hb], writes=[pL])
                    else:
                        P.op("pe", lambda e, pL=pL, lastf=lastf, c0=c0: e.matmul(pL[:128, c0:c0 + 128], lhsT=mkq[0:1, 0, :], rhs=mkq[0:1, 1, :], start=False, stop=lastf), reads=[mkq], writes=[pL])
                return (m, kb, kl, s0, ncol, pL)

            def exp_unit(u, idx):
                (m, kb, kl, s0, ncol, pL) = u
                pt_ = PT[idx % 2]
                P.op("act", lambda e, kl=kl, pL=pL, pt_=pt_, h=h, ncol=ncol: e.activation(out=pt_[:kl, :ncol], in_=pL[:kl, :ncol], func=AF.Exp, bias=b15[:kl, h:h + 1], scale=0.125), reads=[pL, b15], writes=[pt_])
                return pt_

            def pv_unit(u, pt_):
                (m, kb, kl, s0, ncol, pL) = u
                for s in range(s0, NS):
                    c0 = (s - s0) * L
                    P.op("pe", lambda e, m=m, kb=kb, kl=kl, pt_=pt_, s=s, c0=c0, pOs=pOs: e.matmul(pOs[s][:L, m * 256:m * 256 + 129], lhsT=pt_[:kl, c0:c0 + L], rhs=Vaug[:kl, kb, 0:129], start=(kb == 0), stop=(kb == gq0 + s)), reads=[pt_, Vaug], writes=[pOs[s]])

            pend = None
            idx = 0
            for m in range(2):
                for kb in range(nkb):
                    u = qk_unit(m, kb)
                    if pend is not None:
                        pv_unit(*pend)
                    pt_ = exp_unit(u, idx)
                    pend = (u, pt_)
                    idx += 1
            pv_unit(*pend)
            for s in range(NS):
                pO = pOs[s]
                P.op("dve", lambda e, pO=pO: e.reciprocal(out=small[:L, 14:15], in_=pO[:L, 128:129]), reads=[pO], writes=[small])
                P.op("dve", lambda e, pO=pO: e.reciprocal(out=small[:L, 15:16], in_=pO[:L, 256 + 128:256 + 129]), reads=[pO], writes=[small])
                P.op("dve", lambda e: e.tensor_tensor(out=small[:L, 15:16], in0=small[:L, 15:16], in1=lamt[:L, l, 1:2], op=ALU.mult), reads=[small, lamt], writes=[small])
                P.op("dve", lambda e, pO=pO: e.tensor_scalar(out=ot2[:L, :], in0=pO[:L, 256:256 + 128], scalar1=small[:L, 15:16], scalar2=None, op0=ALU.mult), reads=[pO, small], writes=[ot2])
                P.op("dve", lambda e, pO=pO: e.scalar_tensor_tensor(out=ot[:L, :], in0=pO[:L, 0:128], scalar=small[:L, 14:15], in1=ot2[:L, :], op0=ALU.mult, op1=ALU.add), reads=[pO, small, ot2], writes=[ot])
                P.op("act", lambda e: e.activation(out=ot2[:L, :], in_=ot[:L, :], func=AF.Square, accum_out=small[:L, 16:17]), reads=[ot], writes=[ot2, small])
                P.op("act", lambda e: e.activation(out=small[:L, 16:17], in_=small[:L, 16:17], func=AF.Sqrt, bias=epsb[:L, 0:1], scale=1.0 / 128), reads=[small, epsb], writes=[small])
                P.op("dve", lambda e: e.reciprocal(out=small[:L, 16:17], in_=small[:L, 16:17]), reads=[small], writes=[small])
                P.op("dve", lambda e: e.scalar_tensor_tensor(out=ot[:L, :], in0=ot[:L, :], scalar=small[:L, 16:17], in1=subg[:L, l, :], op0=ALU.mult, op1=ALU.mult), reads=[ot, small, subg], writes=[ot])
                P.op("dve", lambda e, h=h, s=s: e.tensor_scalar(out=mix_tok[:L, s, 1536 + h * 128:1536 + (h + 1) * 128], in0=ot[:L, :], scalar1=float(1.0 - LAM_INIT[l]), scalar2=None, op0=ALU.mult), reads=[ot], writes=[(mix_tok, s)])

        for s in range(NS):
            for g4 in range(4):
                transpose4(mixT[:, g4 * 4:g4 * 4 + 4, sl(s)], [mix_tok[:L, s, kc * 128:(kc + 1) * 128] for kc in range(g4 * 4, g4 * 4 + 4)], L, [(mix_tok, s)], [(mixT, g4 * 4 + i) for i in range(4)])
        for cb in range(4):
            wb, wv = wload(wout_b[l], wout_b[l][cb], 16, 512)
            for j in range(4):
                pa_ = proj_feat(wb, wv, j, TT, rhs=mixT, role="F")
                c = cb * 4 + j
                P.op("dve", lambda e, c=c, pa_=pa_: e.tensor_tensor(out=xT[:, c, :TT], in0=xT[:, c, :TT], in1=pa_[:, :TT], op=ALU.add), reads=[(xT, c), pa_], writes=[(xT, c)])

        rmsnorm_T(xT, l, 1, TT)
        wb, wv = wload(wxq_b[l], wxq_b[l][0], 16, 512)
        for s in range(NS):
            pq = proj_tok(wb, wv, L, 512, s=s)
            rstd_free(pq, L, 4, 128, 0)
            for h in range(4):
                P.op("dve", lambda e, h=h, pq=pq: e.scalar_tensor_tensor(out=qn[:L, h * 128:(h + 1) * 128], in0=pq[:L, h * 128:(h + 1) * 128], scalar=small[:L, h:h + 1], in1=xqng[:L, l, :], op0=ALU.mult, op1=ALU.mult), reads=[pq, small, xqng], writes=[qn])
            transpose4(cqT[:, :, sl(s)], [qn[:L, h * 128:(h + 1) * 128] for h in range(4)], L, [qn], [cqT])
        for h in range(4):
            pOs = [ps("O") for s in range(NS)]
            for mb in range(2):
                pL = ps("L")
                P.op("pe", lambda e, h=h, mb=mb, pL=pL: e.matmul(pL[:, :TT], lhsT=mKT[l][:, h, mb * 128:(mb + 1) * 128], rhs=cqT[:, h, :TT], start=True, stop=True), reads=[mKT[l], cqT], writes=[pL])
                pt_ = PT[mb]
                P.op("act", lambda e, pL=pL, pt_=pt_: e.activation(out=pt_[:, :TT], in_=pL[:, :TT], func=AF.Exp, scale=128.0 ** -0.5), reads=[pL], writes=[pt_])
                for s in range(NS):
                    P.op("pe", lambda e, h=h, mb=mb, pt_=pt_, s=s, pOs=pOs: e.matmul(pOs[s][:L, 0:129], lhsT=pt_[:, sl(s)], rhs=mVa[l][:, mb, h, :], start=(mb == 0), stop=(mb == 1)), reads=[pt_, mVa[l]], writes=[pOs[s]])
            for s in range(NS):
                pO = pOs[s]
                P.op("dve", lambda e, pO=pO: e.reciprocal(out=small[:L, 18:19], in_=pO[:L, 128:129]), reads=[pO], writes=[small])
                P.op("dve", lambda e, h=h, pO=pO, s=s: e.tensor_scalar(out=mo_tok[:L, s, h * 128:(h + 1) * 128], in0=pO[:L, 0:128], scalar1=small[:L, 18:19], scalar2=None, op0=ALU.mult), reads=[pO, small], writes=[(mo_tok, s)])
        for s in range(NS):
            transpose4(moT[:, :, sl(s)], [mo_tok[:L, s, h * 128:(h + 1) * 128] for h in range(4)], L, [(mo_tok, s)], [(moT, h) for h in range(4)])
        for cb in range(4):
            wb, wv = wload(wxo_b[l], wxo_b[l][cb], 4, 512)
            for j in range(4):
                pa_ = proj_feat(wb, wv, j, TT, nk=4, rhs=moT, role="F")
                c = cb * 4 + j
                P.op("dve", lambda e, c=c, pa_=pa_: e.tensor_tensor(out=xT[:, c, :TT], in0=xT[:, c, :TT], in1=pa_[:, :TT], op=ALU.add), reads=[(xT, c), pa_], writes=[(xT, c)])

        rmsnorm_T(xT, l, 2, TT)
        for (b0, b1) in ((0, 6), (6, 11)):
            nfc = (b1 - b0) * 4
            for b in range(b0, b1):
                for (blk, off) in ((b, 0), (11 + b, 44)):
                  for half in range(2):
                    wb_, wv_ = wload_half(wup_b[l], wup_b[l][blk][:, :, half * 256:(half + 1) * 256], 16, 256)
                    for jj in range(2):
                        j = half * 2 + jj
                        fc = b * 4 + j
                        pa_ = proj_feat(wb_, wv_, jj, TT, role="F")
                        ue = uext[0] if off == 0 else uext[1]
                        ch = off + fc
                        dstc = cv4[:, j, :] if off == 0 else cg[:, :]
                        dkey = (cv4, j) if off == 0 else cg
                        P.op("pool", lambda e, ue=ue, ch=ch: e.tensor_copy(out=ue[:, 0:2], in_=u_hist[l][:, ch, :]), reads=[(u_hist[l], ch)], writes=[ue])
                        P.op("act", lambda e, ue=ue, pa_=pa_: e.activation(out=ue[:, 2:2 + TT], in_=pa_[:, :TT], func=AF.Copy), reads=[pa_], writes=[ue])
                        P.op("pool", lambda e, ue=ue, ch=ch: e.tensor_copy(out=u_hist[l][:, ch, :], in_=ue[:, TT:TT + 2]), reads=[ue], writes=[(u_hist[l], ch)])
                        P.op("act", lambda e, pa_=pa_, ch=ch, dstc=dstc: e.activation(out=dstc[:, :TT], in_=pa_[:, :TT], func=AF.Identity, bias=convfb[:, l, ch:ch + 1], scale=convfw[:, l, ch, 2:3]), reads=[pa_, convfw, convfb], writes=[dkey])
                        for k in (0, 1):
                            P.op("dve", lambda e, ue=ue, ch=ch, dstc=dstc, k=k: e.scalar_tensor_tensor(out=dstc[:, :TT], in0=ue[:, k:k + TT], scalar=convfw[:, l, ch, k:k + 1], in1=dstc[:, :TT], op0=ALU.mult, op1=ALU.add), reads=[ue, convfw, dkey], writes=[dkey])
                        if off == 44:
                            P.op("act", lambda e: e.activation(out=sgt[:, :TT], in_=cg[:, :TT], func=AF.Silu), reads=[cg], writes=[sgt])
                            P.op("dve", lambda e, fc=fc, b0=b0, j=j: e.tensor_tensor(out=gTt[:, fc - b0 * 4, :TT], in0=sgt[:, :TT], in1=cv4[:, j, :TT], op=ALU.mult), reads=[sgt, (cv4, j)], writes=[(gTt, fc - b0 * 4)])
            for c in range(16):
                wbk, wv = wload_half(wdn_b[l], wdn_b[l][c][:, b0 * 4:b1 * 4, :], nfc, 128)
                pa_ = ps("F")
                for fi in range(nfc):
                    P.op("pe", lambda e, fi=fi, pa_=pa_, wv=wv, nfc=nfc: e.matmul(pa_[:, :TT], lhsT=wv[:, fi, :], rhs=gTt[:, fi, :TT], start=(fi == 0), stop=(fi == nfc - 1)), reads=[(gTt, fi), wbk], writes=[pa_])
                P.op("dve", lambda e, c=c, pa_=pa_: e.tensor_tensor(out=xT[:, c, :TT], in0=xT[:, c, :TT], in1=pa_[:, :TT], op=ALU.add), reads=[(xT, c), pa_], writes=[(xT, c)])

    def run_group(stream, xsrc, ydst, pos0, L, NS, okd, ovd):
        for s in range(NS):
            dma("sp", xin[:L, :], xsrc[s], [stream["xbuf"]], [xin], "xin")
            for g in range(4):
                pt = ps("T")
                for j in range(4):
                    kc = g * 4 + j
                    P.op("pe", lambda e, kc=kc, j=j, pt=pt: e.transpose(pt[:, j * L:(j + 1) * L], xin[:L, kc * 128:(kc + 1) * 128], identf[:L, :L]), reads=[xin, identf], writes=[pt])
                P.op("act", lambda e, g=g, pt=pt, s=s: e.activation(out=xT[:, g * 4:g * 4 + 4, s * L:(s + 1) * L], in_=pt[:, 0:4 * L].rearrange("p (a b) -> p a b", a=4), func=AF.Copy), reads=[pt], writes=[(xT, g * 4 + i) for i in range(4)])
        for l in range(2):
            layer(l, stream, pos0, L, NS, okd[l], ovd[l])
        for s in range(NS):
            for g in range(4):
                pt = ps("T")
                for j in range(4):
                    kc = g * 4 + j
                    P.op("pe", lambda e, kc=kc, j=j, pt=pt, s=s: e.transpose(pt[:L, j * 128:(j + 1) * 128], xT[:, kc, s * L:(s + 1) * L], identf[:, :]), reads=[(xT, kc), identf], writes=[pt])
                P.op("act", lambda e, g=g, pt=pt: e.activation(out=yout[:L, g * 512:(g + 1) * 512], in_=pt[:L, 0:512], func=AF.Copy), reads=[pt], writes=[yout])
            dma("sp", ydst[s], yout[:L, :], [yout], [stream["ybuf"]], "yout")

    for l in range(2):
        P.op("pool", lambda e, l=l: e.memset(a_hist[l][:], 0.0), writes=[a_hist[l]])
        P.op("pool", lambda e, l=l: e.memset(Sst[l][:], 0.0), writes=[Sst[l]])
        P.op("pool", lambda e, l=l: e.memset(Sbf[l][:], 0.0), writes=[Sbf[l]])
        P.op("pool", lambda e, l=l: e.memset(u_hist[l][:], 0.0), writes=[u_hist[l]])
    mem_kv_prompt()
    pst = {"xbuf": xp, "ybuf": y_p, "okbuf": o_k_p, "ovbuf": o_v_p}
    NSP = NS_PROMPT
    for t in range(NT_PROMPT // NSP):
        p0 = t * 128 * NSP
        blk = lambda s, p0=p0: slice(p0 + s * 128, p0 + (s + 1) * 128)
        run_group(pst, [xp[blk(s), :] for s in range(NSP)], [y_p[blk(s), :] for s in range(NSP)], p0, 128, NSP,
                  [[o_k_p[l, blk(s), :] for s in range(NSP)] for l in range(2)], [[o_v_p[l, blk(s), :] for s in range(NSP)] for l in range(2)])
    for l in range(2):
        dma("sp", o_ca_p[l], a_hist[l][:], [a_hist[l]], [o_ca_p], "o_ca")
        dma("sp", o_rs_p[l].rearrange("h d e -> d h e"), Sst[l][:], [Sst[l]], [o_rs_p], "o_rs")
        dma("sp", o_cf_p[l], u_hist[l][:], [u_hist[l]], [o_cf_p], "o_cf")

    for si in range(2):
        for l in range(2):
            dma("sp", a_hist[l][:], s_conva[l, si], [s_conva], [a_hist[l]], "ld_ah")
            dma("sp", Sst[l][:], s_ret[l, si].rearrange("h d e -> d h e"), [s_ret], [Sst[l]], "ld_S")
            P.op("pool", lambda e, l=l: e.tensor_copy(out=Sbf[l][:], in_=Sst[l][:]), reads=[Sst[l]], writes=[Sbf[l]])
            dma("sp", u_hist[l][:], s_convf[l, si], [s_convf], [u_hist[l]], "ld_uh")
        mem_kv_sample(si)
        past_kv_sample(si)
        sst = {"xbuf": xs, "ybuf": y_s, "okbuf": o_k_s, "ovbuf": o_v_s}
        run_group(sst, [xs[si]], [y_s[si]], PAST, 64, 1, [[o_k_s[l, si]] for l in range(2)], [[o_v_s[l, si]] for l in range(2)])
        for l in range(2):
            dma("sp", o_ca_s[l, si], a_hist[l][:], [a_hist[l]], [o_ca_s], "o_ca")
            dma("sp", o_rs_s[l, si].rearrange("h d e -> d h e"), Sst[l][:], [Sst[l]], [o_rs_s], "o_rs")
            dma("sp", o_cf_s[l, si], u_hist[l][:], [u_hist[l]], [o_cf_s], "o_cf")

    nsem = P.emit()
    return nc


NT_PROMPT = 32
NS_PROMPT = 2
VAUG_SEP = 0
_NC = None


def _bucket(rel):
    n = abs(rel)
    if n < 8:
        v = n
    else:
        v = 8 + int(np.float32(np.log(np.float32(max(n, 1)) / np.float32(8)) / np.float32(math.log(16.0)) * np.float32(8)))
        v = min(v, 15)
    return (16 if rel > 0 else 0) + v


def _consts():
    c = {}
    c["ident"] = np.eye(128, dtype=np.float32)
    c["anti"] = np.ascontiguousarray(np.eye(128, dtype=np.float32)[::-1])
    half = 64
    inv = (1.0 / (10000.0 ** (np.arange(half, dtype=np.float32) / half))).astype(np.float32)
    pos = np.arange(SEQ, dtype=np.float32)
    ang = pos[:, None] * inv[None, :]
    cos = np.cos(ang).astype(np.float32); sin = np.sin(ang).astype(np.float32)
    rq = np.stack([np.tile(cos, (1, 4)), np.tile(sin, (1, 4))], axis=1).astype(np.float32)
    c["ropeq"] = np.ascontiguousarray(rq)
    c["ropek"] = np.ascontiguousarray(rq * np.float32(128.0 ** -0.5))
    lg = np.log(np.array(GAM, dtype=np.float64))
    i = np.arange(128)
    dect = np.zeros((128, 4, 128), np.float32)
    for h in range(4):
        d = i[None, :] - i[:, None]
        dect[:, h, :] = np.where(d >= 0, np.exp(lg[h] * np.maximum(d, 0)), 0.0)
    c["dect"] = dect
    qd = np.zeros((128, 4, 128), np.float32)
    for h in range(4):
        qd[:, h, :] = np.exp(lg[h] * (i + 1.0))[None, :]
    c["qdect"] = qd
    kd = np.zeros((128, 2, 4), np.float32)
    for h in range(4):
        kd[:, 0, h] = np.exp(lg[h] * (127.0 - i))
        kd[:64, 1, h] = np.exp(lg[h] * (63.0 - i[:64]))
    c["kdec"] = kd
    oh = np.zeros((32, 2, 256), np.float32)
    for ty in range(2):
        for u in range(255):
            t = 254 - u
            rel = (t - 127) if ty == 0 else (t - 255)
            oh[_bucket(rel), ty, u] = 1.0
    c["oh"] = oh
    mk = np.zeros((1, 2, 128), np.float32)
    mk[0, 0, 64:] = 1.0
    mk[0, 1, :64] = -240000.0
    c["mkq"] = mk
    return c


def kernel(**inp):
    global _NC
    f = lambda a: np.ascontiguousarray(np.asarray(a, dtype=np.float32))
    I = {k: f(v) for k, v in inp.items()}
    if _NC is None:
        _NC = build()
    nc = _NC
    cst = _consts()
    rep = lambda v: np.ascontiguousarray(np.broadcast_to(v[None], (128,) + v.shape))
    shared = dict(cst)
    for k in ["w_in", "w_out", "w_xq", "w_xk", "w_xv", "w_xo", "w_up", "w_down"]:
        shared[k] = I[k]
    gs = np.stack([I["norm1_g"], I["norm2_g"], I["norm3_g"], I["mem_norm_g"]], axis=1)
    shared["gT"] = np.ascontiguousarray(gs.reshape(2, 4, 16, 128).transpose(3, 0, 1, 2))
    shared["convaw"] = np.ascontiguousarray(I["conv_a_w"].reshape(2, 31, 4, 128).transpose(3, 0, 2, 1))
    shared["convab"] = np.ascontiguousarray(I["conv_a_b"].reshape(2, 4, 128).transpose(2, 0, 1))
    shared["lnag"] = rep(I["ln_a_g"]); shared["lnab"] = rep(I["ln_a_b"]); shared["retg"] = rep(I["ret_gn_g"])
    shared["qng"] = rep(I["diff_qn_g"]); shared["kng"] = rep(I["diff_kn_g"])
    shared["lqk"] = rep(np.stack([I["diff_lq1"], I["diff_lk1"], I["diff_lq2"], I["diff_lk2"]], axis=1))
    shared["subg"] = rep(I["diff_subln_g"])
    shared["relb"] = I["rel_bias"]; shared["relbT"] = np.ascontiguousarray(I["rel_bias"].T)
    shared["b15"] = rep(I["rel_bias"][15])
    shared["xqng"] = rep(I["xqn_g"]); shared["xkng"] = rep(I["xkn_g"])
    shared["convfw"] = np.ascontiguousarray(I["conv_f_w"].reshape(2, 3, 88, 128).transpose(3, 0, 2, 1))
    shared["convfb"] = np.ascontiguousarray(I["conv_f_b"].reshape(2, 88, 128).transpose(2, 0, 1))
    in_maps = []
    for c in range(8):
        m = dict(shared)
        b = c % 2
        ss = [2 * c, 2 * c + 1]
        m["xp"] = I["x_prompt"][b]; m["xs"] = np.ascontiguousarray(I["x_sample"][ss]); m["memp"] = I["mem_prompt"][b]
        sca = I["state_conv_a"][:, ss]
        m["s_conva"] = np.ascontiguousarray(sca.reshape(2, 2, 30, 4, 128).transpose(0, 1, 4, 3, 2))
        m["s_ret"] = np.ascontiguousarray(I["state_ret"][:, ss])
        m["c_dk"] = np.ascontiguousarray(I["cache_diff_k"][:, ss].reshape(2, 2, PAST, 512))
        m["c_dv"] = np.ascontiguousarray(I["cache_diff_v"][:, ss].reshape(2, 2, PAST, 512))
        m["c_mk"] = np.ascontiguousarray(I["cache_mem_k"][:, ss].reshape(2, 2, 256, 512))
        m["c_mv"] = np.ascontiguousarray(I["cache_mem_v"][:, ss].reshape(2, 2, 256, 512))
        scf = I["state_conv_f"][:, ss]
        m["s_convf"] = np.ascontiguousarray(scf.reshape(2, 2, 2, 88, 128).transpose(0, 1, 4, 3, 2))
        in_maps.append(m)
    res = run_bass_kernel_spmd(nc, in_maps, core_ids=list(range(8)))
    R = res.results
    P2 = [R[0], R[1]]
    y_p = np.stack([r["y_p"] for r in P2])
    y_s = np.concatenate([R[c]["y_s"] for c in range(8)], axis=0)
    ca_p = np.stack([r["o_ca_p"].transpose(0, 3, 2, 1).reshape(2, 30, 512) for r in P2], axis=1)
    rs_p = np.stack([r["o_rs_p"] for r in P2], axis=1)
    k_p = np.stack([r["o_k_p"].reshape(2, SEQ, 4, 128) for r in P2], axis=1)
    v_p = np.stack([r["o_v_p"].reshape(2, SEQ, 4, 128) for r in P2], axis=1)
    mk_p = np.stack([r["o_mk_p"].reshape(2, 256, 4, 128) for r in P2], axis=1)
    mv_p = np.stack([r["o_mv_p"].reshape(2, 256, 4, 128) for r in P2], axis=1)
    cf_p = np.stack([r["o_cf_p"].transpose(0, 3, 2, 1).reshape(2, 2, 2 * FF) for r in P2], axis=1)
    ca_s = np.concatenate([R[c]["o_ca_s"].transpose(0, 1, 4, 3, 2).reshape(2, 2, 30, 512) for c in range(8)], axis=1)
    rs_s = np.concatenate([R[c]["o_rs_s"] for c in range(8)], axis=1)
    k_s = np.concatenate([R[c]["o_k_s"].reshape(2, 2, 64, 4, 128) for c in range(8)], axis=1)
    v_s = np.concatenate([R[c]["o_v_s"].reshape(2, 2, 64, 4, 128) for c in range(8)], axis=1)
    cf_s = np.concatenate([R[c]["o_cf_s"].transpose(0, 1, 4, 3, 2).reshape(2, 2, 2, 2 * FF) for c in range(8)], axis=1)
    outs = (y_p, y_s, ca_p, rs_p, k_p, v_p, mk_p, mv_p, cf_p, ca_s, rs_s, k_s, v_s, cf_s)
    return tuple(np.ascontiguousarray(o.astype(np.float32)) for o in outs)
```

```python
import numpy as np
import concourse.bass as bass
import concourse.mybir as mybir

F32 = mybir.dt.float32
BF16 = mybir.dt.bfloat16
ALU = mybir.AluOpType
AF = mybir.ActivationFunctionType
AX = mybir.AxisListType

ENGS = ["pe", "act", "dve", "pool", "sp"]
EPOCH = 16000


class Buf:
    def __init__(self, name, t, parent=None):
        self.name = name
        self.t = t
        self._p = parent
        self._whole = {"w": None, "r": {}}
        self._parts = {}

    @property
    def whole(self):
        return self._p.whole if self._p is not None else self._whole

    @whole.setter
    def whole(self, v):
        if self._p is not None:
            self._p.whole = v
        else:
            self._whole = v

    @property
    def parts(self):
        return self._p.parts if self._p is not None else self._parts

    @parts.setter
    def parts(self, v):
        if self._p is not None:
            self._p.parts = v
        else:
            self._parts = v

    def __getitem__(self, idx):
        return self.t[idx]


class Op:
    __slots__ = ("eng", "fn", "waits", "signal", "seq", "dma", "dmaval", "sigval")

    def __init__(self, eng, fn):
        self.eng = eng
        self.fn = fn
        self.waits = []
        self.signal = False
        self.seq = 0
        self.dma = None
        self.dmaval = 0
        self.sigval = 0


class Prog:
    def __init__(self, nc):
        self.nc = nc
        self.ops = {e: [] for e in ENGS}
        self.dma_cnt = {}
        self.waited = {e: {} for e in ENGS}

    @staticmethod
    def _norm(lst):
        out = []
        for x in lst:
            if isinstance(x, tuple):
                out.append(x)
            else:
                out.append((x, None))
        return out

    def op(self, eng, fn, reads=(), writes=(), dma=None):
        o = Op(eng, fn)
        o.seq = len(self.ops[eng])
        reads = self._norm(reads)
        writes = self._norm(writes)
        deps = []

        def add(tok):
            if tok is not None:
                deps.append(tok)

        for b, k in reads:
            add(b.whole["w"])
            if k is None:
                for st in b.parts.values():
                    add(st["w"])
            elif k in b.parts:
                add(b.parts[k]["w"])
        for b, k in writes:
            add(b.whole["w"])
            for t in b.whole["r"].values():
                add(t)
            if k is None:
                for st in b.parts.values():
                    add(st["w"])
                    for t in st["r"].values():
                        add(t)
            elif k in b.parts:
                add(b.parts[k]["w"])
                for t in b.parts[k]["r"].values():
                    add(t)
        if dma is not None:
            self.dma_cnt[dma] = self.dma_cnt.get(dma, 0) + 16
            o.dma = dma
            o.dmaval = self.dma_cnt[dma]
            tok = ("d", dma, o.dmaval, o)
        else:
            tok = ("e", eng, o.seq, o)
        wd = self.waited[eng]
        for t in deps:
            if t[0] == "e":
                if t[1] == "pe" and eng == "pe":
                    continue
                key = ("e", t[1])
                val = t[2]
            else:
                key = ("d", t[1])
                val = t[2]
            if wd.get(key, -1) >= val:
                continue
            wd[key] = val
            o.waits.append(t)
            t[3].signal = True
        self.ops[eng].append(o)
        for b, k in reads:
            st = b.whole if k is None else b.parts.setdefault(k, {"w": None, "r": {}})
            rk = (tok[0], tok[1])
            st["r"][rk] = tok
        for b, k in writes:
            if k is None:
                b.parts = {}
                b.whole = {"w": tok, "r": {}}
            else:
                b.parts[k] = {"w": tok, "r": {}}
        return o

    def emit(self, final_wait_eng="sp"):
        nc = self.nc
        sems = {}

        def getsem(key):
            if key not in sems:
                sems[key] = nc.alloc_semaphore("s_%s" % "_".join(str(x) for x in key))
            return sems[key]

        for e in ENGS:
            c = 0
            for o in self.ops[e]:
                if o.dma is None and o.signal:
                    c += 1
                    o.sigval = c
        fin = Op(final_wait_eng, None)
        for dkey, val in self.dma_cnt.items():
            fin.waits.append(("d", dkey, val, None))
        engmap = {"pe": "tensor", "act": "scalar", "dve": "vector", "pool": "gpsimd", "sp": "sync"}
        with nc.Block() as block:
            for e in ENGS:
                ops = self.ops[e] + ([fin] if e == final_wait_eng else [])

                def body(eng, ops=ops, e=e):
                    for o in ops:
                        for t in o.waits:
                            if t[0] == "e":
                                p = t[3]
                                ep = (p.sigval - 1) // EPOCH
                                eng.wait_ge(getsem(("e", t[1], ep)), p.sigval - ep * EPOCH)
                            else:
                                eng.wait_ge(getsem(("d", t[1])), t[2])
                        if o.fn is None:
                            continue
                        ins = o.fn(eng)
                        if o.dma is not None:
                            ins.then_inc(getsem(("d", o.dma)), 16)
                        elif o.signal:
                            ep = (o.sigval - 1) // EPOCH
                            ins.then_inc(getsem(("e", e, ep)), 1)

                getattr(block, engmap[e])(body)
        return len(sems)


import math
import ml_dtypes
from concourse.bass_utils import run_bass_kernel_spmd


D = 2048
FF = 5632
GAM = [1.0 - 2.0 ** (-5 - h) for h in range(4)]
LAM_INIT = [0.8 - 0.6 * math.exp(-0.3 * l) for l in range(2)]
EPS = 1e-6
SEQ = 4096
PAST = 1024


def build():
    nc = bass.Bass("TRN2", target_bir_lowering=False)
    P = Prog(nc)

    def din(name, shape):
        return Buf(name, nc.dram_tensor(name, list(shape), F32, kind="ExternalInput").ap())

    def dout(name, shape):
        return Buf(name, nc.dram_tensor(name, list(shape), F32, kind="ExternalOutput").ap())

    def dscr(name, shape, dt):
        return Buf(name, nc.dram_tensor(name, list(shape), dt).ap())

    def sb(name, shape, dt=F32):
        return Buf(name, nc.alloc_sbuf_tensor(name, list(shape), dt).ap())

    xp = din("xp", [SEQ, D]); xs = din("xs", [2, 64, D]); memp = din("memp", [256, D])
    s_conva = din("s_conva", [2, 2, 128, 4, 30]); s_ret = din("s_ret", [2, 2, 4, 128, 256])
    c_dk = din("c_dk", [2, 2, PAST, 512]); c_dv = din("c_dv", [2, 2, PAST, 512])
    c_mk = din("c_mk", [2, 2, 256, 512]); c_mv = din("c_mv", [2, 2, 256, 512])
    s_convf = din("s_convf", [2, 2, 128, 88, 2])
    w_in = din("w_in", [2, D, 5632]); w_out = din("w_out", [2, D, D])
    w_xq = din("w_xq", [2, D, 512]); w_xk = din("w_xk", [2, D, 512]); w_xv = din("w_xv", [2, D, 512])
    w_xo = din("w_xo", [2, 512, D]); w_up = din("w_up", [2, D, 2 * FF]); w_down = din("w_down", [2, FF, D])
    gT_d = din("gT", [128, 2, 4, 16])
    convaw_d = din("convaw", [128, 2, 4, 31]); convab_d = din("convab", [128, 2, 4])
    lnag_d = din("lnag", [128, 2, 512]); lnab_d = din("lnab", [128, 2, 512])
    retg_d = din("retg", [128, 2, 1024])
    qng_d = din("qng", [128, 2, 64]); kng_d = din("kng", [128, 2, 64])
    lqk_d = din("lqk", [128, 2, 4, 64])
    subg_d = din("subg", [128, 2, 128])
    relb_d = din("relb", [32, 4]); relbT_d = din("relbT", [4, 32]); b15_d = din("b15", [128, 4])
    xqng_d = din("xqng", [128, 2, 128]); xkng_d = din("xkng", [128, 2, 128])
    convfw_d = din("convfw", [128, 2, 88, 3]); convfb_d = din("convfb", [128, 2, 88])
    ident_d = din("ident", [128, 128]); anti_d = din("anti", [128, 128])
    ropeq_d = din("ropeq", [SEQ, 2, 256]); ropek_d = din("ropek", [SEQ, 2, 256])
    dect_d = din("dect", [128, 4, 128]); qdect_d = din("qdect", [128, 4, 128]); kdec_d = din("kdec", [128, 2, 4])
    oh_d = din("oh", [32, 2, 256]); mkq_d = din("mkq", [1, 2, 128])

    y_p = dout("y_p", [SEQ, D]); y_s = dout("y_s", [2, 64, D])
    o_ca_p = dout("o_ca_p", [2, 128, 4, 30]); o_rs_p = dout("o_rs_p", [2, 4, 128, 256])
    o_k_p = dout("o_k_p", [2, SEQ, 512]); o_v_p = dout("o_v_p", [2, SEQ, 512])
    o_mk_p = dout("o_mk_p", [2, 256, 512]); o_mv_p = dout("o_mv_p", [2, 256, 512])
    o_cf_p = dout("o_cf_p", [2, 128, 88, 2])
    o_ca_s = dout("o_ca_s", [2, 2, 128, 4, 30]); o_rs_s = dout("o_rs_s", [2, 2, 4, 128, 256])
    o_k_s = dout("o_k_s", [2, 2, 64, 512]); o_v_s = dout("o_v_s", [2, 2, 64, 512])
    o_cf_s = dout("o_cf_s", [2, 2, 128, 88, 2])

    win_b = [dscr("win_b%d" % l, [11, 128, 16, 512], BF16) for l in range(2)]
    wout_b = [dscr("wout_b%d" % l, [4, 128, 16, 512], BF16) for l in range(2)]
    wxq_b = [dscr("wxq_b%d" % l, [1, 128, 16, 512], BF16) for l in range(2)]
    wxk_b = [dscr("wxk_b%d" % l, [1, 128, 16, 512], BF16) for l in range(2)]
    wxv_b = [dscr("wxv_b%d" % l, [1, 128, 16, 512], BF16) for l in range(2)]
    wxo_b = [dscr("wxo_b%d" % l, [4, 128, 4, 512], BF16) for l in range(2)]
    wup_b = [dscr("wup_b%d" % l, [22, 128, 16, 512], BF16) for l in range(2)]
    wdn_b = [dscr("wdn_b%d" % l, [16, 128, 44, 128], BF16) for l in range(2)]
    KVLEN = SEQ
    Kc = [[dscr("Kc%d_%d" % (l, h), [128, KVLEN], BF16) for h in range(4)] for l in range(2)]
    Vc = [dscr("Vc%d" % l, [KVLEN, 512], BF16) for l in range(2)]
    tsc = dscr("tsc", [4, 2, 256], F32)

    NSM = NS_PROMPT
    TM = 128 * NSM
    xT = sb("xT", [128, 16, TM]); hT = sb("hT", [128, 16, TM], BF16)
    rstd = sb("rstd", [128, TM]); epsb = sb("epsb", [128, 1])
    sqr = [sb("sq%d" % i, [128, TM], BF16) for i in range(8)]
    wring = [sb("wr%d" % i, [128, 8192], BF16) for i in range(2)]
    R2 = sb("R2", [128, max(24 * TM, 4096)], BF16)
    xin = Buf("xin", R2[:, 0:4096].bitcast(F32), parent=R2)
    mixT = Buf("mixT", R2[:, 0:16 * TM].rearrange("p (a b) -> p a b", a=16), parent=R2)
    gTt = Buf("gTt", R2[:, 0:24 * TM].rearrange("p (a b) -> p a b", a=24), parent=R2)
    AEW = 30 + TM
    R1 = sb("R1", [128, max(4 * AEW + 4 * TM, 2064)], F32)
    a_ext = Buf("a_ext", R1[:, 0:4 * AEW].rearrange("p (a b) -> p a b", a=4), parent=R1)
    cacc = Buf("cacc", R1[:, 4 * AEW:4 * AEW + 4 * TM].rearrange("p (a b) -> p a b", a=4), parent=R1)
    Vaug = sb("Vaug", [128, 32, 129], BF16) if VAUG_SEP else Buf("Vaug", R1[:, 0:2064].bitcast(BF16).rearrange("p (a b) -> p a b", a=32), parent=R1)
    identf = sb("identf", [128, 128]); identb = sb("identb", [128, 128], BF16); antib = sb("antib", [128, 128], BF16)
    onesb = sb("onesb", [128, 128], BF16)
    gTs = sb("gTs", [128, 2, 4, 16]); convaw = sb("convaw_s", [128, 2, 4, 31]); convab = sb("convab_s", [128, 2, 4])
    lnag = sb("lnag_s", [128, 2, 512], BF16); lnab = sb("lnab_s", [128, 2, 512], BF16); retg = sb("retg_s", [128, 2, 1024], BF16)
    qng = sb("qng_s", [128, 2, 64]); kng = sb("kng_s", [128, 2, 64]); lqk = sb("lqk_s", [128, 2, 4, 64])
    subg = sb("subg_s", [128, 2, 128]); b15 = sb("b15_s", [128, 4])
    xqng = sb("xqng_s", [128, 2, 128]); xkng = sb("xkng_s", [128, 2, 128])
    convfw = sb("convfw_s", [128, 2, 88, 3]); convfb = sb("convfb_s", [128, 2, 88])
    dect = sb("dect_s", [128, 4, 128]); qdect = sb("qdect_s", [128, 4, 128]); kdec = sb("kdec_s", [128, 2, 4])
    mkq = sb("mkq_s", [1, 2, 128], BF16)
    Hb = sb("Hb", [128, 2, 4, 128], BF16)
    relb = Buf("relb_s", R2[0:32, 0:8].bitcast(F32), parent=R2)
    relbT = Buf("relbT_s", R2[0:4, 8:72].bitcast(F32), parent=R2)
    ohs = Buf("oh_s", R2[0:32, 128:1152].bitcast(F32).rearrange("p (a b) -> p a b", a=2), parent=R2)
    tsb = Buf("tsb", R2[0:4, 1152:2176].bitcast(F32).rearrange("p (a b) -> p a b", a=2), parent=R2)
    Hf = Buf("Hf", R2[:, 2176:4224].bitcast(F32).rearrange("p (a b c) -> p a b c", a=2, b=4), parent=R2)
    lamt = sb("lamt", [128, 2, 8])
    ropeq = sb("ropeq_s", [128, NSM, 2, 256]); ropek = sb("ropek_s", [128, NSM, 2, 256])
    a_hist = [sb("a_hist%d" % l, [128, 4, 30]) for l in range(2)]
    sig = sb("sig", [128, TM])
    Sst = [sb("S%d" % l, [128, 4, 256]) for l in range(2)]; Sbf = [sb("Sbf%d" % l, [128, 4, 256], BF16) for l in range(2)]
    u_hist = [sb("u_hist%d" % l, [128, 88, 2]) for l in range(2)]
    mix_tok = sb("mix_tok", [128, NSM, 2048], BF16)
    KT = sb("KT", [128, KVLEN], BF16)
    PT = [sb("PT%d" % i, [128, TM], BF16) for i in range(2)]
    t1 = sb("t1", [128, 512]); t2 = sb("t2", [128, 512]); t3 = sb("t3", [128, 512]); t4 = sb("t4", [128, 512])
    qr = sb("qr", [128, 512], BF16); kr = sb("kr", [128, 512], BF16)
    ktil = sb("ktil", [128, NSM, 512], BF16)
    vB = sb("vB", [128, NSM, 1024], BF16); sgB = sb("sgB", [128, NSM, 1024], BF16)
    qTr = sb("qTr", [128, NSM, 4, 128], BF16); kTr = sb("kTr", [128, NSM, 4, 128], BF16); qtT = sb("qtT", [128, NSM, 4, 128], BF16)
    STb = sb("STb", [128, 4, 128], BF16); rsm = sb("rsm", [128, 4])
    small = sb("small", [128, 32])
    qn = sb("qn", [128, 512], BF16); knf = sb("knf", [128, 512]); knb = sb("knb", [128, 512], BF16)
    vCf = sb("vCf", [128, 512]); vCb = sb("vCb", [128, 512], BF16)
    cqT = sb("cqT", [128, 4, TM], BF16); KTn = sb("KTn", [128, 4, 128], BF16)
    ot = sb("ot", [128, 128]); ot2 = sb("ot2", [128, 128])
    mKT = [sb("mKT%d" % l, [128, 4, 256], BF16) for l in range(2)]
    mVa = [sb("mVa%d" % l, [128, 2, 4, 129], BF16) for l in range(2)]
    memx = xin; yout = xin
    mo_tok = sb("mo_tok", [128, NSM, 512], BF16); moT = sb("moT", [128, 4, TM], BF16)
    uext = [sb("uext%d" % i, [128, TM + 2]) for i in range(2)]
    cv4 = sb("cv4", [128, 4, TM]); cg = sb("cg", [128, TM]); sgt = sb("sgt", [128, TM])

    banks = [Buf("ps%d" % i, nc.alloc_psum_tensor("ps%d" % i, [128, 512], F32).ap()) for i in range(8)]
    rr = {"A": [0, [0, 1]], "T": [0, [2, 3]], "L": [0, [4, 5]], "O": [0, [6, 7]], "F": [0, [0, 1, 2, 3, 4, 5, 6, 7]]}

    def ps(role):
        r = rr[role]
        b = banks[r[1][r[0] % len(r[1])]]
        r[0] += 1
        return b

    sqi = [0]

    def sqn():
        sqi[0] += 1
        return sqr[sqi[0] % 8]

    def dma(eng, out_ap, in_ap, reads, writes, key):
        P.op(eng, lambda e: e.dma_start(out=out_ap, in_=in_ap), reads=reads, writes=writes, dma=key)

    def load_small(dst, src):
        dma("sp", dst[:], src[:], [src], [dst], "ld_" + dst.name)

    widx = [0]

    def wload(srcbuf, src_ap, a, b):
        slot = widx[0] % 2
        widx[0] += 1
        wb = wring[slot]
        dst = wb[:, 0:a * b].rearrange("p (a b) -> p a b", a=a)
        dma("sp", dst, src_ap, [srcbuf], [wb], "w%d" % slot)
        return wb, dst

    hidx = [0]

    def wload_half(srcbuf, src_ap, a, b):
        k = hidx[0] % 4
        hidx[0] += 1
        wb = wring[k // 2]
        off = (k % 2) * 4096
        key = "ab"[k % 2]
        dst = wb[:, off:off + a * b].rearrange("p (a b) -> p a b", a=a)
        dma("sp", dst, src_ap, [srcbuf], [(wb, key)], "w%d%s" % (k // 2, key))
        return (wb, key), dst

    def transpose_bf(dst_ap, src_ap, rows, cols, reads, writes):
        pt = ps("T")
        ptb = pt[:, :].bitcast(BF16)
        P.op("pe", lambda e: e.transpose(ptb[:cols, 0:rows], src_ap, identb[:rows, :rows]), reads=reads + [identb], writes=[pt])
        P.op("act", lambda e: e.activation(out=dst_ap, in_=ptb[:cols, 0:rows], func=AF.Copy), reads=[pt], writes=writes)

    def transpose4(dst3, srcs, rows, reads, writes):
        pt = ps("T")
        ptb = pt[:, :].bitcast(BF16)
        for i, sa in enumerate(srcs):
            P.op("pe", lambda e, i=i, sa=sa: e.transpose(ptb[:, i * rows:(i + 1) * rows], sa, identb[:rows, :rows]), reads=reads + [identb], writes=[pt])
        P.op("act", lambda e: e.activation(out=dst3, in_=ptb[:, 0:4 * rows].rearrange("p (a b) -> p a b", a=4), func=AF.Copy), reads=[pt], writes=writes)

    def rmsnorm_T(src, l, gi, TT):
        pa = ps("A")
        for kc in range(16):
            sq = sqn()
            P.op("act", lambda e, kc=kc, sq=sq: e.activation(out=sq[:, :TT], in_=src[:, kc, :TT], func=AF.Square), reads=[(src, kc)], writes=[sq])
            P.op("pe", lambda e, kc=kc, sq=sq: e.matmul(pa[:, :TT], lhsT=onesb[:, :], rhs=sq[:, :TT], start=(kc == 0), stop=(kc == 15)), reads=[sq, onesb], writes=[pa])
        P.op("act", lambda e: e.activation(out=rstd[:, :TT], in_=pa[:, :TT], func=AF.Sqrt, bias=epsb[:, 0:1], scale=1.0 / D), reads=[pa, epsb], writes=[rstd])
        P.op("dve", lambda e: e.reciprocal(out=rstd[:, :TT], in_=rstd[:, :TT]), reads=[rstd], writes=[rstd])
        for kc in range(16):
            P.op("dve", lambda e, kc=kc: e.scalar_tensor_tensor(out=hT[:, kc, :TT], in0=src[:, kc, :TT], scalar=gTs[:, l, gi, kc:kc + 1], in1=rstd[:, :TT], op0=ALU.mult, op1=ALU.mult),
                 reads=[(src, kc), rstd, gTs], writes=[(hT, kc)])

    def proj_tok(wb, wv, L, ncol, nk=16, lhs=None, s=0):
        lhs = hT if lhs is None else lhs
        pa = ps("A")
        for kc in range(nk):
            P.op("pe", lambda e, kc=kc: e.matmul(pa[:L, :ncol], lhsT=lhs[:, kc, s * L:(s + 1) * L], rhs=wv[:, kc, :ncol], start=(kc == 0), stop=(kc == nk - 1)), reads=[(lhs, kc), wb], writes=[pa])
        return pa

    def proj_feat(wb, wv, j, TT, nk=16, rhs=None, role="A"):
        rhs = hT if rhs is None else rhs
        pa = ps(role)
        for kc in range(nk):
            P.op("pe", lambda e, kc=kc: e.matmul(pa[:, :TT], lhsT=wv[:, kc, j * 128:(j + 1) * 128], rhs=rhs[:, kc, :TT], start=(kc == 0), stop=(kc == nk - 1)), reads=[(rhs, kc), wb], writes=[pa])
        return pa

    def rstd_free(pa, L, ngrp, gsz, col0):
        P.op("act", lambda e: e.activation(out=t4[:L, :ngrp * gsz], in_=pa[:L, :ngrp * gsz], func=AF.Square), reads=[pa], writes=[t4])
        P.op("dve", lambda e: e.reduce_sum(out=small[:L, col0:col0 + ngrp], in_=t4[:L, :ngrp * gsz].rearrange("p (g j) -> p g j", g=ngrp), axis=AX.X), reads=[t4], writes=[small])
        P.op("act", lambda e: e.activation(out=small[:L, col0:col0 + ngrp], in_=small[:L, col0:col0 + ngrp], func=AF.Sqrt, bias=epsb[:L, 0:1], scale=1.0 / gsz), reads=[small, epsb], writes=[small])
        P.op("dve", lambda e: e.reciprocal(out=small[:L, col0:col0 + ngrp], in_=small[:L, col0:col0 + ngrp]), reads=[small], writes=[small])

    P.op("dve", lambda e: e.memset(epsb[:, :], EPS), writes=[epsb])
    P.op("dve", lambda e: e.memset(onesb[:, :], 1.0), writes=[onesb])
    P.op("dve", lambda e: e.memset(Vaug[:, :, 128:129], 1.0), writes=[Vaug])
    for l in range(2):
        P.op("dve", lambda e, l=l: e.memset(mVa[l][:, :, :, 128:129], 1.0), writes=[mVa[l]])
    for dst, src in [(identf, ident_d), (gTs, gT_d), (convaw, convaw_d), (convab, convab_d), (qng, qng_d), (kng, kng_d), (lqk, lqk_d), (subg, subg_d), (b15, b15_d), (relb, relb_d),
                     (relbT, relbT_d), (ohs, oh_d), (xqng, xqng_d), (xkng, xkng_d), (convfw, convfw_d), (convfb, convfb_d),
                     (dect, dect_d), (qdect, qdect_d), (kdec, kdec_d)]:
        load_small(dst, src)
    for dst, src in [(lnag, lnag_d), (lnab, lnab_d), (retg, retg_d)]:
        dma("pool", dst[:], src[:], [src], [dst], "ld_" + dst.name)
    dma("pool", identb[:], ident_d[:], [ident_d], [identb], "ld_identb")
    dma("pool", antib[:], anti_d[:], [anti_d], [antib], "ld_antib")
    dma("pool", mkq[:], mkq_d[:], [mkq_d], [mkq], "ld_mkq")

    for l in range(2):
        for wi, (src, dst) in enumerate([(w_xk, wxk_b), (w_xv, wxv_b), (w_in, win_b), (w_out, wout_b), (w_xq, wxq_b), (w_xo, wxo_b), (w_up, wup_b), (w_down, wdn_b)]):
            nb, _, nk, bw = dst[l].t.shape
            key = "cv%d_%d" % (wi, l)
            for c in range(nb):
                dma("pool", dst[l][c], src[l, :, c * bw:(c + 1) * bw].rearrange("(kc p) c -> p kc c", p=128), [src], [dst[l]], key)
            _last = P.ops["pool"][-1]
            dst[l].whole["w"] = ("d", key, _last.dmaval, _last)

    for l in range(2):
        P.op("dve", lambda e, l=l: e.tensor_tensor(out=t1[:, 0:64], in0=lqk[:, l, 0, :], in1=lqk[:, l, 1, :], op=ALU.mult), reads=[lqk], writes=[t1])
        P.op("dve", lambda e, l=l: e.reduce_sum(out=lamt[:, l, 2:3], in_=t1[:, 0:64], axis=AX.X), reads=[t1], writes=[lamt])
        P.op("dve", lambda e, l=l: e.tensor_tensor(out=t1[:, 64:128], in0=lqk[:, l, 2, :], in1=lqk[:, l, 3, :], op=ALU.mult), reads=[lqk], writes=[t1])
        P.op("dve", lambda e, l=l: e.reduce_sum(out=lamt[:, l, 3:4], in_=t1[:, 64:128], axis=AX.X), reads=[t1], writes=[lamt])
        P.op("act", lambda e, l=l: e.activation(out=lamt[:, l, 4:6], in_=lamt[:, l, 2:4], func=AF.Exp), reads=[lamt], writes=[lamt])
        P.op("dve", lambda e, l=l: e.tensor_tensor(out=lamt[:, l, 6:7], in0=lamt[:, l, 4:5], in1=lamt[:, l, 5:6], op=ALU.subtract), reads=[lamt], writes=[lamt])
        P.op("dve", lambda e, l=l: e.tensor_scalar(out=lamt[:, l, 1:2], in0=lamt[:, l, 6:7], scalar1=LAM_INIT[l], scalar2=-1.0, op0=ALU.add, op1=ALU.mult), reads=[lamt], writes=[lamt])

    pa = ps("A")
    P.op("pe", lambda e: e.matmul(pa[:4, 0:512], lhsT=relb[:, :], rhs=ohs[:, :, :].rearrange("p a b -> p (a b)"), start=True, stop=True), reads=[relb, ohs], writes=[pa])
    P.op("dve", lambda e: e.tensor_scalar(out=tsb[:, :, :].rearrange("p a b -> p (a b)"), in0=pa[:4, 0:512], scalar1=relbT[:, 15:16], scalar2=8.0, op0=ALU.subtract, op1=ALU.mult), reads=[pa, relbT], writes=[tsb])
    dma("sp", tsc[:], tsb[:], [tsb], [tsc], "tsc")
    for ty in range(2):
        for h in range(4):
            src = bass.AP(tensor=tsc.t.tensor, offset=(h * 2 + ty) * 256, ap=[[1, 128], [1, 128]])
            dma("sp", Hf[:, ty, h, :], src, [tsc], [Hf], "Hf")
    P.op("dve", lambda e: e.tensor_copy(out=Hb[:], in_=Hf[:]), reads=[Hf], writes=[Hb])

    def mem_kv_prompt():
        for l in range(2):
            wk, wkv = wload(wxk_b[l], wxk_b[l][0], 16, 512)
            wv_, wvv = wload(wxv_b[l], wxv_b[l][0], 16, 512)
            for mb in range(2):
                dma("sp", memx[:, :], memp[mb * 128:(mb + 1) * 128, :], [memp], [memx], "memx")
                P.op("act", lambda e: e.activation(out=mix_tok[:, 0, :], in_=memx[:, :], func=AF.Square, accum_out=small[:, 20:21]), reads=[memx], writes=[mix_tok, small])
                P.op("act", lambda e: e.activation(out=small[:, 20:21], in_=small[:, 20:21], func=AF.Sqrt, bias=epsb[:, 0:1], scale=1.0 / D), reads=[small, epsb], writes=[small])
                P.op("dve", lambda e: e.reciprocal(out=small[:, 20:21], in_=small[:, 20:21]), reads=[small], writes=[small])
                P.op("dve", lambda e: e.tensor_scalar(out=memx[:, :], in0=memx[:, :], scalar1=small[:, 20:21], scalar2=None, op0=ALU.mult), reads=[memx, small], writes=[memx])
                for kc in range(16):
                    pt = ps("T")
                    P.op("pe", lambda e, kc=kc, pt=pt: e.transpose(pt[:, 0:128], memx[:, kc * 128:(kc + 1) * 128], identf[:, :]), reads=[memx, identf], writes=[pt])
                    P.op("dve", lambda e, kc=kc, pt=pt, l=l: e.tensor_scalar(out=hT[:, kc, 0:128], in0=pt[:, 0:128], scalar1=gTs[:, l, 3, kc:kc + 1], scalar2=None, op0=ALU.mult), reads=[pt, gTs], writes=[(hT, kc)])
                pk = proj_tok(wk, wkv, 128, 512)
                rstd_free(pk, 128, 4, 128, 0)
                for h in range(4):
                    P.op("dve", lambda e, h=h, l=l, pk=pk: e.scalar_tensor_tensor(out=knf[:, h * 128:(h + 1) * 128], in0=pk[:, h * 128:(h + 1) * 128], scalar=small[:, h:h + 1], in1=xkng[:, l, :], op0=ALU.mult, op1=ALU.mult), reads=[pk, small, xkng], writes=[knf])
                dma("sp", o_mk_p[l, mb * 128:(mb + 1) * 128, :], knf[:, :], [knf], [o_mk_p], "o_knf")
                P.op("dve", lambda e: e.tensor_copy(out=knb[:, :], in_=knf[:, :]), reads=[knf], writes=[knb])
                for h in range(4):
                    transpose_bf(mKT[l][:, h, mb * 128:(mb + 1) * 128], knb[:, h * 128:(h + 1) * 128], 128, 128, [knb], [mKT[l]])
                pv = proj_tok(wv_, wvv, 128, 512)
                P.op("act", lambda e, pv=pv: e.activation(out=vCf[:, :], in_=pv[:, 0:512], func=AF.Copy), reads=[pv], writes=[vCf])
                dma("sp", o_mv_p[l, mb * 128:(mb + 1) * 128, :], vCf[:, :], [vCf], [o_mv_p], "o_vCf")
                P.op("dve", lambda e, l=l, mb=mb: e.tensor_copy(out=mVa[l][:, mb, :, 0:128], in_=vCf[:, :].rearrange("p (h d) -> p h d", h=4)), reads=[vCf], writes=[mVa[l]])

    def mem_kv_sample(si):
        for l in range(2):
            for mb in range(2):
                dma("pool", knb[:, :], c_mk[l, si, mb * 128:(mb + 1) * 128, :], [c_mk], [knb], "ld_knb")
                for h in range(4):
                    transpose_bf(mKT[l][:, h, mb * 128:(mb + 1) * 128], knb[:, h * 128:(h + 1) * 128], 128, 128, [knb], [mKT[l]])
                dma("pool", mVa[l][:, mb, :, 0:128], c_mv[l, si, mb * 128:(mb + 1) * 128, :].rearrange("p (h d) -> p h d", h=4), [c_mv], [mVa[l]], "ld_mVa")

    def past_kv_sample(si):
        for l in range(2):
            dma("pool", Vc[l][0:PAST, :], c_dv[l, si, :, :], [c_dv], [Vc[l]], "convV%d" % l)
            for kb in range(PAST // 128):
                dma("pool", knb[:, :], c_dk[l, si, kb * 128:(kb + 1) * 128, :], [c_dk], [knb], "ld_knb")
                for h in range(4):
                    transpose_bf(KTn[:, h, :], knb[:, h * 128:(h + 1) * 128], 128, 128, [knb], [KTn])
                for h in range(4):
                    dma("sp", Kc[l][h][:, kb * 128:(kb + 1) * 128], KTn[:, h, :], [KTn], [Kc[l][h]], "st_KTn")

    def layer(l, stream, pos0, L, NS, okd, ovd):
        TT = L * NS
        gq0 = pos0 // 128
        kvlen = pos0 + TT
        nkb = (kvlen + 127) // 128
        kd = 0 if L == 128 else 1
        gamL = [g ** L for g in GAM]
        sl = lambda s: slice(s * L, (s + 1) * L)
        rmsnorm_T(xT, l, 0, TT)
        for s in range(NS):
            dma("sp", ropeq[:L, s, :, :], ropeq_d[pos0 + s * L:pos0 + (s + 1) * L, :, :], [ropeq_d], [ropeq], "ropeq")
            dma("sp", ropek[:L, s, :, :], ropek_d[pos0 + s * L:pos0 + (s + 1) * L, :, :], [ropek_d], [ropek], "ropek")
        P.op("pool", lambda e: e.tensor_copy(out=a_ext[:, :, 0:30], in_=a_hist[l][:, :, :]), reads=[a_hist[l]], writes=[(a_ext, "a")])
        def conv_gen_f():
            for j in range(4):
                P.op("dve", lambda e, j=j: e.tensor_scalar(out=cacc[:, j, :TT], in0=a_ext[:, j, 0:TT], scalar1=convaw[:, l, j, 0:1], scalar2=convab[:, l, j:j + 1], op0=ALU.mult, op1=ALU.add), reads=[(a_ext, "a"), convaw, convab], writes=[(cacc, j)])
                yield
                for k in range(1, 31):
                    P.op("dve", lambda e, j=j, k=k: e.scalar_tensor_tensor(out=cacc[:, j, :TT], in0=a_ext[:, j, k:k + TT], scalar=convaw[:, l, j, k:k + 1], in1=cacc[:, j, :TT], op0=ALU.mult, op1=ALU.add), reads=[(a_ext, "a"), convaw, (cacc, j)], writes=[(cacc, j)])
                    yield
            P.op("pool", lambda e: e.tensor_copy(out=a_hist[l][:, :, :], in_=a_ext[:, :, TT:TT + 30]), reads=[(a_ext, "a")], writes=[a_hist[l]])
            yield

        conv_gen = conv_gen_f()

        def conv_step(n):
            for _ in range(n):
                if next(conv_gen, "done") == "done":
                    break

        def ln_phase():
            conv_step(1000)
            for s in range(NS):
                pt = ps("T")
                for j in range(4):
                    P.op("pe", lambda e, j=j, s=s, pt=pt: e.transpose(pt[:L, j * 128:(j + 1) * 128], cacc[:, j, sl(s)], identf[:, :]), reads=[(cacc, j), identf], writes=[pt])
                P.op("dve", lambda e, pt=pt: e.reduce_sum(out=small[:L, 8:9], in_=pt[:L, 0:512], axis=AX.X), reads=[pt], writes=[small])
                P.op("dve", lambda e: e.tensor_scalar(out=small[:L, 9:10], in0=small[:L, 8:9], scalar1=-1.0 / 512, scalar2=None, op0=ALU.mult), reads=[small], writes=[small])
                P.op("act", lambda e, pt=pt: e.activation(out=t1[:L, :], in_=pt[:L, 0:512], func=AF.Identity, bias=small[:L, 9:10], scale=1.0), reads=[pt, small], writes=[t1])
                P.op("act", lambda e: e.activation(out=t2[:L, :], in_=t1[:L, :], func=AF.Square, accum_out=small[:L, 10:11]), reads=[t1], writes=[t2, small])
                P.op("act", lambda e: e.activation(out=small[:L, 10:11], in_=small[:L, 10:11], func=AF.Sqrt, bias=epsb[:L, 0:1], scale=1.0 / 512), reads=[small, epsb], writes=[small])
                P.op("dve", lambda e: e.reciprocal(out=small[:L, 10:11], in_=small[:L, 10:11]), reads=[small], writes=[small])
                P.op("dve", lambda e: e.scalar_tensor_tensor(out=t1[:L, :], in0=t1[:L, :], scalar=small[:L, 10:11], in1=lnag[:L, l, :], op0=ALU.mult, op1=ALU.mult), reads=[t1, small, lnag], writes=[t1])
                P.op("dve", lambda e: e.tensor_tensor(out=t1[:L, :], in0=t1[:L, :], in1=lnab[:L, l, :], op=ALU.add), reads=[t1, lnab], writes=[t1])
                P.op("act", lambda e, s=s: e.activation(out=mix_tok[:L, s, 0:512], in_=t1[:L, :], func=AF.Silu), reads=[t1], writes=[(mix_tok, s)])

        def ret_phase():
            for s in range(NS):
                for h in range(4):
                    pS = ps("L")
                    P.op("pe", lambda e, h=h, s=s, pS=pS: e.matmul(pS[:L, :L], lhsT=kTr[:, s, h, :L], rhs=qTr[:, s, h, :L], start=True, stop=True), reads=[(kTr, s), (qTr, s)], writes=[pS])
                    P.op("dve", lambda e, h=h, pS=pS: e.tensor_tensor(out=STb[:L, h, :L], in0=pS[:L, :L], in1=dect[:L, h, :L], op=ALU.mult), reads=[pS, dect], writes=[(STb, h)])
                    pO = ps("O")
                    P.op("pe", lambda e, h=h, s=s, pO=pO: e.matmul(pO[:L, 0:256], lhsT=STb[:L, h, :L], rhs=vB[:L, s, h * 256:(h + 1) * 256], start=True, stop=False), reads=[(STb, h), (vB, s)], writes=[pO])
                    P.op("pe", lambda e, h=h, s=s, pO=pO: e.matmul(pO[:L, 0:256], lhsT=qtT[:, s, h, :L], rhs=Sbf[l][:, h, :], start=False, stop=True), reads=[(qtT, s), Sbf[l]], writes=[pO])
                    pU = ps("O")
                    P.op("pe", lambda e, h=h, s=s, pU=pU: e.matmul(pU[:, 0:256], lhsT=ktil[:L, s, h * 128:(h + 1) * 128], rhs=vB[:L, s, h * 256:(h + 1) * 256], start=True, stop=True), reads=[(ktil, s), (vB, s)], writes=[pU])
                    P.op("dve", lambda e, h=h, pU=pU: e.scalar_tensor_tensor(out=Sst[l][:, h, :], in0=Sst[l][:, h, :], scalar=float(gamL[h]), in1=pU[:, 0:256], op0=ALU.mult, op1=ALU.add), reads=[(Sst[l], h), pU], writes=[(Sst[l], h)])
                    P.op("pool", lambda e, h=h: e.tensor_copy(out=Sbf[l][:, h, :], in_=Sst[l][:, h, :]), reads=[(Sst[l], h)], writes=[(Sbf[l], h)])
                    rt = (t1 if h < 2 else t3)
                    rc = (h % 2) * 256
                    P.op("act", lambda e, pO=pO, h=h: e.activation(out=t2[:L, 0:256], in_=pO[:L, 0:256], func=AF.Square, accum_out=rsm[:L, h:h + 1]), reads=[pO], writes=[t2, (rsm, h)])
                    P.op("act", lambda e, h=h: e.activation(out=rsm[:L, h:h + 1], in_=rsm[:L, h:h + 1], func=AF.Sqrt, bias=epsb[:L, 0:1], scale=1.0 / 256), reads=[(rsm, h), epsb], writes=[(rsm, h)])
                    P.op("dve", lambda e, h=h: e.reciprocal(out=rsm[:L, h:h + 1], in_=rsm[:L, h:h + 1]), reads=[(rsm, h)], writes=[(rsm, h)])
                    P.op("dve", lambda e, h=h, pO=pO, rt=rt, rc=rc: e.scalar_tensor_tensor(out=rt[:L, rc:rc + 256], in0=pO[:L, 0:256], scalar=rsm[:L, h:h + 1], in1=retg[:L, l, h * 256:(h + 1) * 256], op0=ALU.mult, op1=ALU.mult), reads=[pO, (rsm, h), retg], writes=[(rt, h % 2)])
                    P.op("dve", lambda e, h=h, s=s, rt=rt, rc=rc: e.tensor_tensor(out=mix_tok[:L, s, 512 + h * 256:512 + (h + 1) * 256], in0=rt[:L, rc:rc + 256], in1=sgB[:L, s, h * 256:(h + 1) * 256], op=ALU.mult), reads=[(rt, h % 2), (sgB, s)], writes=[(mix_tok, s)])

        for cb in range(1, 11):
            wb, wv = wload(win_b[l], win_b[l][cb], 16, 512)
            if cb == 1:
                wb0, wv0 = wload(win_b[l], win_b[l][0], 16, 512)
                for j in range(4):
                    pg = proj_feat(wb, wv, j, TT)
                    P.op("act", lambda e, pg=pg: e.activation(out=sig[:, :TT], in_=pg[:, :TT], func=AF.Sigmoid), reads=[pg], writes=[sig])
                    pvv = proj_feat(wb0, wv0, j, TT)
                    P.op("dve", lambda e, j=j, pvv=pvv: e.tensor_tensor(out=a_ext[:, j, 30:30 + TT], in0=pvv[:, :TT], in1=sig[:, :TT], op=ALU.mult), reads=[pvv, sig, (a_ext, "a")], writes=[(a_ext, "a")])
                continue
            for s in range(NS):
                pa_ = proj_tok(wb, wv, L, 512, s=s)
                if cb in (2, 3):
                    tab = ropeq if cb == 2 else ropek
                    dst = qr if cb == 2 else kr
                    pv4 = pa_[:L, 0:512].rearrange("p (h d) -> p h d", h=4)
                    cosv = tab[:L, s, 0, :].rearrange("p (h d) -> p h d", h=4)
                    sinv = tab[:L, s, 1, :].rearrange("p (h d) -> p h d", h=4)
                    d4 = dst[:L, :].rearrange("p (h d) -> p h d", h=4)
                    tv = [t[:L, 0:256].rearrange("p (h d) -> p h d", h=4) for t in (t1, t2)]
                    P.op("dve", lambda e, pv4=pv4, cosv=cosv, tv=tv: e.tensor_tensor(out=tv[0], in0=pv4[:, :, 0:64], in1=cosv, op=ALU.mult), reads=[pa_, tab], writes=[t1])
                    P.op("dve", lambda e, pv4=pv4, sinv=sinv, tv=tv: e.tensor_tensor(out=tv[1], in0=pv4[:, :, 64:128], in1=sinv, op=ALU.mult), reads=[pa_, tab], writes=[t2])
                    P.op("dve", lambda e, d4=d4, tv=tv: e.tensor_tensor(out=d4[:, :, 0:64], in0=tv[0], in1=tv[1], op=ALU.subtract), reads=[t1, t2], writes=[dst])
                    P.op("dve", lambda e, pv4=pv4, sinv=sinv, tv=tv: e.tensor_tensor(out=tv[0], in0=pv4[:, :, 0:64], in1=sinv, op=ALU.mult), reads=[pa_, tab], writes=[t1])
                    P.op("dve", lambda e, pv4=pv4, cosv=cosv, tv=tv: e.tensor_tensor(out=tv[1], in0=pv4[:, :, 64:128], in1=cosv, op=ALU.mult), reads=[pa_, tab], writes=[t2])
                    P.op("dve", lambda e, d4=d4, tv=tv: e.tensor_tensor(out=d4[:, :, 64:128], in0=tv[0], in1=tv[1], op=ALU.add), reads=[t1, t2], writes=[dst])
                    dT = qTr if cb == 2 else kTr
                    transpose4(dT[:, s, :, :L], [dst[:L, h * 128:(h + 1) * 128] for h in range(4)], L, [dst], [(dT, s)])
                    if cb == 2:
                        P.op("dve", lambda e, s=s: e.tensor_tensor(out=qtT[:, s, :, :L], in0=qTr[:, s, :, :L], in1=qdect[:, :, :L], op=ALU.mult), reads=[(qTr, s), qdect], writes=[(qtT, s)])
                    else:
                        for h in range(4):
                            P.op("dve", lambda e, h=h, s=s: e.tensor_scalar(out=ktil[:L, s, h * 128:(h + 1) * 128], in0=kr[:L, h * 128:(h + 1) * 128], scalar1=kdec[:L, kd, h:h + 1], scalar2=None, op0=ALU.mult), reads=[kr, kdec], writes=[(ktil, s)])
                elif cb in (4, 5):
                    P.op("act", lambda e, pa_=pa_, cb=cb, s=s: e.activation(out=vB[:L, s, (cb - 4) * 512:(cb - 3) * 512], in_=pa_[:L, 0:512], func=AF.Copy), reads=[pa_], writes=[(vB, s)])
                elif cb in (6, 7):
                    P.op("act", lambda e, pa_=pa_, cb=cb, s=s: e.activation(out=sgB[:L, s, (cb - 6) * 512:(cb - 5) * 512], in_=pa_[:L, 0:512], func=AF.Silu), reads=[pa_], writes=[(sgB, s)])
                elif cb in (8, 9):
                    rstd_free(pa_, L, 8, 64, 0)
                    gsrc = qng if cb == 8 else kng
                    dstf = t3 if cb == 8 else knf
                    for g in range(8):
                        P.op("dve", lambda e, g=g, pa_=pa_, gsrc=gsrc, dstf=dstf: e.scalar_tensor_tensor(out=dstf[:L, g * 64:(g + 1) * 64], in0=pa_[:L, g * 64:(g + 1) * 64], scalar=small[:L, g:g + 1], in1=gsrc[:L, l, :], op0=ALU.mult, op1=ALU.mult),
                             reads=[pa_, small, gsrc], writes=[dstf])
                    if cb == 8:
                        P.op("act", lambda e: e.activation(out=qn[:L, :], in_=t3[:L, :], func=AF.Copy), reads=[t3], writes=[qn])
                        transpose4(cqT[:, :, sl(s)], [qn[:L, h * 128:(h + 1) * 128] for h in range(4)], L, [qn], [cqT])
                    else:
                        dma("sp", okd[s], knf[:L, :], [knf], [stream["okbuf"]], "o_knf")
                        P.op("act", lambda e: e.activation(out=knb[:L, :], in_=knf[:L, :], func=AF.Copy), reads=[knf], writes=[knb])
                        transpose4(KTn[:, :, :L], [knb[:L, h * 128:(h + 1) * 128] for h in range(4)], L, [knb], [KTn])
                        for h in range(4):
                            dma("sp", Kc[l][h][:, pos0 + s * L:pos0 + (s + 1) * L], KTn[:, h, :L], [KTn], [Kc[l][h]], "st_KTn")
                elif cb == 10:
                    P.op("act", lambda e, pa_=pa_: e.activation(out=vCf[:L, :], in_=pa_[:L, 0:512], func=AF.Copy), reads=[pa_], writes=[vCf])
                    dma("sp", ovd[s], vCf[:L, :], [vCf], [stream["ovbuf"]], "o_vCf")
                    P.op("dve", lambda e: e.tensor_copy(out=vCb[:L, :], in_=vCf[:L, :]), reads=[vCf], writes=[vCb])
                    dma("sp", Vc[l][pos0 + s * L:pos0 + (s + 1) * L, :], vCb[:L, :], [vCb], [Vc[l]], "st_vCb")
                conv_step((125 + 6 * NS - 1) // (6 * NS))
            if cb == 7:
                ret_phase()
            if cb == 8:
                ln_phase()

        P.op("pool", lambda e: e.memset(Vaug[:, :, 128:129], 1.0), writes=[Vaug])
        for h in range(4):
            CH = 8
            nfull = kvlen // 128
            for c in range((nkb + CH - 1) // CH):
                k0 = c * CH * 128
                k1 = min(kvlen, (c + 1) * CH * 128)
                dma("sp", KT[:, k0:k1], Kc[l][h][:, k0:k1], [Kc[l][h]], [(KT, c)], "ld_KT%d" % c)
                bb0 = c * CH
                bb1 = min(nfull, (c + 1) * CH)
                if bb1 > bb0:
                    dma("sp", Vaug[:, bb0:bb1, 0:128], Vc[l][bb0 * 128:bb1 * 128, h * 128:(h + 1) * 128].rearrange("(kb p) d -> p kb d", p=128), [Vc[l]], [(Vaug, "v%d" % c)], "ld_V%d" % c)
                if kvlen % 128 and bb0 <= nfull < (c + 1) * CH:
                    dma("sp", Vaug[0:64, nfull, 0:128], Vc[l][nfull * 128:nfull * 128 + 64, h * 128:(h + 1) * 128], [Vc[l]], [(Vaug, "v%d" % c)], "ld_V%d" % c)
            pOs = [ps("O") for s in range(NS)]

            def qk_unit(m, kb):
                kl = min(128, kvlen - kb * 128)
                s0 = max(0, kb - gq0)
                ncol = TT - s0 * L
                spec = [(s, 0 if kb == gq0 + s else 1) for s in range(s0, NS) if kb in (gq0 + s, gq0 + s - 1)]
                pL = ps("L")
                P.op("pe", lambda e, m=m, kb=kb, kl=kl, pL=pL, h=h, s0=s0, ncol=ncol, spec=spec: e.matmul(pL[:kl, :ncol], lhsT=KT[64 * m:64 * m + 64, kb * 128:kb * 128 + kl], rhs=cqT[64 * m:64 * m + 64, h, s0 * L:TT], start=True, stop=(len(spec) == 0)), reads=[(KT, kb // 8), cqT], writes=[pL])
                nsp = []
                for (s, ty) in spec:
                    nsp.append((s, ty, False))
                    if ty == 0 and L == 128:
                        nsp.append((s, ty, True))
                for i, (s, ty, ismask) in enumerate(nsp):
                    lastf = (i == len(nsp) - 1)
                    c0 = (s - s0) * L
                    if not ismask:
                        P.op("pe", lambda e, kl=kl, pL=pL, ty=ty, h=h, lastf=lastf, c0=c0: e.matmul(pL[:kl, c0:c0 + L], lhsT=antib[:, 0:kl], rhs=Hb[:, ty, h, :L], start=False, stop=lastf), reads=[antib, Hb], writes=[pL])
                    else:
                        P.op("pe", lambda e, pL=pL, lastf=lastf, c0=c0: e.matmul(pL[:128, c0:c0 + 128], lhsT=mkq[0:1, 0, :], rhs=mkq[0:1, 1, :], start=False, stop=lastf), reads=[mkq], writes=[pL])
                return (m, kb, kl, s0, ncol, pL)

            def exp_unit(u, idx):
                (m, kb, kl, s0, ncol, pL) = u
                pt_ = PT[idx % 2]
                P.op("act", lambda e, kl=kl, pL=pL, pt_=pt_, h=h, ncol=ncol: e.activation(out=pt_[:kl, :ncol], in_=pL[:kl, :ncol], func=AF.Exp, bias=b15[:kl, h:h + 1], scale=0.125), reads=[pL, b15], writes=[pt_])
                return pt_

            def pv_unit(u, pt_):
                (m, kb, kl, s0, ncol, pL) = u
                for s in range(s0, NS):
                    c0 = (s - s0) * L
                    P.op("pe", lambda e, m=m, kb=kb, kl=kl, pt_=pt_, s=s, c0=c0, pOs=pOs: e.matmul(pOs[s][:L, m * 256:m * 256 + 129], lhsT=pt_[:kl, c0:c0 + L], rhs=Vaug[:kl, kb, 0:129], start=(kb == 0), stop=(kb == gq0 + s)), reads=[pt_, (Vaug, "v%d" % (kb // 8))], writes=[pOs[s]])

            pend = None
            idx = 0
            for m in range(2):
                for kb in range(nkb):
                    u = qk_unit(m, kb)
                    if pend is not None:
                        pv_unit(*pend)
                    pt_ = exp_unit(u, idx)
                    pend = (u, pt_)
                    idx += 1
            pv_unit(*pend)
            for s in range(NS):
                pO = pOs[s]
                P.op("dve", lambda e, pO=pO: e.reciprocal(out=small[:L, 14:15], in_=pO[:L, 128:129]), reads=[pO], writes=[small])
                P.op("dve", lambda e, pO=pO: e.reciprocal(out=small[:L, 15:16], in_=pO[:L, 256 + 128:256 + 129]), reads=[pO], writes=[small])
                P.op("dve", lambda e: e.tensor_tensor(out=small[:L, 15:16], in0=small[:L, 15:16], in1=lamt[:L, l, 1:2], op=ALU.mult), reads=[small, lamt], writes=[small])
                P.op("dve", lambda e, pO=pO: e.tensor_scalar(out=ot2[:L, :], in0=pO[:L, 256:256 + 128], scalar1=small[:L, 15:16], scalar2=None, op0=ALU.mult), reads=[pO, small], writes=[ot2])
                P.op("dve", lambda e, pO=pO: e.scalar_tensor_tensor(out=ot[:L, :], in0=pO[:L, 0:128], scalar=small[:L, 14:15], in1=ot2[:L, :], op0=ALU.mult, op1=ALU.add), reads=[pO, small, ot2], writes=[ot])
                P.op("act", lambda e: e.activation(out=ot2[:L, :], in_=ot[:L, :], func=AF.Square, accum_out=small[:L, 16:17]), reads=[ot], writes=[ot2, small])
                P.op("act", lambda e: e.activation(out=small[:L, 16:17], in_=small[:L, 16:17], func=AF.Sqrt, bias=epsb[:L, 0:1], scale=1.0 / 128), reads=[small, epsb], writes=[small])
                P.op("dve", lambda e: e.reciprocal(out=small[:L, 16:17], in_=small[:L, 16:17]), reads=[small], writes=[small])
                P.op("dve", lambda e: e.scalar_tensor_tensor(out=ot[:L, :], in0=ot[:L, :], scalar=small[:L, 16:17], in1=subg[:L, l, :], op0=ALU.mult, op1=ALU.mult), reads=[ot, small, subg], writes=[ot])
                P.op("dve", lambda e, h=h, s=s: e.tensor_scalar(out=mix_tok[:L, s, 1536 + h * 128:1536 + (h + 1) * 128], in0=ot[:L, :], scalar1=float(1.0 - LAM_INIT[l]), scalar2=None, op0=ALU.mult), reads=[ot], writes=[(mix_tok, s)])

        for s in range(NS):
            for g4 in range(4):
                transpose4(mixT[:, g4 * 4:g4 * 4 + 4, sl(s)], [mix_tok[:L, s, kc * 128:(kc + 1) * 128] for kc in range(g4 * 4, g4 * 4 + 4)], L, [(mix_tok, s)], [(mixT, g4 * 4 + i) for i in range(4)])
        for cb8 in range(8):
            wb, wv = wload_half(wout_b[l], wout_b[l][cb8 // 2][:, :, (cb8 % 2) * 256:(cb8 % 2 + 1) * 256], 16, 256)
            for j in range(2):
                pa_ = proj_feat(wb, wv, j, TT, rhs=mixT, role="F")
                c = cb8 * 2 + j
                P.op("dve", lambda e, c=c, pa_=pa_: e.tensor_tensor(out=xT[:, c, :TT], in0=xT[:, c, :TT], in1=pa_[:, :TT], op=ALU.add), reads=[(xT, c), pa_], writes=[(xT, c)])

        rmsnorm_T(xT, l, 1, TT)
        wb, wv = wload(wxq_b[l], wxq_b[l][0], 16, 512)
        for s in range(NS):
            pq = proj_tok(wb, wv, L, 512, s=s)
            rstd_free(pq, L, 4, 128, 0)
            for h in range(4):
                P.op("dve", lambda e, h=h, pq=pq: e.scalar_tensor_tensor(out=qn[:L, h * 128:(h + 1) * 128], in0=pq[:L, h * 128:(h + 1) * 128], scalar=small[:L, h:h + 1], in1=xqng[:L, l, :], op0=ALU.mult, op1=ALU.mult), reads=[pq, small, xqng], writes=[qn])
            transpose4(cqT[:, :, sl(s)], [qn[:L, h * 128:(h + 1) * 128] for h in range(4)], L, [qn], [cqT])
        for h in range(4):
            pOs = [ps("O") for s in range(NS)]
            for mb in range(2):
                pL = ps("L")
                P.op("pe", lambda e, h=h, mb=mb, pL=pL: e.matmul(pL[:, :TT], lhsT=mKT[l][:, h, mb * 128:(mb + 1) * 128], rhs=cqT[:, h, :TT], start=True, stop=True), reads=[mKT[l], cqT], writes=[pL])
                pt_ = PT[mb]
                P.op("act", lambda e, pL=pL, pt_=pt_: e.activation(out=pt_[:, :TT], in_=pL[:, :TT], func=AF.Exp, scale=128.0 ** -0.5), reads=[pL], writes=[pt_])
                for s in range(NS):
                    P.op("pe", lambda e, h=h, mb=mb, pt_=pt_, s=s, pOs=pOs: e.matmul(pOs[s][:L, 0:129], lhsT=pt_[:, sl(s)], rhs=mVa[l][:, mb, h, :], start=(mb == 0), stop=(mb == 1)), reads=[pt_, mVa[l]], writes=[pOs[s]])
            for s in range(NS):
                pO = pOs[s]
                P.op("dve", lambda e, pO=pO: e.reciprocal(out=small[:L, 18:19], in_=pO[:L, 128:129]), reads=[pO], writes=[small])
                P.op("dve", lambda e, h=h, pO=pO, s=s: e.tensor_scalar(out=mo_tok[:L, s, h * 128:(h + 1) * 128], in0=pO[:L, 0:128], scalar1=small[:L, 18:19], scalar2=None, op0=ALU.mult), reads=[pO, small], writes=[(mo_tok, s)])
        for s in range(NS):
            transpose4(moT[:, :, sl(s)], [mo_tok[:L, s, h * 128:(h + 1) * 128] for h in range(4)], L, [(mo_tok, s)], [(moT, h) for h in range(4)])
        for cb in range(4):
            wb, wv = wload_half(wxo_b[l], wxo_b[l][cb], 4, 512)
            for j in range(4):
                pa_ = proj_feat(wb, wv, j, TT, nk=4, rhs=moT, role="F")
                c = cb * 4 + j
                P.op("dve", lambda e, c=c, pa_=pa_: e.tensor_tensor(out=xT[:, c, :TT], in0=xT[:, c, :TT], in1=pa_[:, :TT], op=ALU.add), reads=[(xT, c), pa_], writes=[(xT, c)])

        rmsnorm_T(xT, l, 2, TT)
        for (b0, b1) in ((0, 6), (6, 11)):
            nfc = (b1 - b0) * 4
            for b in range(b0, b1):
                for (blk, off) in ((b, 0), (11 + b, 44)):
                  for half in range(2):
                    wb_, wv_ = wload_half(wup_b[l], wup_b[l][blk][:, :, half * 256:(half + 1) * 256], 16, 256)
                    for jj in range(2):
                        j = half * 2 + jj
                        fc = b * 4 + j
                        pa_ = proj_feat(wb_, wv_, jj, TT, role="F")
                        ue = uext[0] if off == 0 else uext[1]
                        ch = off + fc
                        dstc = cv4[:, j, :] if off == 0 else cg[:, :]
                        dkey = (cv4, j) if off == 0 else cg
                        P.op("pool", lambda e, ue=ue, ch=ch: e.tensor_copy(out=ue[:, 0:2], in_=u_hist[l][:, ch, :]), reads=[(u_hist[l], ch)], writes=[ue])
                        P.op("act", lambda e, ue=ue, pa_=pa_: e.activation(out=ue[:, 2:2 + TT], in_=pa_[:, :TT], func=AF.Copy), reads=[pa_], writes=[ue])
                        P.op("pool", lambda e, ue=ue, ch=ch: e.tensor_copy(out=u_hist[l][:, ch, :], in_=ue[:, TT:TT + 2]), reads=[ue], writes=[(u_hist[l], ch)])
                        P.op("act", lambda e, pa_=pa_, ch=ch, dstc=dstc: e.activation(out=dstc[:, :TT], in_=pa_[:, :TT], func=AF.Identity, bias=convfb[:, l, ch:ch + 1], scale=convfw[:, l, ch, 2:3]), reads=[pa_, convfw, convfb], writes=[dkey])
                        for k in (0, 1):
                            P.op("dve", lambda e, ue=ue, ch=ch, dstc=dstc, k=k: e.scalar_tensor_tensor(out=dstc[:, :TT], in0=ue[:, k:k + TT], scalar=convfw[:, l, ch, k:k + 1], in1=dstc[:, :TT], op0=ALU.mult, op1=ALU.add), reads=[ue, convfw, dkey], writes=[dkey])
                        if off == 44:
                            P.op("act", lambda e: e.activation(out=sgt[:, :TT], in_=cg[:, :TT], func=AF.Silu), reads=[cg], writes=[sgt])
                            P.op("dve", lambda e, fc=fc, b0=b0, j=j: e.tensor_tensor(out=gTt[:, fc - b0 * 4, :TT], in0=sgt[:, :TT], in1=cv4[:, j, :TT], op=ALU.mult), reads=[sgt, (cv4, j)], writes=[(gTt, fc - b0 * 4)])
            for c in range(16):
                wbk, wv = wload_half(wdn_b[l], wdn_b[l][c][:, b0 * 4:b1 * 4, :], nfc, 128)
                pa_ = ps("F")
                for fi in range(nfc):
                    P.op("pe", lambda e, fi=fi, pa_=pa_, wv=wv, nfc=nfc: e.matmul(pa_[:, :TT], lhsT=wv[:, fi, :], rhs=gTt[:, fi, :TT], start=(fi == 0), stop=(fi == nfc - 1)), reads=[(gTt, fi), wbk], writes=[pa_])
                P.op("dve", lambda e, c=c, pa_=pa_: e.tensor_tensor(out=xT[:, c, :TT], in0=xT[:, c, :TT], in1=pa_[:, :TT], op=ALU.add), reads=[(xT, c), pa_], writes=[(xT, c)])

    def run_group(stream, xsrc, ydst, pos0, L, NS, okd, ovd):
        for s in range(NS):
            dma("sp", xin[:L, :], xsrc[s], [stream["xbuf"]], [xin], "xin")
            for g in range(4):
                pt = ps("T")
                for j in range(4):
                    kc = g * 4 + j
                    P.op("pe", lambda e, kc=kc, j=j, pt=pt: e.transpose(pt[:, j * L:(j + 1) * L], xin[:L, kc * 128:(kc + 1) * 128], identf[:L, :L]), reads=[xin, identf], writes=[pt])
                P.op("act", lambda e, g=g, pt=pt, s=s: e.activation(out=xT[:, g * 4:g * 4 + 4, s * L:(s + 1) * L], in_=pt[:, 0:4 * L].rearrange("p (a b) -> p a b", a=4), func=AF.Copy), reads=[pt], writes=[(xT, g * 4 + i) for i in range(4)])
        for l in range(2):
            layer(l, stream, pos0, L, NS, okd[l], ovd[l])
        for s in range(NS):
            for g in range(4):
                pt = ps("T")
                for j in range(4):
                    kc = g * 4 + j
                    P.op("pe", lambda e, kc=kc, j=j, pt=pt, s=s: e.transpose(pt[:L, j * 128:(j + 1) * 128], xT[:, kc, s * L:(s + 1) * L], identf[:, :]), reads=[(xT, kc), identf], writes=[pt])
                P.op("act", lambda e, g=g, pt=pt: e.activation(out=yout[:L, g * 512:(g + 1) * 512], in_=pt[:L, 0:512], func=AF.Copy), reads=[pt], writes=[yout])
            dma("sp", ydst[s], yout[:L, :], [yout], [stream["ybuf"]], "yout")

    for l in range(2):
        P.op("pool", lambda e, l=l: e.memset(a_hist[l][:], 0.0), writes=[a_hist[l]])
        P.op("pool", lambda e, l=l: e.memset(Sst[l][:], 0.0), writes=[Sst[l]])
        P.op("pool", lambda e, l=l: e.memset(Sbf[l][:], 0.0), writes=[Sbf[l]])
        P.op("pool", lambda e, l=l: e.memset(u_hist[l][:], 0.0), writes=[u_hist[l]])
    mem_kv_prompt()
    pst = {"xbuf": xp, "ybuf": y_p, "okbuf": o_k_p, "ovbuf": o_v_p}
    NSP = NS_PROMPT
    for t in range(NT_PROMPT // NSP):
        p0 = t * 128 * NSP
        blk = lambda s, p0=p0: slice(p0 + s * 128, p0 + (s + 1) * 128)
        run_group(pst, [xp[blk(s), :] for s in range(NSP)], [y_p[blk(s), :] for s in range(NSP)], p0, 128, NSP,
                  [[o_k_p[l, blk(s), :] for s in range(NSP)] for l in range(2)], [[o_v_p[l, blk(s), :] for s in range(NSP)] for l in range(2)])
    for l in range(2):
        dma("sp", o_ca_p[l], a_hist[l][:], [a_hist[l]], [o_ca_p], "o_ca")
        dma("sp", o_rs_p[l].rearrange("h d e -> d h e"), Sst[l][:], [Sst[l]], [o_rs_p], "o_rs")
        dma("sp", o_cf_p[l], u_hist[l][:], [u_hist[l]], [o_cf_p], "o_cf")

    for si in range(2):
        for l in range(2):
            dma("sp", a_hist[l][:], s_conva[l, si], [s_conva], [a_hist[l]], "ld_ah")
            dma("sp", Sst[l][:], s_ret[l, si].rearrange("h d e -> d h e"), [s_ret], [Sst[l]], "ld_S")
            P.op("pool", lambda e, l=l: e.tensor_copy(out=Sbf[l][:], in_=Sst[l][:]), reads=[Sst[l]], writes=[Sbf[l]])
            dma("sp", u_hist[l][:], s_convf[l, si], [s_convf], [u_hist[l]], "ld_uh")
        mem_kv_sample(si)
        past_kv_sample(si)
        sst = {"xbuf": xs, "ybuf": y_s, "okbuf": o_k_s, "ovbuf": o_v_s}
        run_group(sst, [xs[si]], [y_s[si]], PAST, 64, 1, [[o_k_s[l, si]] for l in range(2)], [[o_v_s[l, si]] for l in range(2)])
        for l in range(2):
            dma("sp", o_ca_s[l, si], a_hist[l][:], [a_hist[l]], [o_ca_s], "o_ca")
            dma("sp", o_rs_s[l, si].rearrange("h d e -> d h e"), Sst[l][:], [Sst[l]], [o_rs_s], "o_rs")
            dma("sp", o_cf_s[l, si], u_hist[l][:], [u_hist[l]], [o_cf_s], "o_cf")

    nsem = P.emit()
    return nc


NT_PROMPT = 32
NS_PROMPT = 2
VAUG_SEP = 0
_NC = None


def _bucket(rel):
    n = abs(rel)
    if n < 8:
        v = n
    else:
        v = 8 + int(np.float32(np.log(np.float32(max(n, 1)) / np.float32(8)) / np.float32(math.log(16.0)) * np.float32(8)))
        v = min(v, 15)
    return (16 if rel > 0 else 0) + v


def _consts():
    c = {}
    c["ident"] = np.eye(128, dtype=np.float32)
    c["anti"] = np.ascontiguousarray(np.eye(128, dtype=np.float32)[::-1])
    half = 64
    inv = (1.0 / (10000.0 ** (np.arange(half, dtype=np.float32) / half))).astype(np.float32)
    pos = np.arange(SEQ, dtype=np.float32)
    ang = pos[:, None] * inv[None, :]
    cos = np.cos(ang).astype(np.float32); sin = np.sin(ang).astype(np.float32)
    rq = np.stack([np.tile(cos, (1, 4)), np.tile(sin, (1, 4))], axis=1).astype(np.float32)
    c["ropeq"] = np.ascontiguousarray(rq)
    c["ropek"] = np.ascontiguousarray(rq * np.float32(128.0 ** -0.5))
    lg = np.log(np.array(GAM, dtype=np.float64))
    i = np.arange(128)
    dect = np.zeros((128, 4, 128), np.float32)
    for h in range(4):
        d = i[None, :] - i[:, None]
        dect[:, h, :] = np.where(d >= 0, np.exp(lg[h] * np.maximum(d, 0)), 0.0)
    c["dect"] = dect
    qd = np.zeros((128, 4, 128), np.float32)
    for h in range(4):
        qd[:, h, :] = np.exp(lg[h] * (i + 1.0))[None, :]
    c["qdect"] = qd
    kd = np.zeros((128, 2, 4), np.float32)
    for h in range(4):
        kd[:, 0, h] = np.exp(lg[h] * (127.0 - i))
        kd[:64, 1, h] = np.exp(lg[h] * (63.0 - i[:64]))
    c["kdec"] = kd
    oh = np.zeros((32, 2, 256), np.float32)
    for ty in range(2):
        for u in range(255):
            t = 254 - u
            rel = (t - 127) if ty == 0 else (t - 255)
            oh[_bucket(rel), ty, u] = 1.0
    c["oh"] = oh
    mk = np.zeros((1, 2, 128), np.float32)
    mk[0, 0, 64:] = 1.0
    mk[0, 1, :64] = -240000.0
    c["mkq"] = mk
    return c


def kernel(**inp):
    global _NC
    f = lambda a: np.ascontiguousarray(np.asarray(a, dtype=np.float32))
    I = {k: f(v) for k, v in inp.items()}
    if _NC is None:
        _NC = build()
    nc = _NC
    cst = _consts()
    rep = lambda v: np.ascontiguousarray(np.broadcast_to(v[None], (128,) + v.shape))
    shared = dict(cst)
    for k in ["w_in", "w_out", "w_xq", "w_xk", "w_xv", "w_xo", "w_up", "w_down"]:
        shared[k] = I[k]
    gs = np.stack([I["norm1_g"], I["norm2_g"], I["norm3_g"], I["mem_norm_g"]], axis=1)
    shared["gT"] = np.ascontiguousarray(gs.reshape(2, 4, 16, 128).transpose(3, 0, 1, 2))
    shared["convaw"] = np.ascontiguousarray(I["conv_a_w"].reshape(2, 31, 4, 128).transpose(3, 0, 2, 1))
    shared["convab"] = np.ascontiguousarray(I["conv_a_b"].reshape(2, 4, 128).transpose(2, 0, 1))
    shared["lnag"] = rep(I["ln_a_g"]); shared["lnab"] = rep(I["ln_a_b"]); shared["retg"] = rep(I["ret_gn_g"])
    shared["qng"] = rep(I["diff_qn_g"]); shared["kng"] = rep(I["diff_kn_g"])
    shared["lqk"] = rep(np.stack([I["diff_lq1"], I["diff_lk1"], I["diff_lq2"], I["diff_lk2"]], axis=1))
    shared["subg"] = rep(I["diff_subln_g"])
    shared["relb"] = I["rel_bias"]; shared["relbT"] = np.ascontiguousarray(I["rel_bias"].T)
    shared["b15"] = rep(I["rel_bias"][15])
    shared["xqng"] = rep(I["xqn_g"]); shared["xkng"] = rep(I["xkn_g"])
    shared["convfw"] = np.ascontiguousarray(I["conv_f_w"].reshape(2, 3, 88, 128).transpose(3, 0, 2, 1))
    shared["convfb"] = np.ascontiguousarray(I["conv_f_b"].reshape(2, 88, 128).transpose(2, 0, 1))
    in_maps = []
    for c in range(8):
        m = dict(shared)
        b = c % 2
        ss = [2 * c, 2 * c + 1]
        m["xp"] = I["x_prompt"][b]; m["xs"] = np.ascontiguousarray(I["x_sample"][ss]); m["memp"] = I["mem_prompt"][b]
        sca = I["state_conv_a"][:, ss]
        m["s_conva"] = np.ascontiguousarray(sca.reshape(2, 2, 30, 4, 128).transpose(0, 1, 4, 3, 2))
        m["s_ret"] = np.ascontiguousarray(I["state_ret"][:, ss])
        m["c_dk"] = np.ascontiguousarray(I["cache_diff_k"][:, ss].reshape(2, 2, PAST, 512))
        m["c_dv"] = np.ascontiguousarray(I["cache_diff_v"][:, ss].reshape(2, 2, PAST, 512))
        m["c_mk"] = np.ascontiguousarray(I["cache_mem_k"][:, ss].reshape(2, 2, 256, 512))
        m["c_mv"] = np.ascontiguousarray(I["cache_mem_v"][:, ss].reshape(2, 2, 256, 512))
        scf = I["state_conv_f"][:, ss]
        m["s_convf"] = np.ascontiguousarray(scf.reshape(2, 2, 2, 88, 128).transpose(0, 1, 4, 3, 2))
        in_maps.append(m)
    res = run_bass_kernel_spmd(nc, in_maps, core_ids=list(range(8)))
    R = res.results
    P2 = [R[0], R[1]]
    y_p = np.stack([r["y_p"] for r in P2])
    y_s = np.concatenate([R[c]["y_s"] for c in range(8)], axis=0)
    ca_p = np.stack([r["o_ca_p"].transpose(0, 3, 2, 1).reshape(2, 30, 512) for r in P2], axis=1)
    rs_p = np.stack([r["o_rs_p"] for r in P2], axis=1)
    k_p = np.stack([r["o_k_p"].reshape(2, SEQ, 4, 128) for r in P2], axis=1)
    v_p = np.stack([r["o_v_p"].reshape(2, SEQ, 4, 128) for r in P2], axis=1)
    mk_p = np.stack([r["o_mk_p"].reshape(2, 256, 4, 128) for r in P2], axis=1)
    mv_p = np.stack([r["o_mv_p"].reshape(2, 256, 4, 128) for r in P2], axis=1)
    cf_p = np.stack([r["o_cf_p"].transpose(0, 3, 2, 1).reshape(2, 2, 2 * FF) for r in P2], axis=1)
    ca_s = np.concatenate([R[c]["o_ca_s"].transpose(0, 1, 4, 3, 2).reshape(2, 2, 30, 512) for c in range(8)], axis=1)
    rs_s = np.concatenate([R[c]["o_rs_s"] for c in range(8)], axis=1)
    k_s = np.concatenate([R[c]["o_k_s"].reshape(2, 2, 64, 4, 128) for c in range(8)], axis=1)
    v_s = np.concatenate([R[c]["o_v_s"].reshape(2, 2, 64, 4, 128) for c in range(8)], axis=1)
    cf_s = np.concatenate([R[c]["o_cf_s"].transpose(0, 1, 4, 3, 2).reshape(2, 2, 2, 2 * FF) for c in range(8)], axis=1)
    outs = (y_p, y_s, ca_p, rs_p, k_p, v_p, mk_p, mv_p, cf_p, ca_s, rs_s, k_s, v_s, cf_s)
    return tuple(np.ascontiguousarray(o.astype(np.float32)) for o in outs)
```
